# Optimizing a Trainium2 kernel written in Bass

```python
import math
import jax
import jax.numpy as jnp
from jax import lax
import numpy as np


D_MODEL = 1024
BATCH = 16
SEQ = 256
DEPTH = 2
DEC_BATCH = 2
DEC_SEQ = 1024
PAST_LEN = 512

GRID_W = 64
D_MIX = D_MODEL
CHUNK = 128
N_DIR = 2
EPS = 1e-6
SSD_INNER = D_MIX // 2
SSD_HEAD_DIM = 64
SSD_HEADS = SSD_INNER // SSD_HEAD_DIM
SSD_GROUPS = 2
SSD_STATE = 128
SSD_CONV = 5
SSD_CONV_DIM = SSD_INNER + 2 * SSD_GROUPS * SSD_STATE
SGU_DIM = D_MIX // 4
SGU_HEADS = 4
SGU_HEAD_DIM = SGU_DIM // SGU_HEADS
S5_DIM = D_MIX // 4
S5_GROUP_CH = 16
S5_GROUPS = S5_DIM // S5_GROUP_CH
S5_STATE = 64
D_FF = 4 * D_MODEL
OFF_Z = 0
OFF_XBC = OFF_Z + SSD_INNER
OFF_DT = OFF_XBC + SSD_CONV_DIM
OFF_SGU = OFF_DT + N_DIR * SSD_HEADS
OFF_S5 = OFF_SGU + 2 * SGU_DIM
IN_DIM = OFF_S5 + S5_DIM

kernel_name = 'hybrid_ssd_sgu_s5_diffusion_step'


def rmsnorm(x, g):
    xf = x.astype(jnp.float32)
    y = xf * lax.rsqrt(jnp.mean(xf * xf, axis=-1, keepdims=True) + EPS)
    return (y * g.astype(jnp.float32)).astype(x.dtype)


def centred_depthwise_conv(x, w, b):
    pad = (SSD_CONV - 1) // 2
    y = lax.conv_general_dilated(x, w[:, None, :], window_strides=(1,), padding=[(pad, pad)],
                                 dimension_numbers=('NWC', 'WIO', 'NWC'),
                                 feature_group_count=x.shape[-1])
    return y + b


def ssd_scan(x, dt, a, bm, cm, h0):
    b_, L, H, P = x.shape
    nc = L // CHUNK
    R = H // SSD_GROUPS
    x = x.reshape(b_, nc, CHUNK, SSD_GROUPS, R, P)
    dt = dt.reshape(b_, nc, CHUNK, SSD_GROUPS, R)
    bm = bm.reshape(b_, nc, CHUNK, SSD_GROUPS, SSD_STATE)
    cm = cm.reshape(b_, nc, CHUNK, SSD_GROUPS, SSD_STATE)
    a_cum = jnp.cumsum(dt * a.reshape(SSD_GROUPS, R), axis=2)
    xdt = x * dt[..., None]
    seg = a_cum[:, :, :, None] - a_cum[:, :, None, :]
    lower = jnp.tril(jnp.ones((CHUNK, CHUNK), dtype=bool))[:, :, None, None]
    lmat = jnp.exp(jnp.where(lower, seg, -jnp.inf))
    cb = jnp.einsum('bcqgn,bckgn->bcqkg', cm, bm)
    y_diag = jnp.einsum('bcqkg,bcqkgr,bckgrp->bcqgrp', cb, lmat, xdt)
    decay_to_end = jnp.exp(a_cum[:, :, -1:] - a_cum)
    chunk_states = jnp.einsum('bckgn,bckgr,bckgrp->bcgrpn', bm, decay_to_end, xdt)
    chunk_decay = jnp.exp(a_cum[:, :, -1])

    def step(h, inp):
        dec, st = inp
        return dec[..., None, None] * h + st, h

    h_final, h_prev = lax.scan(step, h0.reshape(b_, SSD_GROUPS, R, P, SSD_STATE),
                               (jnp.moveaxis(chunk_decay, 1, 0), jnp.moveaxis(chunk_states, 1, 0)))
    h_prev = jnp.moveaxis(h_prev, 0, 1)
    y_off = jnp.einsum('bcqgn,bcgrpn,bcqgr->bcqgrp', cm, h_prev, jnp.exp(a_cum))
    y = (y_diag + y_off).reshape(b_, L, H, P)
    return y, h_final.reshape(b_, H, P, SSD_STATE)


def ssd_mixer(s, p, h0):
    dtype = s.dtype
    b_, L, _ = s.shape
    f32 = jnp.float32
    z = s[..., OFF_Z:OFF_XBC]
    xbc = jax.nn.silu(centred_depthwise_conv(s[..., OFF_XBC:OFF_DT], p['ssd_conv_w'], p['ssd_conv_b'])).astype(f32)
    gn = SSD_GROUPS * SSD_STATE
    xs = xbc[..., :SSD_INNER].reshape(b_, L, SSD_HEADS, SSD_HEAD_DIM)
    bm = xbc[..., SSD_INNER:SSD_INNER + gn].reshape(b_, L, SSD_GROUPS, SSD_STATE)
    cm = xbc[..., SSD_INNER + gn:].reshape(b_, L, SSD_GROUPS, SSD_STATE)
    dt = jax.nn.softplus(s[..., OFF_DT:OFF_SGU].astype(f32).reshape(b_, L, N_DIR, SSD_HEADS)
                         + p['ssd_dt_bias'].astype(f32))
    a = -jnp.exp(p['ssd_a_log'].astype(f32))
    flip = lambda t: jnp.flip(t, axis=1)
    y_f, h_f = ssd_scan(xs, dt[:, :, 0], a[0], bm, cm, h0[:, 0])
    y_b, h_b = ssd_scan(flip(xs), flip(dt[:, :, 1]), a[1], flip(bm), flip(cm), h0[:, 1])
    y = y_f + flip(y_b) + p['ssd_d'].astype(f32)[:, None] * xs
    y = y.reshape(b_, L, SSD_INNER) * jax.nn.silu(z.astype(f32))
    y = rmsnorm(y, p['ssd_norm_g']).astype(dtype)
    return y, jnp.stack([h_f, h_b], axis=1)


def sgu_mixer(uv, p):
    uv = jax.nn.gelu(uv)
    u, v = uv[..., :SGU_DIM], uv[..., SGU_DIM:]
    v = rmsnorm(v, p['sgu_norm_g'])
    b_, L, _ = v.shape
    v = v.reshape(b_, L // CHUNK, CHUNK, SGU_HEADS, SGU_HEAD_DIM)
    mix = (jnp.einsum('hqk,bckhd->bcqhd', p['sgu_w'], v)
           + jnp.swapaxes(p['sgu_b'], 0, 1)[None, None, :, :, None])
    return u * mix.reshape(b_, L, SGU_DIM)


def to_col_major(u):
    b_, L, C = u.shape
    rows = L // GRID_W
    return u.reshape(b_, rows, GRID_W, C).transpose(0, 2, 1, 3).reshape(b_, L, C)


def from_col_major(u):
    b_, L, C = u.shape
    rows = L // GRID_W
    return u.reshape(b_, GRID_W, rows, C).transpose(0, 2, 1, 3).reshape(b_, L, C)


def s5_scan(bu, lam_bar, h0):
    a = jnp.broadcast_to(lam_bar, bu.shape)

    def combine(left, right):
        a_l, b_l = left
        a_r, b_r = right
        return a_r * a_l, a_r * b_l + b_r

    a_cum, b_cum = lax.associative_scan(combine, (a, bu), axis=1)
    h = a_cum * h0[:, None] + b_cum
    return h, h[:, -1]


def s5_mixer(u, p, h0, column_major):
    dtype = u.dtype
    f32 = jnp.float32
    if column_major:
        u = to_col_major(u)
    b_, L, _ = u.shape
    uf = u.astype(f32).reshape(b_, L, S5_GROUPS, S5_GROUP_CH)
    b_mat = lax.complex(p['s5_b_re'].astype(f32), p['s5_b_im'].astype(f32))
    c_mat = lax.complex(p['s5_c_re'].astype(f32), p['s5_c_im'].astype(f32))
    h_sum = 0.0
    finals = []
    for d in range(N_DIR):
        lam = lax.complex(p['s5_lambda_re'][d].astype(f32), p['s5_lambda_im'][d].astype(f32))
        step = jnp.exp(p['s5_log_dt'][d].astype(f32))[:, None]
        lam_bar = jnp.exp(lam * step)
        b_bar = ((lam_bar - 1.0) / lam)[..., None] * b_mat
        u_d = uf if d == 0 else jnp.flip(uf, axis=1)
        bu = jnp.einsum('gsc,blgc->blgs', b_bar, u_d.astype(jnp.complex64))
        h, h_last = s5_scan(bu, lam_bar, h0[:, d])
        h_sum = h_sum + (h if d == 0 else jnp.flip(h, axis=1))
        finals.append(h_last)
    y = jnp.real(jnp.einsum('gcs,blgs->blgc', c_mat, h_sum))
    y = y + p['s5_d'].astype(f32).reshape(S5_GROUPS, S5_GROUP_CH) * uf
    y = jax.nn.gelu(y.reshape(b_, L, S5_DIM))
    y = y * jax.nn.sigmoid(y @ p['s5_glu_w'].astype(f32) + p['s5_glu_b'].astype(f32))
    y = y.astype(dtype)
    if column_major:
        y = from_col_major(y)
    return y, jnp.stack(finals, axis=1)


def trunk_layer(x, mod, p, h0_ssd, h0_s5, latent):
    shift1, scale1, gate1, shift2, scale2, gate2 = jnp.split(mod, 6, axis=-1)
    h = rmsnorm(x, p['norm1_g']) * (1 + scale1) + shift1
    proj = h @ p['w_in']
    y_ssd, hs_ssd = ssd_mixer(proj[..., :OFF_SGU], p, h0_ssd)
    y_sgu = sgu_mixer(proj[..., OFF_SGU:OFF_S5], p)
    y_s5, hs_s5 = s5_mixer(proj[..., OFF_S5:], p, h0_s5, latent)
    mixed = jnp.concatenate([y_ssd, y_sgu, y_s5], axis=-1)
    x = x + gate1 * (mixed @ p['w_out'])
    h = rmsnorm(x, p['norm2_g']) * (1 + scale2) + shift2
    x = x + gate2 * (jnp.square(jax.nn.relu(h @ p['ffn_w1'])) @ p['ffn_w2'])
    return x, hs_ssd, hs_s5


def setup_inputs(seed: int = 0) -> dict:
    key = jax.random.key(seed)
    ks = iter(jax.random.split(key, 48))
    f32 = jnp.float32

    def nrm(shape, scale):
        return scale * jax.random.normal(next(ks), shape, f32)

    lo, hi = math.log(1e-3), math.log(1e-1)
    x_prompt = nrm((BATCH, SEQ, D_MODEL), 1.0)
    x_sample = nrm((DEC_BATCH, DEC_SEQ, D_MODEL), 1.0)
    state_ssd = nrm((DEC_BATCH, DEPTH, N_DIR, SSD_HEADS, SSD_HEAD_DIM, SSD_STATE), 0.5)
    state_s5_re = nrm((DEC_BATCH, DEPTH, N_DIR, S5_GROUPS, S5_STATE), 0.5)
    state_s5_im = nrm((DEC_BATCH, DEPTH, N_DIR, S5_GROUPS, S5_STATE), 0.5)
    c = nrm((DEC_BATCH, D_MODEL), 1.0)
    c_ctx = nrm((D_MODEL,), 1.0)
    ada_w = nrm((DEPTH, D_MODEL, 6 * D_MODEL), 0.3 * D_MODEL ** -0.5)
    ada_b = nrm((DEPTH, 6 * D_MODEL), 0.01)
    norm1_g = 1.0 + nrm((DEPTH, D_MODEL), 0.01)
    norm2_g = 1.0 + nrm((DEPTH, D_MODEL), 0.01)
    w_in = nrm((DEPTH, D_MODEL, IN_DIM), D_MODEL ** -0.5)
    ssd_conv_w = nrm((DEPTH, SSD_CONV, SSD_CONV_DIM), SSD_CONV ** -0.5)
    ssd_conv_b = nrm((DEPTH, SSD_CONV_DIM), 0.01)
    dt0 = jnp.exp(jax.random.uniform(next(ks), (DEPTH, N_DIR, SSD_HEADS), f32, lo, hi))
    ssd_dt_bias = dt0 + jnp.log(-jnp.expm1(-dt0))
    ssd_a_log = jnp.log(jax.random.uniform(next(ks), (DEPTH, N_DIR, SSD_HEADS), f32, 1.0, 16.0))
    ssd_d = 1.0 + nrm((DEPTH, SSD_HEADS), 0.01)
    ssd_norm_g = 1.0 + nrm((DEPTH, SSD_INNER), 0.01)
    sgu_norm_g = 1.0 + nrm((DEPTH, SGU_DIM), 0.01)
    sgu_w = nrm((DEPTH, SGU_HEADS, CHUNK, CHUNK), CHUNK ** -0.5)
    sgu_b = 1.0 + nrm((DEPTH, SGU_HEADS, CHUNK), 0.01)
    s5_lambda_re = -0.5 + nrm((DEPTH, N_DIR, S5_GROUPS, S5_STATE), 0.01)
    s5_lambda_im = math.pi * jnp.arange(S5_STATE, dtype=f32) + nrm((DEPTH, N_DIR, S5_GROUPS, S5_STATE), 0.01)
    s5_log_dt = jax.random.uniform(next(ks), (DEPTH, N_DIR, S5_GROUPS), f32, lo, hi)
    s5_b_re = nrm((DEPTH, S5_GROUPS, S5_STATE, S5_GROUP_CH), (2 * S5_GROUP_CH) ** -0.5)
    s5_b_im = nrm((DEPTH, S5_GROUPS, S5_STATE, S5_GROUP_CH), (2 * S5_GROUP_CH) ** -0.5)
    s5_c_re = nrm((DEPTH, S5_GROUPS, S5_GROUP_CH, S5_STATE), (2 * S5_STATE) ** -0.5)
    s5_c_im = nrm((DEPTH, S5_GROUPS, S5_GROUP_CH, S5_STATE), (2 * S5_STATE) ** -0.5)
    s5_d = nrm((DEPTH, S5_DIM), 1.0)
    s5_glu_w = nrm((DEPTH, S5_DIM, S5_DIM), S5_DIM ** -0.5)
    s5_glu_b = nrm((DEPTH, S5_DIM), 0.01)
    w_out = nrm((DEPTH, D_MIX, D_MODEL), D_MIX ** -0.5)
    ffn_w1 = nrm((DEPTH, D_MODEL, D_FF), D_MODEL ** -0.5)
    ffn_w2 = nrm((DEPTH, D_FF, D_MODEL), D_FF ** -0.5)
    final_norm_g = 1.0 + nrm((D_MODEL,), 0.01)
    return {'x_prompt': x_prompt, 'x_sample': x_sample, 'state_ssd': state_ssd,
            'state_s5_re': state_s5_re, 'state_s5_im': state_s5_im, 'c': c, 'c_ctx': c_ctx,
            'ada_w': ada_w, 'ada_b': ada_b, 'norm1_g': norm1_g, 'norm2_g': norm2_g, 'w_in': w_in,
            'ssd_conv_w': ssd_conv_w, 'ssd_conv_b': ssd_conv_b, 'ssd_dt_bias': ssd_dt_bias,
            'ssd_a_log': ssd_a_log, 'ssd_d': ssd_d, 'ssd_norm_g': ssd_norm_g,
            'sgu_norm_g': sgu_norm_g, 'sgu_w': sgu_w, 'sgu_b': sgu_b,
            's5_lambda_re': s5_lambda_re, 's5_lambda_im': s5_lambda_im, 's5_log_dt': s5_log_dt,
            's5_b_re': s5_b_re, 's5_b_im': s5_b_im, 's5_c_re': s5_c_re, 's5_c_im': s5_c_im,
            's5_d': s5_d, 's5_glu_w': s5_glu_w, 's5_glu_b': s5_glu_b, 'w_out': w_out,
            'ffn_w1': ffn_w1, 'ffn_w2': ffn_w2, 'final_norm_g': final_norm_g}


def reference(x_prompt, x_sample, state_ssd, state_s5_re, state_s5_im, c, c_ctx,
              ada_w, ada_b, norm1_g, norm2_g, w_in, ssd_conv_w, ssd_conv_b, ssd_dt_bias,
              ssd_a_log, ssd_d, ssd_norm_g, sgu_norm_g, sgu_w, sgu_b,
              s5_lambda_re, s5_lambda_im, s5_log_dt, s5_b_re, s5_b_im, s5_c_re, s5_c_im,
              s5_d, s5_glu_w, s5_glu_b, w_out, ffn_w1, ffn_w2, final_norm_g):
    f32 = jnp.float32

    def params_of(l):
        return dict(norm1_g=norm1_g[l], norm2_g=norm2_g[l], w_in=w_in[l],
                    ssd_conv_w=ssd_conv_w[l], ssd_conv_b=ssd_conv_b[l], ssd_dt_bias=ssd_dt_bias[l],
                    ssd_a_log=ssd_a_log[l], ssd_d=ssd_d[l], ssd_norm_g=ssd_norm_g[l],
                    sgu_norm_g=sgu_norm_g[l], sgu_w=sgu_w[l], sgu_b=sgu_b[l],
                    s5_lambda_re=s5_lambda_re[l], s5_lambda_im=s5_lambda_im[l], s5_log_dt=s5_log_dt[l],
                    s5_b_re=s5_b_re[l], s5_b_im=s5_b_im[l], s5_c_re=s5_c_re[l], s5_c_im=s5_c_im[l],
                    s5_d=s5_d[l], s5_glu_w=s5_glu_w[l], s5_glu_b=s5_glu_b[l], w_out=w_out[l],
                    ffn_w1=ffn_w1[l], ffn_w2=ffn_w2[l])

    bp = x_prompt.shape[0]
    x = x_prompt
    ctx_ssd, ctx_s5 = [], []
    zero_ssd = jnp.zeros((bp, N_DIR, SSD_HEADS, SSD_HEAD_DIM, SSD_STATE), f32)
    zero_s5 = jnp.zeros((bp, N_DIR, S5_GROUPS, S5_STATE), jnp.complex64)
    for l in range(DEPTH):
        mod = (jax.nn.silu(c_ctx)[None, :] @ ada_w[l] + ada_b[l])[:, None, :]
        x, hs_ssd, hs_s5 = trunk_layer(x, mod, params_of(l), zero_ssd, zero_s5, False)
        ctx_ssd.append(hs_ssd)
        ctx_s5.append(hs_s5)
    y_prompt = rmsnorm(x, final_norm_g)
    new_state_ssd = jnp.stack(ctx_ssd, axis=1).astype(x_prompt.dtype)
    s5_all = jnp.stack(ctx_s5, axis=1)
    new_state_s5_re = jnp.real(s5_all).astype(x_prompt.dtype)
    new_state_s5_im = jnp.imag(s5_all).astype(x_prompt.dtype)

    x = x_sample
    for l in range(DEPTH):
        mod = (jax.nn.silu(c) @ ada_w[l] + ada_b[l])[:, None, :]
        h0_ssd = state_ssd[:, l].astype(f32)
        h0_s5 = lax.complex(state_s5_re[:, l].astype(f32), state_s5_im[:, l].astype(f32))
        x, _, _ = trunk_layer(x, mod, params_of(l), h0_ssd, h0_s5, True)
    y_sample = rmsnorm(x, final_norm_g)

    return (y_prompt, y_sample, new_state_ssd, new_state_s5_re, new_state_s5_im)
```

```python
import numpy as np
from contextlib import ExitStack
import concourse.bass as bass
import concourse.mybir as mybir
from concourse.bass_utils import run_bass_kernel_spmd

F32 = mybir.dt.float32
BF16 = mybir.dt.bfloat16
I32 = mybir.dt.int32
AF = mybir.ActivationFunctionType
ALU = mybir.AluOpType
AX = mybir.AxisListType

D = 1024
NT = 1536
EPS = 1e-6
TWO_PI = 6.283185307179586


class Sem:
    def __init__(self, h, name):
        self.h = h
        self.name = name
        self.total = 0


class Tile:
    def __init__(self, name, t):
        self.name = name
        self.t = t
        self.last_w = None
        self.reads = {}
        self.dsem = None


class V:
    def __init__(self, tile, ap):
        self.tile = tile
        self.ap = ap

    def __getitem__(self, key):
        return V(self.tile, self.ap[key])

    def re(self, pat, **kw):
        return V(self.tile, self.ap.rearrange(pat, **kw))

    def bc(self, shape):
        return V(self.tile, self.ap.to_broadcast(list(shape)))

    def unsq(self, ax):
        return V(self.tile, self.ap.unsqueeze(ax))

    @property
    def shape(self):
        return self.ap.shape


class Eng:
    def __init__(self, name, handle, sem):
        self.name = name
        self.h = handle
        self.sem = sem
        self.waited = {}
        self.snaps = {}


class Kern:
    def __init__(self, nc, stack):
        self.nc = nc
        self.stack = stack
        self.engs = {}
        self.nsem = 0
        self.pe_pending = []
        self.pe_pending_w = []

    def new_sem(self, name):
        h = self.stack.enter_context(self.nc.semaphore(name))
        self.nsem += 1
        return Sem(h, name)

    def setup(self):
        nc = self.nc
        for name, h in (("pe", nc.tensor), ("act", nc.scalar), ("dve", nc.vector), ("pool", nc.gpsimd), ("sp", nc.sync)):
            self.engs[name] = Eng(name, h, self.new_sem("e_" + name) if name != "sp" else None)

    def sb(self, name, shape, dtype):
        t = self.stack.enter_context(self.nc.sbuf_tensor(name, list(shape), dtype))
        tl = Tile(name, t)
        return V(tl, t[:])

    def ps(self, name, shape, dtype=F32):
        t = self.stack.enter_context(self.nc.psum_tensor(name, list(shape), dtype))
        tl = Tile(name, t)
        return V(tl, t[:])

    def _need(self, eng, reads, writes):
        need = {}

        def add(p):
            if p is None:
                return
            s, v = p
            if s.name not in need or need[s.name][1] < v:
                need[s.name] = (s, v)

        for t in reads:
            add(t.last_w)
        for t in writes:
            add(t.last_w)
            for p in t.reads.values():
                add(p)
        owner = {e.sem.name: e for e in self.engs.values() if e.sem is not None}
        for nm, (s, v) in sorted(need.items(), key=lambda kv: -kv[1][1]):
            if eng.sem is not None and s is eng.sem and eng.name == "pe":
                continue
            if eng.waited.get(nm, 0) >= v:
                continue
            eng.h.wait_ge(s.h, v)
            eng.waited[nm] = v
            ox = owner.get(nm)
            if ox is not None and ox is not eng and v in ox.snaps:
                for k2, v2 in ox.snaps[v].items():
                    if eng.waited.get(k2, 0) < v2:
                        eng.waited[k2] = v2

    def op(self, engname, fn, reads, writes, noinc=False):
        eng = self.engs[engname]
        reads = [r for r in reads if r is not None]
        if engname != "pe":
            for t in writes:
                assert t not in self.pe_pending, "write to a tile read by an unfinished matmul group: " + t.name
        self._need(eng, reads, writes)
        ins = fn(eng.h)
        if noinc:
            for t in reads:
                if t not in self.pe_pending:
                    self.pe_pending.append(t)
            for t in writes:
                if t not in self.pe_pending_w:
                    self.pe_pending_w.append(t)
            return ins
        eng.sem.total += 1
        ins.then_inc(eng.sem.h, 1)
        eng.snaps[eng.sem.total] = dict(eng.waited)
        p = (eng.sem, eng.sem.total)
        if engname == "pe":
            for t in self.pe_pending:
                if t not in writes:
                    t.reads[eng.sem.name] = p
            for t in self.pe_pending_w:
                t.last_w = p
                t.reads = {}
            self.pe_pending = []
            self.pe_pending_w = []
        for t in writes:
            t.last_w = p
            t.reads = {}
        for t in reads:
            if t not in writes:
                t.reads[eng.sem.name] = p
        return ins

    def dma(self, q, out, in_, **kw):
        eng = self.engs[q]
        reads = [in_.tile] if isinstance(in_, V) else []
        writes = [out.tile] if isinstance(out, V) else []
        self._need(eng, reads, writes)
        st = writes[0] if writes else reads[0]
        if st.dsem is None:
            st.dsem = {}
        qk = "sw" if q == "pool" else "hw"
        if qk not in st.dsem:
            st.dsem[qk] = self.new_sem("d%s_%s" % (qk, st.name))
        dsem = st.dsem[qk]
        oap = out.ap if isinstance(out, V) else out
        iap = in_.ap if isinstance(in_, V) else in_
        ins = eng.h.dma_start(out=oap, in_=iap, **kw)
        dsem.total += 16
        ins.then_inc(dsem.h, 16)
        p = (dsem, dsem.total)
        for t in writes:
            t.last_w = p
            t.reads = {}
        for t in reads:
            t.reads[dsem.name] = p

    def wait_all(self, q, tiles):
        self._need(self.engs[q], [], tiles)


K = None


def _tiles(*vs):
    return [v.tile for v in vs if isinstance(v, V)]


def _a(v):
    return v.ap if isinstance(v, V) else v


def mm(out, lhsT, rhs, start=True, stop=True):
    K.op("pe", lambda e: e.matmul(out.ap, lhsT.ap, rhs.ap, start=start, stop=stop), _tiles(lhsT, rhs), _tiles(out), noinc=(not stop))


def tr(out, in_, ident):
    K.op("pe", lambda e: e.transpose(out.ap, in_.ap, ident.ap), _tiles(in_, ident), _tiles(out))


def act(out, in_, func, bias=None, scale=None, eng="act"):
    kw = {}
    if bias is not None:
        kw["bias"] = _a(bias)
    if scale is not None:
        kw["scale"] = _a(scale)
    K.op("act", lambda e: e.activation(out.ap, in_.ap, func, **kw), _tiles(in_, bias, scale), _tiles(out))


def tt(out, a, b, op, eng="dve"):
    K.op(eng, lambda e: e.tensor_tensor(out.ap, a.ap, b.ap, op), _tiles(a, b), _tiles(out))


def ts(out, a, s1, op0, s2=None, op1=None, eng="dve"):
    if op1 is None:
        K.op(eng, lambda e: e.tensor_scalar(out.ap, a.ap, _a(s1), None, op0), _tiles(a, s1), _tiles(out))
    else:
        K.op(eng, lambda e: e.tensor_scalar(out.ap, a.ap, _a(s1), _a(s2), op0, op1), _tiles(a, s1, s2), _tiles(out))


def stt(out, a, s, b, op0, op1, eng="dve"):
    K.op(eng, lambda e: e.scalar_tensor_tensor(out.ap, a.ap, _a(s), b.ap, op0, op1), _tiles(a, s, b), _tiles(out))


def cp(out, a, eng="dve"):
    if eng == "act":
        K.op("act", lambda e: e.copy(out.ap, a.ap), _tiles(a), _tiles(out))
    else:
        K.op(eng, lambda e: e.tensor_copy(out.ap, a.ap), _tiles(a), _tiles(out))


def memset(out, val, eng="dve"):
    K.op(eng, lambda e: e.memset(out.ap, val), [], _tiles(out))


def scan(out, d0, d1, init, op0=ALU.mult, op1=ALU.add):
    K.op("dve", lambda e: e.tensor_tensor_scan(out.ap, d0.ap, d1.ap, _a(init), op0, op1), _tiles(d0, d1, init), _tiles(out))


def build_program(nc, dbg_names=(), mode='all'):
    global K
    PASS_P = ("P", 0, 512, [(0, 512, 0)], [(0, 512, [(0, 256), (256, 256)], 0)])
    if mode == 'all':
        NTT = 1536; NTM = 1024; ML = 1024
        PASSES = [("S", 512, 1024, [(0, 512, 1), (512, 512, 1)], [(0, 1024, [(0, 1024)], 1)]), PASS_P]
    elif mode == 'P':
        NTT = 512; NTM = 512; ML = 512
        PASSES = [PASS_P]
    else:
        NTT = 1024; NTM = 1024; ML = 1024
        PASSES = [("S", 0, 1024, [(0, 512, 1), (512, 512, 1)], [(0, 1024, [(0, 1024)], 1)])]
    NT = NTM
    BLKS = None
    GROUPS = None
    pname = None
    row0 = 0
    dram_in = {}
    dram_out = {}

    def din(name, shape, dt=F32):
        dram_in[name] = nc.dram_tensor(name, list(shape), dt, kind="ExternalInput").ap()
        return dram_in[name]

    def dout(name, shape):
        dram_out[name] = nc.dram_tensor(name, list(shape), F32, kind="ExternalOutput").ap()
        return dram_out[name]

    xin = din("xin", [NTT, D])
    st_ssd = din("st_ssd", [2, 2, 512, 128])
    st_re = din("st_re", [2, 2, 1024])
    st_im = din("st_im", [2, 2, 1024])
    cvec = din("cvec", [2, D])
    ada_w = din("ada_w", [2, D, 6 * D])
    ada_b = din("ada_b", [2, 6 * D])
    norm1_g = din("norm1_g", [2, D])
    norm2_g = din("norm2_g", [2, D])
    w_in = din("w_in", [2, D, 2320])
    conv_w = din("ssd_conv_w", [2, 5, D])
    conv_b = din("ssd_conv_b", [2, D])
    dt_bias = din("ssd_dt_bias", [2, 16])
    a_log = din("ssd_a_log", [2, 16])
    ssd_d = din("ssd_d", [2, 8])
    ssd_ng = din("ssd_norm_g", [2, 512])
    sgu_ng = din("sgu_norm_g", [2, 256])
    sgu_w = din("sgu_w", [2, 4, 128, 128])
    sgu_b = din("sgu_b", [2, 4, 128])
    lam_re = din("s5_lambda_re", [2, 2, 1024])
    lam_im = din("s5_lambda_im", [2, 2, 1024])
    log_dt = din("s5_log_dt", [2, 2, 16])
    b_re = din("s5_b_re", [2, 1024, 16])
    b_im = din("s5_b_im", [2, 1024, 16])
    c_re = din("s5_c_re", [2, 16, 16, 64])
    c_im = din("s5_c_im", [2, 16, 16, 64])
    s5_d = din("s5_d", [2, 256])
    glu_w = din("s5_glu_w", [2, 256, 256])
    glu_b = din("s5_glu_b", [2, 256])
    w_out = din("w_out", [2, D, D])
    ffn_w1 = din("ffn_w1", [2, D, 4 * D])
    ffn_w2 = din("ffn_w2", [2, 4 * D, D])
    fin_g = din("final_norm_g", [D])
    cst = din("consts", [128, 8 * 128 + 10])
    emat = din("emat", [128, 16 * 128])
    posrow = din("posrow", [1, 1024])

    yout = dout("yout", [NTT, D])
    ns_ssd = dout("ns_ssd", [2, 2, 2, 512, 128])
    ns_s5 = dout("ns_s5", [128, 128])
    dbg_out = {}

    del EXTRA_OUT_TILES[:]
    stack = ExitStack()
    with stack:
        K = Kern(nc, stack)
        K.setup()
        del PHASES[:]

        def mark(label):
            PHASES.append((label, {e: (g.sem.total if g.sem else 0) for e, g in K.engs.items()}))

        sb, ps = K.sb, K.ps

        XT = sb("XT", [128, 8, NTM], F32)
        HTB = [sb("HT%d" % i, [128, 8, 512], BF16) for i in range(NTM // 512)]

        class _HTW:
            def __getitem__(self, key):
                p_, k_, c_ = key
                a_ = c_.start or 0
                blk_ = a_ // 512
                assert (c_.stop - 1) // 512 == blk_
                return HTB[blk_][p_, k_, a_ - 512 * blk_:c_.stop - 512 * blk_]

        HT = _HTW()
        WB = [sb("WB%d" % i, [128, 4096], BF16) for i in range(2)] + [sb("WB%d" % i, [128, 4 * ML], BF16) for i in range(2, 4)]
        UP = sb("UP", [128, max(6 * ML, 4 * NTM)], BF16)
        CST = sb("CST", [128, 8 * 128 + 10], F32)
        IDB = sb("IDB", [128, 128], BF16)
        ONB = sb("ONB", [128, 128], BF16)
        PSB = [ps("PS%d" % i, [128, 512]) for i in range(8)]
        psi = [0]

        ada_all_done = [False]

        def nps():
            nb = 8 if ada_all_done[0] else 6
            p = PSB[psi[0] % nb]
            psi[0] += 1
            return p

        IDF = CST[:, 0:128]
        ONF = CST[:, 128:256]
        TRI_LE = CST[:, 256:384]
        TRI_GE = CST[:, 384:512]
        TRI_GT = CST[:, 512:640]
        TRI_LT = CST[:, 640:768]
        MG2 = CST[:, 768:770]
        MG2N = sb("MG2N", [128, 2], F32)
        MLO = CST[:, 770:898]
        MUP = CST[:, 898:1026]
        MROW = CST[:, 1026:1034]

        K.dma("sp", CST, cst)
        ts(MG2N, MG2, -1.0, ALU.mult)
        K.dma("pool", IDB, cst[:, 0:128])
        K.dma("pool", ONB, cst[:, 128:256])

        def dump(name, v, shape):
            if name in dbg_names:
                o = nc.dram_tensor("dbg_" + name, list(shape), v.ap.dtype, kind="ExternalOutput").ap()
                dbg_out[name] = o
                K.dma("sp", o, v)
                dump_tiles.append(v.tile)

        dump_tiles = []

        PT = {}

        def stage(name, rows):
            n = sum(r[1].shape[0] for r in rows)
            stg = sb("stg_" + name, [n, 128], F32)
            off = 0
            cols = {}
            for key, ap in rows:
                r = ap.shape[0]
                K.dma("sp", stg[off:off + r, :], ap)
                cols[key] = (off, r)
                off += r
            pt = sb("pt_" + name, [128, n], F32)
            p = nps()
            mm(p[:, 0:n], stg[0:n, :], IDF[0:n, 0:n])
            cp(pt, p[:, 0:n])
            for key, (o, r) in cols.items():
                PT[key] = pt[:, o:o + r]

        r128 = lambda ap: ap.rearrange("(t p) -> t p", p=128)
        rowsA = []
        for l in range(2):
            rowsA += [("n1g%d" % l, r128(norm1_g[l])), ("n2g%d" % l, r128(norm2_g[l])), ("convb%d" % l, r128(conv_b[l])),
                      ("ssdng%d" % l, r128(ssd_ng[l])), ("sgung%d" % l, r128(sgu_ng[l])), ("s5d%d" % l, r128(s5_d[l])),
                      ("glub%d" % l, r128(glu_b[l]))]
        rowsA += [("fng", r128(fin_g)), ("cv0", r128(cvec[0])), ("cv1", r128(cvec[1]))]
        stage("A", rowsA)
        stage("B", [("adab%d" % l, r128(ada_b[l])) for l in range(2)])
        rowsC = []
        r128d = lambda ap: ap.rearrange("d (t p) -> (d t) p", p=128)
        for l in range(2):
            rowsC += [("lreL%d" % l, r128d(lam_re[l])), ("limL%d" % l, r128d(lam_im[l])),
                      ("sreL%d" % l, r128d(st_re[l])), ("simL%d" % l, r128d(st_im[l]))]
        stage("C", rowsC)
        for l in range(2):
            for d in range(2):
                for nm in ("lre", "lim", "sre", "sim"):
                    PT["%s%d%d" % (nm, l, d)] = PT["%sL%d" % (nm, l)][:, d * 8:(d + 1) * 8]
        rowsD = []
        for l in range(2):
            for tap in range(5):
                rowsD.append(("cw%d%d" % (l, tap), r128(conv_w[l, tap])))
        stage("D", rowsD)

        SC = sb("SC", [128, 2, 8], BF16)
        act(SC[:, 0, :], PT["cv0"], AF.Silu)
        act(SC[:, 1, :], PT["cv1"], AF.Silu)
        wbi = [0]

        wc_slots = {}
        wc = nc.dram_tensor("wcache", [48, 128, 4096], BF16).ap() if len(PASSES) > 1 else None
        pass_idx = [0]

        def load_w(src_ap, ncols_total, key=None):
            w = WB[wbi[0] % 2]
            wbi[0] += 1
            t = src_ap.shape[0] // 128
            n = src_ap.shape[1]
            flat = w[:, 0:t * n]
            view = flat.re("p (t n) -> p t n", t=t)
            if key is not None and wc is not None and pass_idx[0] > 0:
                K.dma("sp", flat, wc[wc_slots[key]][:, 0:t * n])
                return view
            K.dma("pool", view, src_ap.rearrange("(t p) n -> p t n", p=128))
            if key is not None and wc is not None:
                wc_slots[key] = len(wc_slots)
                K.dma("sp", wc[wc_slots[key]][:, 0:t * n], flat)
            return view


        FT = [sb("FT%d" % i, [128, 1024 if i < 2 else ML], F32) for i in range(3)]
        XS = None
        def load_x():
            XS_ = [FT[2], FT[3]] if ML == 1024 else [FT[0], FT[1]]
            for tb in range(NT // 128):
                xs = XS_[tb % 2]
                K.dma("sp", xs, xin[row0 + tb * 128:row0 + (tb + 1) * 128, :])
                for half in range(2):
                    p = nps()
                    for q in range(4):
                        t = half * 4 + q
                        tr(p[:, q * 128:(q + 1) * 128], xs[:, t * 128:(t + 1) * 128], IDF)
                    cp(XT[:, half * 4:half * 4 + 4, tb * 128:(tb + 1) * 128], p.re("p (q n) -> p q n", q=4), eng=("act" if half else "dve"))

        YP = sb("YP", [128, max(4 * ML, 4096)], BF16)
        SQ = YP[:, 0:4096].re("p (k n) -> p k n", k=8)
        RS = sb("RS", [128, 512], F32)
        NTMP = sb("NTMP", [128, 512], F32)
        NTMP2 = sb("NTMP2", [128, 512], F32)

        def rstd_from_sq(sqv, ntile, n, dim):
            p = nps()
            for k in range(ntile):
                mm(p[:, 0:n], ONB, sqv[:, k, :], start=(k == 0), stop=(k == ntile - 1))
            act(RS[:, 0:n], p[:, 0:n], AF.Ln, bias=float(dim * EPS))
            act(RS[:, 0:n], RS[:, 0:n], AF.Exp, scale=-0.5)

        def norm_to_HT(l, which, final=False, blks=None):
            for (c0, n, v) in (blks if blks is not None else BLKS):
                act(SQ[:, 0:4, :], XT[:, 0:4, c0:c0 + n], AF.Square)
                act(SQ[:, 4:8, :], XT[:, 4:8, c0:c0 + n], AF.Square)
                rstd_from_sq(SQ, 8, n, D)
                for k in range(8):
                    if final:
                        A = FNA
                    else:
                        A, B = NA[(l, v, which)]
                    nt_ = NTMP if k % 2 == 0 else NTMP2
                    stt(nt_, XT[:, k, c0:c0 + n], A[:, k:k + 1], RS, ALU.mult, ALU.mult)
                    if final:
                        yield (c0, k, nt_)
                    else:
                        act(HT[:, k, c0:c0 + n], nt_, AF.Identity, bias=B[:, k:k + 1])

        XBC = sb("XBC", [128, 8 * (ML + 8)], BF16)
        BCT = sb("BCT", [128, 4 * ML], BF16)
        BMT = sb("BMT", [128, max(2 * ML, 2048)], BF16)
        XBCf = V(XBC.tile, XBC.ap.bitcast(F32))
        for i_ in range(3, 7):
            FT.append(XBCf[:, (i_ - 3) * ML:(i_ - 2) * ML])
        WB3f = V(WB[3].tile, WB[3].ap.bitcast(F32))
        FT.append(WB3f[:, 0:ML])
        TI32 = V(WB[3].tile, WB[3].ap.bitcast(I32))[:, ML:2 * ML]
        MEXPS = [sb("MEXP%d" % i, [128, 1024], BF16) for i in range(2)]
        MMTS = [sb("MMT%d" % i, [128, 1024], BF16) for i in range(2)]
        XDT = [sb("XDT%d" % i, [128, 512], BF16) for i in range(2)]
        CBM = [sb("CBM%d" % i, [128, 256], BF16) for i in range(2)]
        WDT = sb("WDT", [128, 128], BF16)
        DG = sb("DG", [128, 8 * 5 * 128], BF16)
        DD = sb("DD", [128, 8 * 128], BF16)
        CBR = sb("CBR", [1, 1024], BF16)
        SGB = sb("SGB", [1, 512], BF16)
        WST = sb("WST", [128, 512], BF16)
        WSL = sb("WSL", [128, 128], BF16)
        DTB = sb("DTB", [128, 16], F32)
        NEGA = sb("NEGA", [128, 16], F32)
        SDD = sb("SDD", [128, 8], F32)
        DT = sb("DT", [128, 8, 16], F32)
        ADT = sb("ADT", [128, 8, 16], F32)
        ACUM = sb("ACUM", [128, 2, 8, 8], F32)
        TOT = sb("TOT", [128, 2, 8, 8], F32)
        DTE = sb("DTE", [128, 2, 8, 8], F32)
        EA = sb("EA", [128, 2, 8, 8], F32)
        CDC = sb("CDC", [128, 2, 8, 8], F32)
        HST = [sb("HST%d" % i, [128, 512], F32) for i in range(2)]
        HSTB = [sb("HSTB%d" % i, [128, 512], BF16) for i in range(2)]
        STG4 = FT[2][:, 0:512]
        SSO = [FT[i][:, 512:1024].re("p (q n) -> p q n", q=4) for i in range(2)]
        NS5 = sb("NS5", [128, 128], F32)
        NS5T = sb("NS5T", [128, 128], F32)
        SSNG = sb("SSNG", [128, 4], F32)
        SGNG = sb("SGNG", [128, 2], F32)
        BT = DG[:, 0:32 * 128]
        CT = sb("CT", [128, 16 * 128], BF16)
        CQ = [BMT[:, 0:2048].re("p (r q n) -> p r q n", r=2, q=8), CT.re("p (r q n) -> p r q n", r=2, q=8)]
        BQv = BT.re("p (d r q n) -> p d r q n", d=2, r=2, q=8)
        TSv = MMTS[0].re("p (q n) -> p q n", q=8)
        EM = sb("EM", [128, 16 * 128], BF16)
        K.dma("pool", EM, emat)
        EMv = EM.re("p (a b n) -> p a b n", a=4, b=4)
        POS2 = sb("POS2", [128, 256], F32)
        K.dma("sp", POS2, posrow[:, 0:256].partition_broadcast(128))
        DBLK = sb("DBLK", [128, 8], F32)
        CN = [sb("CN%d" % i, [128, 8, 16], F32) for i in range(2)]
        RHO4L = [[sb("RHO4%d%d" % (l_, d), [128, 8], F32) for d in range(2)] for l_ in range(2)]
        TH4L = [[sb("TH4%d%d" % (l_, d), [128, 8], F32) for d in range(2)] for l_ in range(2)]
        H0LL = [[[sb("H0LL%d%d%d" % (l_, d, ri), [128, 8], F32) for ri in range(2)] for d in range(2)] for l_ in range(2)]
        s5w = [nc.dram_tensor("s5w%d" % l_, [128, 9216], BF16).ap() for l_ in range(2)]
        GW = sb("GW", [128, 2 * 256], BF16)
        BRAW = [sb("BRAW%d" % i, [128, 8, 16], F32) for i in range(2)]
        CC = [FT[i][0:16, :].re("c (g s) -> c g s", g=16) for i in range(2)]
        HSB = [YP[:, 0:ML], YP[:, ML:2 * ML]]
        Y5 = BCT[:, 0:2 * ML].re("p (t n) -> p t n", t=2)
        YG = BCT[:, 2 * ML:4 * ML].re("p (t n) -> p t n", t=2)
        SIG = NTMP
        memset(NS5, 0.0)
        PI = 3.141592653589793

        def sincos(out_s, out_c, ang, tmpf, tmpi):
            ts(tmpi, ang, 1.0 / TWO_PI, ALU.mult)
            cp(tmpf, tmpi)
            stt(tmpf, tmpf, -TWO_PI, ang, ALU.mult, ALU.add)
            ts(tmpf, tmpf, PI, ALU.min, -PI, ALU.max)
            act(out_s, tmpf, AF.Sin)
            act(tmpf, tmpf, AF.Abs)
            act(out_c, tmpf, AF.Sin, bias=PI / 2, scale=-1.0)

        def cmul(or_, oi_, ar, ai, br, bi, t0, t1):
            tt(or_, ar, br, ALU.mult)
            tt(t0, ai, bi, ALU.mult)
            tt(or_, or_, t0, ALU.subtract)
            tt(oi_, ar, bi, ALU.mult)
            tt(t1, ai, br, ALU.mult)
            tt(oi_, oi_, t1, ALU.add)


        LDT2 = sb("LDT2", [128, 32], F32)
        S16 = {nm: sb("S16_" + nm, [128, 16], F32) for nm in
               ("step", "are", "the", "lbr", "lbi", "t0", "t1", "t2", "qr", "qi", "nr")}

        def s5_pre(l, startup=True):
            if startup:
                hs0 = HST[0]
                hs1 = HST[1]
                G = [None, None, hs1, V(MEXPS[0].tile, MEXPS[0].ap.bitcast(F32)), None, None]
                PWN = V(MEXPS[1].tile, MEXPS[1].ap.bitcast(I32))[:, 0:128]
            else:
                f32v = lambda T_: V(T_.tile, T_.ap.bitcast(F32))
                Gb, Gy, Gw = f32v(BCT), f32v(YP), f32v(WB[2])
                G = [Gb[:, 0:1024], Gb[:, 1024:2048], Gy[:, 0:1024], Gy[:, 1024:2048], None, Gw[:, 0:1024]]
                PWN = V(WB[2].tile, WB[2].ap.bitcast(I32))[:, 1024:1152]
            PWR, PWI, PWA, PWB = (G[3][:, i * 128:(i + 1) * 128] for i in range(4))
            if startup:
                BBD = [hs0[:, i * 256:(i + 1) * 256].re("p (d q c) -> p d q c", d=2, q=8) for i in range(2)]
                BBT = [hs1[:, i * 256:(i + 1) * 256].re("p (d q c) -> p d q c", d=2, q=8) for i in range(2)]
            else:
                BBD = [G[5][:, i * 256:(i + 1) * 256].re("p (d q c) -> p d q c", d=2, q=8) for i in range(2)]
                BBT = [G[5][:, (2 + i) * 256:(3 + i) * 256].re("p (d q c) -> p d q c", d=2, q=8) for i in range(2)]
            K.dma("sp", BRAW[0], b_re[l].rearrange("(pr p) c -> p pr c", p=128))
            K.dma("sp", BRAW[1], b_im[l].rearrange("(pr p) c -> p pr c", p=128))
            K.dma("sp", CC[0], c_re[l].rearrange("g c s -> c g s"))
            K.dma("sp", CC[1], c_im[l].rearrange("g c s -> c g s"))
            K.dma("sp", LDT2, log_dt[l:l + 1].rearrange("o d g -> o (d g)").partition_broadcast(128))
            with nc.allow_non_contiguous_dma(reason="tiny D-skip gather"):
                for j0 in range(4):
                    K.dma("sp", DBLK[32 * j0:32 * j0 + 32, :], s5_d[l].rearrange("(q r) -> r q", r=32))
            yield
            pcn = nps()
            for ri in range(2):
                for pair in range(8):
                    mm(pcn[:, (ri * 8 + pair) * 16:(ri * 8 + pair + 1) * 16], CC[ri][:, 2 * pair:2 * pair + 2, :].re("c g s -> c (g s)"), IDF[0:16, 0:16])
            for ri in range(2):
                cp(CN[ri], pcn[:, ri * 128:(ri + 1) * 128].re("p (q c) -> p q c", q=8), eng="act")
            S = S16
            d8 = lambda T_: T_.re("p (d q) -> p d q", d=2)
            LV = LDT2.re("p (d q g) -> p d q g", d=2, g=2)
            ts(d8(S["step"]), LV[:, :, :, 0], MG2[:, 0:1], ALU.mult)
            stt(d8(S["step"]), LV[:, :, :, 1], MG2[:, 1:2], d8(S["step"]), ALU.mult, ALU.add)
            act(S["step"], S["step"], AF.Exp)
            lre = PT["lreL%d" % l]
            lim = PT["limL%d" % l]
            tt(S["are"], lre, S["step"], ALU.mult)
            tt(S["the"], lim, S["step"], ALU.mult)
            p4v = lambda T_: T_.re("p (d m q) -> p d m q", d=2, m=8)
            mrow = MROW.unsq(1).unsq(3).bc([128, 2, 8, 8])
            tt(p4v(PWA), d8(S["are"]).unsq(2).bc([128, 2, 8, 8]), mrow, ALU.mult)
            act(PWA, PWA, AF.Exp)
            tt(p4v(PWB), d8(S["the"]).unsq(2).bc([128, 2, 8, 8]), mrow, ALU.mult)
            sincos(PWI, PWR, PWB, PWB, PWN) if False else None
            ts(PWN, PWB, 1.0 / TWO_PI, ALU.mult)
            stt(PWB, PWB, 1.0 / TWO_PI, PWN, ALU.mult, ALU.subtract)
            act(PWI, PWB, AF.Sin, scale=TWO_PI)
            act(PWB, PWB, AF.Abs)
            act(PWR, PWB, AF.Sin, bias=PI / 2, scale=-TWO_PI)
            tt(PWR, PWR, PWA, ALU.mult)
            tt(PWI, PWI, PWA, ALU.mult)
            PR = p4v(PWR)
            PI_ = p4v(PWI)
            RHO4, TH4, H0L = RHO4L[l], TH4L[l], H0LL[l]
            for d in range(2):
                cp(RHO4[d], p4v(PWA)[:, d, 7, :], eng="act")
                ts(TH4[d], S["the"][:, d * 8:(d + 1) * 8], 4.0 / TWO_PI, ALU.mult)
            cp(d8(S["lbr"]), PR[:, :, 4, :])
            cp(d8(S["lbi"]), PI_[:, :, 4, :])
            ts(S["nr"], S["lbr"], -1.0, ALU.add)
            tt(S["t1"], lre, lre, ALU.mult)
            tt(S["t2"], lim, lim, ALU.mult)
            tt(S["t1"], S["t1"], S["t2"], ALU.add)
            K.op("dve", lambda e: e.reciprocal(S["t1"].ap, S["t1"].ap), [S["t1"].tile], [S["t1"].tile])
            tt(S["qr"], S["nr"], lre, ALU.mult)
            tt(S["t2"], S["lbi"], lim, ALU.mult)
            tt(S["qr"], S["qr"], S["t2"], ALU.add)
            tt(S["qr"], S["qr"], S["t1"], ALU.mult)
            tt(S["qi"], S["lbi"], lre, ALU.mult)
            tt(S["t2"], S["nr"], lim, ALU.mult)
            tt(S["qi"], S["qi"], S["t2"], ALU.subtract)
            tt(S["qi"], S["qi"], S["t1"], ALU.mult)
            qrb = d8(S["qr"]).unsq(3).bc([128, 2, 8, 16])
            qib = d8(S["qi"]).unsq(3).bc([128, 2, 8, 16])
            brb = BRAW[0].unsq(1).bc([128, 2, 8, 16])
            bib = BRAW[1].unsq(1).bc([128, 2, 8, 16])
            cmul(BBD[0], BBD[1], brb, bib, qrb, qib, BBT[0], BBT[1])
            h0r, h0i = PT["sreL%d" % l], PT["simL%d" % l]
            p4r, p4i = S["t0"], S["t1"]
            cp(d8(p4r), PR[:, :, 7, :])
            cp(d8(p4i), PI_[:, :, 7, :])
            cmul(S["qr"], S["qi"], p4r, p4i, h0r, h0i, S["t2"], S["nr"])
            for d in range(2):
                cp(H0L[d][0], S["qr"][:, d * 8:(d + 1) * 8], eng="act")
                cp(H0L[d][1], S["qi"][:, d * 8:(d + 1) * 8], eng="act")
            yield
            t4 = lambda T_: T_.re("p (m q c) -> p m q c", m=8, q=8)
            av = lambda T_: T_.re("p (q i g c) -> p q i g c", q=8, i=4, g=2)
            pv_ = lambda T_, pair: T_[:, pair * 128:(pair + 1) * 128]
            TST = [NTMP.re("p (q n) -> p q n", q=4), RS.re("p (q n) -> p q n", q=4)]

            def asm(dst, tab, s0, step, sign):
                dv = av(dst)
                if step > 0:
                    src = t4(tab)[:, s0:s0 + 4]
                else:
                    src = t4(tab)[:, s0 - 3:s0 + 1][:, ::-1]
                srcq = src.re("p m q c -> p q m c")
                mg = MG2 if sign > 0 else MG2N
                for g2 in range(2):
                    ts(dv[:, :, :, g2, :], srcq, mg[:, g2:g2 + 1], ALU.mult)

            def per_d(d, XR, XI, YR, YI, A5, A6, A7, A2, TA, TB, TC):
                prb = PR[:, d].unsq(3).bc([128, 8, 8, 16])
                pib = PI_[:, d].unsq(3).bc([128, 8, 8, 16])
                cmul(t4(XR), t4(XI), BBD[0][:, d].unsq(1).bc([128, 8, 8, 16]), BBD[1][:, d].unsq(1).bc([128, 8, 8, 16]), prb, pib, t4(TA), t4(TB))
                yield
                cmul(t4(YR), t4(YI), CN[0].unsq(1).bc([128, 8, 8, 16]), CN[1].unsq(1).bc([128, 8, 8, 16]), prb, pib, t4(TA), t4(TB))
                yield
                if d == 0:
                    asm(A5, XR, 3, -1, 1.0)
                    asm(A6, XI, 3, -1, -1.0)
                    asm(A7, YR, 3, 1, 1.0)
                    asm(A2, YI, 3, 1, 1.0)
                else:
                    asm(A5, XR, 3, 1, 1.0)
                    asm(A6, XI, 3, 1, -1.0)
                    asm(A7, YR, 3, -1, 1.0)
                    asm(A2, YI, 3, -1, 1.0)
                yield
                pTs = [nps(), nps()]
                for half in range(2):
                    for p4 in range(4):
                        pair = half * 4 + p4
                        o = pTs[half][:, p4 * 128:(p4 + 1) * 128]
                        mm(o, pv_(A5, pair), pv_(A7, pair), start=True, stop=False)
                        mm(o, pv_(A6, pair), pv_(A2, pair), start=False, stop=True)
                for half in range(2):
                    src = pTs[half].re("p (q n) -> p q n", q=4)
                    if d == 0:
                        tt(TST[half], src, MLO.unsq(1).bc([128, 4, 128]), ALU.mult)
                    else:
                        tmpT = TC[:, half * 512:(half + 1) * 512].re("p (q n) -> p q n", q=4)
                        tt(tmpT, src, MUP.unsq(1).bc([128, 4, 128]), ALU.mult)
                        tt(TST[half], TST[half], tmpT, ALU.add)
                yield
                for ri in range(2):
                    if d == 0:
                        asm(A5 if ri == 0 else A7, XR if ri == 0 else XI, 6, -1, 1.0)
                    else:
                        asm(A5 if ri == 0 else A7, XR if ri == 0 else XI, 3, 1, 1.0)
                yield
                for ri in range(2):
                    srcA = A5 if ri == 0 else A7
                    for half in range(2):
                        p = nps()
                        for p4 in range(4):
                            mm(p[:, p4 * 128:(p4 + 1) * 128], pv_(srcA, half * 4 + p4), IDF)
                        cp(BQv[:, d, ri, half * 4:half * 4 + 4, :], p.re("p (q n) -> p q n", q=4), eng="act")
                    yield
                for ri in range(2):
                    dstA = A6 if ri == 0 else A2
                    if d == 0:
                        asm(dstA, YR if ri == 0 else YI, 4, 1, 1.0 if ri == 0 else -1.0)
                    else:
                        asm(dstA, YR if ri == 0 else YI, 7, -1, 1.0 if ri == 0 else -1.0)
                    cp(CQ[d][:, ri].re("p q n -> p (q n)"), dstA, eng="act")
                yield
            f32v_ = lambda T_: V(T_.tile, T_.ap.bitcast(F32))
            if startup:
                w2f = f32v_(WB[2])
                htf0 = f32v_(HTB[0]).re("p k n -> p (k n)")
                htf1 = f32v_(HTB[1]).re("p k n -> p (k n)")
                upf = f32v_(UP)
                bcf = f32v_(BCT)
                ypf = f32v_(YP)
                g0 = per_d(0, FT[0], FT[1], FT[3], FT[4], FT[5], FT[6], FT[7], FT[2], w2f[:, 0:1024], w2f[:, 1024:2048], G[2])
                g1 = per_d(1, htf0[:, 0:1024], htf0[:, 1024:2048], htf1[:, 0:1024], htf1[:, 1024:2048],
                           upf[:, 0:1024], upf[:, 1024:2048], upf[:, 2048:3072], bcf[:, 0:1024], bcf[:, 1024:2048], ypf[:, 0:1024], ypf[:, 1024:2048])
                alive = [g0, g1]
                while alive:
                    for g_ in list(alive):
                        try:
                            next(g_)
                        except StopIteration:
                            alive.remove(g_)
                    yield
            else:
                for d_ in range(2):
                    yield from per_d(d_, FT[0], FT[1], FT[3], FT[4], FT[5], FT[6], FT[7], FT[2], G[0], G[1], G[2])
            for pair in range(8):
                stt(TSv[:, pair, :], IDF, DBLK[:, pair:pair + 1], TST[pair // 4][:, pair % 4, :], ALU.mult, ALU.add)
            K.dma("sp", s5w[l][:, 0:4096], BT)
            K.dma("sp", s5w[l][:, 4096:6144], BMT[:, 0:2048])
            K.dma("sp", s5w[l][:, 6144:8192], CT)
            K.dma("sp", s5w[l][:, 8192:9216], MMTS[0])
            yield

        def s5_load(l):
            K.dma("pool", GW.re("p (t n) -> p t n", t=2), glu_w[l].rearrange("(t p) n -> p t n", p=128))
            K.dma("sp", BT, s5w[l][:, 0:4096])
            K.dma("sp", BMT[:, 0:2048], s5w[l][:, 4096:6144])
            K.dma("sp", CT, s5w[l][:, 6144:8192])
            K.dma("sp", MMTS[0], s5w[l][:, 8192:9216])

        def mixers(l):
            K.dma("sp", DTB, dt_bias[l:l + 1, :].partition_broadcast(128))
            K.dma("sp", NEGA, a_log[l:l + 1, :].partition_broadcast(128))
            K.dma("sp", SDD, ssd_d[l:l + 1, :].partition_broadcast(128))
            act(NEGA, NEGA, AF.Exp)
            ts(NEGA, NEGA, -1.0, ALU.mult)
            K.dma("pool", CBR, conv_b[l:l + 1, :])
            K.dma("pool", SGB, sgu_b[l:l + 1].rearrange("o h q -> o (h q)"))
            DGv = DG.re("p (t a n) -> p t a n", t=8, a=5)
            for t in range(8):
                for tap in range(5):
                    ts(DGv[:, t, tap, :], IDB, PT["cw%d%d" % (l, tap)][:, t:t + 1], ALU.mult)
            DDv = DD.re("p (h n) -> p h n", h=8)
            for h in range(8):
                ts(DDv[:, h, :], IDB, SDD[:, h:h + 1], ALU.mult)
            ts(SSNG, PT["ssdng%d" % l], float(np.sqrt(512.0)), ALU.mult)
            ts(SGNG, PT["sgung%d" % l], float(np.sqrt(256.0)), ALU.mult)
            WSTv = WST.re("p (h q) -> p h q", h=4)
            for h in range(4):
                K.dma("pool", WSL, sgu_w[l, h])
                p = nps()
                mm(p[:, 0:128], WSL, IDB)
                cp(WSTv[:, h, :], p[:, 0:128])
            groups = GROUPS
            for gi, (g0, GL, seqs, v) in enumerate(groups):
                SL = seqs[0][1]
                nseq = len(seqs)
                nblk = GL // 512
                ZS = UP[:, 0:4 * GL].re("p (t n) -> p t n", t=4)
                GV = UP[:, 4 * ML:4 * ML + 2 * GL].re("p (t n) -> p t n", t=2)
                GU = WB[2][:, 0:2 * GL].re("p (t n) -> p t n", t=2)
                U5 = WB[2][:, 2 * ML:2 * ML + 2 * GL].re("p (t n) -> p t n", t=2)
                XBCv = XBC[:, 0:8 * nseq * (SL + 4)].re("p (t s n) -> p t s n", t=8, s=nseq)
                memset(XBCv[:, :, :, 0:2], 0.0)
                memset(XBCv[:, :, :, SL + 2:SL + 4], 0.0)

                def xbc_dst(t, b):
                    res = []
                    if nseq == 2:
                        for s_ in range(2):
                            res.append((XBCv[:, t, s_, 2:2 + 256], (s_ * 256, 256)))
                    else:
                        res.append((XBCv[:, t, 0, 2 + b * 512:2 + (b + 1) * 512], (0, 512)))
                    return res

                def proj(wv, m, b):
                    p = nps()
                    c0 = g0 + b * 512
                    for k in range(8):
                        mm(p, wv[:, k, m * 128:(m + 1) * 128], HT[:, k, c0:c0 + 512], start=(k == 0), stop=(k == 7))
                    return p

                wv = load_w(w_in[l][:, 0:512], 512, key=("win", l, 0))
                for m in range(4):
                    for b in range(nblk):
                        act(ZS[:, m, b * 512:(b + 1) * 512], proj(wv, m, b), AF.Silu)
                for part in range(2):
                    wv = load_w(w_in[l][:, 512 + part * 512:1024 + part * 512], 512, key=("win", l, 1 + part))
                    for m in range(4):
                        for b in range(nblk):
                            p = proj(wv, m, b)
                            for (dst, (s0, sn)) in xbc_dst(part * 4 + m, b):
                                cp(dst, p[:, s0:s0 + sn], eng=("act" if (m + b) % 2 == 0 else "dve"))
                wv = load_w(w_in[l][:, 1552:2064], 512, key=("win", l, 3))
                for m in range(4):
                    for b in range(nblk):
                        dstt = (GU if m < 2 else GV)[:, m % 2, b * 512:(b + 1) * 512]
                        act(dstt, proj(wv, m, b), AF.Gelu)
                wv = load_w(w_in[l][:, 2064:2320], 256, key=("win", l, 4))
                for m in range(2):
                    for b in range(nblk):
                        cp(U5[:, m, b * 512:(b + 1) * 512], proj(wv, m, b), eng=("act" if (m + b) % 2 == 0 else "dve"))
                K.dma("pool", WDT.re("p (t n) -> p t n", t=8), w_in[l][:, 1536:1552].rearrange("(t p) n -> p t n", p=128))
                WDTv = WDT.re("p (t n) -> p t n", t=8)
                nchg = GL // 128
                pdt = nps()
                for ci in range(nchg):
                    for k in range(8):
                        mm(pdt[:, ci * 16:(ci + 1) * 16], HT[:, k, g0 + ci * 128:g0 + (ci + 1) * 128], WDTv[:, k, :], start=(k == 0), stop=(k == 7))
                DTv = DT[:, 0:nchg, :]
                tt(DTv, pdt[:, 0:nchg * 16].re("p (c h) -> p c h", h=16), DTB.unsq(1).bc([128, nchg, 16]), ALU.add)
                act(DTv, DTv, AF.Exp)
                act(DTv, DTv, AF.Ln, bias=1.0)
                ADTv = ADT[:, 0:nchg, :]
                tt(ADTv, DTv, NEGA.unsq(1).bc([128, nchg, 16]), ALU.mult)
                for d in range(2):
                    rhs_ = ADTv[:, :, d * 8:(d + 1) * 8]
                    pa = nps()
                    mm(pa[:, 0:nchg * 8].re("p (c h) -> p c h", h=8), (TRI_LE if d == 0 else TRI_GE), rhs_)
                    cp(ACUM[:, d, 0:nchg, :], pa[:, 0:nchg * 8].re("p (c h) -> p c h", h=8))
                    pb = nps()
                    mm(pb[:, 0:nchg * 8].re("p (c h) -> p c h", h=8), ONF, rhs_)
                    cp(TOT[:, d, 0:nchg, :], pb[:, 0:nchg * 8].re("p (c h) -> p c h", h=8))
                tt(DTE[:, :, 0:nchg, :], TOT[:, :, 0:nchg, :], ACUM[:, :, 0:nchg, :], ALU.subtract)
                act(DTE[:, :, 0:nchg, :], DTE[:, :, 0:nchg, :], AF.Exp)
                act(EA[:, :, 0:nchg, :], ACUM[:, :, 0:nchg, :], AF.Exp)
                act(CDC[:, :, 0:nchg, :], TOT[:, :, 0:nchg, :], AF.Exp)
                for d in range(2):
                    tt(DTE[:, d, 0:nchg, :], DTE[:, d, 0:nchg, :], DTv[:, :, d * 8:(d + 1) * 8], ALU.mult)

                mark('%s:L%d:win_done' % (pname, l))
                BCTv = BCT[:, 0:4 * GL].re("p (t n) -> p t n", t=4)
                XSFv = YP[:, 0:4 * GL].re("p (t n) -> p t n", t=4)
                for t in range(8):
                    for si in range(nseq):
                        for b in range(max(1, SL // 512)):
                            n = min(512, SL)
                            p = nps()
                            for tap in range(5):
                                mm(p[:, 0:n], DGv[:, t, tap, :], XBCv[:, t, si, b * 512 + tap:b * 512 + tap + n], start=(tap == 0), stop=(tap == 4))
                            dstv = XSFv[:, t] if t < 4 else BCTv[:, t - 4]
                            act(dstv[:, si * SL + b * 512:si * SL + b * 512 + n], p[:, 0:n], AF.Silu, bias=PT["convb%d" % l][:, t:t + 1])

                XST = WB[3].re("p (c n) -> p c n", n=512)
                BMTv = BMT.re("p (c n) -> p c n", n=256)
                YPv = YP[:, 0:4 * ML].re("p (c n) -> p c n", n=512)
                for cg in range(GL // 128):
                    tokc = slice(cg * 128, (cg + 1) * 128)
                    px = nps()
                    for q in range(4):
                        mm(px[:, q * 128:(q + 1) * 128], XSFv[:, q, tokc], IDB)
                    cp(XST[:, cg, :], px, eng="act")
                    pbm = nps()
                    for g in range(2):
                        mm(pbm[:, g * 128:(g + 1) * 128], BCTv[:, g, tokc], IDB)
                    cp(BMTv[:, cg, :], pbm[:, 0:256])
                for si, (s0, L) in enumerate(seqs):
                    nch = L // 128
                    cb0 = s0 // 128
                    for d in range(2):
                        if v == 1:
                            for q in range(4):
                                K.dma("sp", STG4[:, q * 128:(q + 1) * 128], st_ssd[l, d, q * 128:(q + 1) * 128, :])
                            p = nps()
                            for q in range(4):
                                mm(p[:, q * 128:(q + 1) * 128], STG4[:, q * 128:(q + 1) * 128], IDF)
                            cp(HST[d], p)
                        else:
                            memset(HST[d], 0.0)
                    for step in range(nch):
                        bg()
                        first = step < nch // 2
                        for d in range(2):
                            c = step if d == 0 else nch - 1 - step
                            cg = cb0 + c
                            cp(HSTB[d], HST[d], eng="act")
                            xw = XDT[d]
                            tt(xw.re("p (h x) -> p h x", h=8), XST[:, cg, :].re("p (h x) -> p h x", h=8), DTE[:, d, cg, :].unsq(2).bc([128, 8, 64]), ALU.mult)
                            pst = nps()
                            for g in range(2):
                                mm(pst[:, g * 256:(g + 1) * 256], BMTv[:, cg, g * 128:(g + 1) * 128], xw[:, g * 256:(g + 1) * 256])
                            po = nps()
                            for g in range(2):
                                mm(po[:, g * 256:(g + 1) * 256], BCTv[:, 2 + g, s0 + c * 128:s0 + (c + 1) * 128], HSTB[d][:, g * 256:(g + 1) * 256])
                            tt(HST[d].re("p (h x) -> p h x", h=8), HST[d].re("p (h x) -> p h x", h=8), CDC[:, d, cg, :].unsq(2).bc([128, 8, 64]), ALU.mult)
                            tt(HST[d], HST[d], pst, ALU.add)
                            eab = EA[:, d, cg, :].unsq(2).bc([128, 8, 64])
                            pov = po.re("p (h x) -> p h x", h=8)
                            if first:
                                tt(YPv[:, cg, :].re("p (h x) -> p h x", h=8), pov, eab, ALU.mult)
                            else:
                                tmpy = FT[d][:, 0:512]
                                tt(tmpy.re("p (h x) -> p h x", h=8), pov, eab, ALU.mult)
                                tt(YPv[:, cg, :], YPv[:, cg, :], tmpy, ALU.add)
                    if v == 0:
                        for d in range(2):
                            p = nps()
                            for q in range(4):
                                mm(p[:, q * 128:(q + 1) * 128], HST[d][:, q * 128:(q + 1) * 128], IDF)
                            so = SSO[d]
                            cp(so, p.re("p (q n) -> p q n", q=4))
                            K.dma("sp", ns_ssd[si, l, d].rearrange("(q p) n -> p q n", p=128), so)
                            if so.tile not in EXTRA_OUT_TILES:
                                EXTRA_OUT_TILES.append(so.tile)
                    rsb = [FT[0], FT[1]]

                    def emit_rs(cg_):
                        for d_ in range(2):
                            msk = TRI_LE if d_ == 0 else TRI_GE
                            tt(rsb[d_].re("p (h q) -> p h q", h=8), msk.unsq(1).bc([128, 8, 128]), ADT[:, cg_, d_ * 8:(d_ + 1) * 8].unsq(2).bc([128, 8, 128]), ALU.mult)

                    emit_rs(cb0)
                    for c in range(nch):
                        bg()
                        cg = cb0 + c
                        tok = slice(s0 + c * 128, s0 + (c + 1) * 128)
                        for d in range(2):
                            tt(XDT[d].re("p (h x) -> p h x", h=8), XST[:, cg, :].re("p (h x) -> p h x", h=8), DT[:, cg, d * 8:(d + 1) * 8].unsq(2).bc([128, 8, 64]), ALU.mult)
                        pcb = nps()
                        for g in range(2):
                            mm(pcb[:, g * 128:(g + 1) * 128], BCTv[:, g, tok], BCTv[:, 2 + g, tok])
                        for d in range(2):
                            strict = TRI_GT if d == 0 else TRI_LT
                            for hh in range(2):
                                psg = nps()
                                mm(psg, strict, rsb[d][:, hh * 512:(hh + 1) * 512])
                                act(MEXPS[d][:, hh * 512:(hh + 1) * 512], psg, AF.Exp)
                        tt(CBM[0].re("p (g q) -> p g q", g=2), pcb[:, 0:256].re("p (g q) -> p g q", g=2), TRI_LE.unsq(1).bc([128, 2, 128]), ALU.mult)
                        tt(CBM[1].re("p (g q) -> p g q", g=2), pcb[:, 0:256].re("p (g q) -> p g q", g=2), TRI_GE.unsq(1).bc([128, 2, 128]), ALU.mult)
                        for d in range(2):
                            tt(MMTS[d].re("p (g r q) -> p g r q", g=2, r=4), MEXPS[d].re("p (g r q) -> p g r q", g=2, r=4),
                               CBM[d].re("p (g q) -> p g q", g=2).unsq(2).bc([128, 2, 4, 128]), ALU.mult)
                        if c + 1 < nch:
                            emit_rs(cg + 1)
                        py = nps()
                        for h in range(8):
                            for d in range(2):
                                mm(py[:, h * 64:(h + 1) * 64], MMTS[d][:, h * 128:(h + 1) * 128], XDT[d][:, h * 64:(h + 1) * 64], start=(d == 0), stop=False)
                            mm(py[:, h * 64:(h + 1) * 64], DDv[:, h, :], XST[:, cg, h * 64:(h + 1) * 64], start=False, stop=True)
                        tt(YPv[:, cg, :], py, YPv[:, cg, :], ALU.add)
                    for c2 in range(0, nch, 2):
                        t0_ = s0 + c2 * 128
                        Gv = MEXPS[0].re("p (q n) -> p q n", q=4)
                        SQg = MEXPS[1].re("p (q n) -> p q n", q=4)
                        for cc in range(2):
                            pt_ = nps()
                            ptv = pt_.re("p (q n) -> p q n", q=4)
                            for q in range(4):
                                mm(ptv[:, q, :], YPv[:, cb0 + c2 + cc, q * 128:(q + 1) * 128], IDB)
                            tt(Gv[:, :, cc * 128:(cc + 1) * 128], ptv, ZS[:, :, t0_ + cc * 128:t0_ + (cc + 1) * 128], ALU.mult)
                        act(SQg, Gv, AF.Square)
                        rstd_from_sq(SQg, 4, 256, 512)
                        for q in range(4):
                            stt(HT[:, q, g0 + t0_:g0 + t0_ + 256], Gv[:, q, :], SSNG[:, q:q + 1], RS[:, 0:256], ALU.mult, ALU.mult)
                mark('%s:L%d:ssd_done' % (pname, l))
                for b in range(nblk):
                    cs = slice(b * 512, (b + 1) * 512)
                    for t in range(2):
                        act(SQ[:, t, :], GV[:, t, cs], AF.Square)
                    rstd_from_sq(SQ[:, 0:2, :], 2, 512, 256)
                    for t in range(2):
                        stt(GV[:, t, cs], GV[:, t, cs], SGNG[:, t:t + 1], RS, ALU.mult, ALU.mult)
                for ci in range(nchg):
                    tok = slice(ci * 128, (ci + 1) * 128)
                    pv = nps()
                    for t in range(2):
                        mm(pv[:, t * 128:(t + 1) * 128], GV[:, t, tok], IDB)
                    vt = XDT[0]
                    cp(vt[:, 0:256], pv[:, 0:256], eng="act")
                    pm_ = nps()
                    for h in range(4):
                        o = pm_[(h % 2) * 64:(h % 2 + 1) * 64, (h // 2) * 128:(h // 2 + 1) * 128]
                        mm(o, vt[:, h * 64:(h + 1) * 64], WSTv[:, h, :], start=True, stop=False)
                        mm(o, ONB[0:1, 0:64], SGB[0:1, h * 128:(h + 1) * 128], start=False, stop=True)
                    for t in range(2):
                        tt(HT[:, 4 + t, g0 + ci * 128:g0 + (ci + 1) * 128], pm_[:, t * 128:(t + 1) * 128], GU[:, t, tok], ALU.mult)

                mark('%s:L%d:sgu_done' % (pname, l))
                s5_load(l)
                RHO4, TH4, H0L = RHO4L[l], TH4L[l], H0LL[l]
                NJs = SL // 4
                NJ = GL // 4
                ppb = 512 // NJ
                ngrp = 8 // ppb
                UB = WB[2][:, 0:8 * NJ].re("p (q n) -> p q n", q=8)
                YBS = UP[:, 4 * ML:4 * ML + 8 * NJ].re("p (q n) -> p q n", q=8)
                HX = [[YP[:, 0:8 * NJ].re("p (q n) -> p q n", q=8), YP[:, 2048:2048 + 8 * NJ].re("p (q n) -> p q n", q=8)],
                      [UP[:, 0:8 * NJ].re("p (q n) -> p q n", q=8), UP[:, 2048:2048 + 8 * NJ].re("p (q n) -> p q n", q=8)]]
                if v == 1:
                    uview = lambda t, j0: U5[:, t, :].re("p (jj k c) -> p k c jj", jj=4, k=4)[:, j0]
                    yview = lambda t, k0: YG[:, t, 0:1024].re("p (jj k c) -> p k c jj", jj=4, k=4)[:, k0]
                    pv3 = lambda p_: p_[:, 0:256].re("p (c jj) -> p c jj", jj=4)
                else:
                    uview = lambda t, j0: U5[:, t, :].re("p (j k) -> p k j", k=4)[:, j0]
                    yview = lambda t, k0: YG[:, t, 0:512].re("p (j k) -> p k j", k=4)[:, k0]
                    pv3 = lambda p_: p_[:, 0:128]
                for pair in range(8):
                    t, p4 = pair // 4, pair % 4
                    p = nps()
                    for j0 in range(4):
                        mm(pv3(p), EMv[:, p4, j0, :], uview(t, j0), start=(j0 == 0), stop=(j0 == 3))
                    cp(UB[:, pair, :], p[:, 0:NJ], eng="act")
                T5 = lambda i: FT[i // 2][:, (i % 2) * 512:(i % 2 + 1) * 512]
                v4 = lambda T_: T_.re("p (q s n) -> p q s n", q=ppb, s=nseq)
                ntab = ppb * NJs
                tabv = lambda T_: T_[:, 0:ntab].re("p (q n) -> p q n", q=ppb)
                tbc = lambda T_: tabv(T_).unsq(2).bc([128, ppb, nseq, NJs])
                CH = [FT[3][:, 0:512], FT[3][:, 512:1024], FT[4][:, 0:512], FT[4][:, 512:1024], FT[5][:, 0:512]]
                TAB = [(FT[0][:, 0:512], FT[0][:, 512:1024], HST[0]), (FT[1][:, 0:512], FT[1][:, 512:1024], HST[1])]
                grp_l2 = [(d_, g_) for d_ in range(2) for g_ in range(ngrp)]

                def emit_tables(i_):
                    d_, g_ = grp_l2[i_]
                    fr_, c_, s_n = TAB[i_ % 2]
                    TI = TI32[:, 0:ntab]
                    tt(tabv(fr_), POS2[:, 0:NJs].unsq(1).bc([128, ppb, NJs]), TH4[d_][:, g_ * ppb:(g_ + 1) * ppb].unsq(2).bc([128, ppb, NJs]), ALU.mult)
                    ts(TI, fr_[:, 0:ntab], 1.0, ALU.mult)
                    tt(fr_[:, 0:ntab], fr_[:, 0:ntab], TI, ALU.subtract)
                    act(s_n[:, 0:ntab], fr_[:, 0:ntab], AF.Sin, scale=TWO_PI)
                    act(fr_[:, 0:ntab], fr_[:, 0:ntab], AF.Abs)
                    act(c_[:, 0:ntab], fr_[:, 0:ntab], AF.Sin, bias=PI / 2, scale=-TWO_PI)

                emit_tables(0)
                for gi_l2, (d, gq) in enumerate(grp_l2):
                    if True:
                        bg()
                        prs = slice(gq * ppb, (gq + 1) * ppb)
                        psS = [nps(), nps()]
                        for ri in range(2):
                            for q in range(ppb):
                                mm(psS[ri][:, q * NJ:(q + 1) * NJ], BQv[:, d, ri, gq * ppb + q, :], UB[:, gq * ppb + q, :])
                        if gi_l2 + 1 < len(grp_l2):
                            emit_tables(gi_l2 + 1)
                        fr, cs_, sn_ = TAB[gi_l2 % 2]
                        wr, wi, t1, gr, gi_ = CH
                        Sr = v4(psS[0]) if d == 0 else v4(psS[0])[:, :, :, ::-1]
                        Si = v4(psS[1]) if d == 0 else v4(psS[1])[:, :, :, ::-1]
                        tt(v4(wr), Sr, tbc(cs_), ALU.mult)
                        tt(v4(t1), Si, tbc(sn_), ALU.mult)
                        tt(wr, wr, t1, ALU.add)
                        tt(v4(wi), Si, tbc(cs_), ALU.mult)
                        tt(v4(t1), Sr, tbc(sn_), ALU.mult)
                        tt(wi, wi, t1, ALU.subtract)
                        if v == 1:
                            tt(v4(wr)[:, :, 0, 0:1], v4(wr)[:, :, 0, 0:1], H0L[d][0][:, prs].unsq(2), ALU.add)
                            tt(v4(wi)[:, :, 0, 0:1], v4(wi)[:, :, 0, 0:1], H0L[d][1][:, prs].unsq(2), ALU.add)
                        for q in range(ppb):
                            pair = gq * ppb + q
                            rb = RHO4[d][:, pair:pair + 1].bc([128, NJs])
                            for s_ in range(nseq):
                                sl = slice(q * NJ + s_ * NJs, q * NJ + (s_ + 1) * NJs)
                                scan(gr[:, sl], rb, wr[:, sl], 0.0)
                                scan(gi_[:, sl], rb, wi[:, sl], 0.0)
                        hr, hi = wr, wi
                        tt(v4(hr), v4(gr), tbc(cs_), ALU.mult)
                        tt(v4(t1), v4(gi_), tbc(sn_), ALU.mult)
                        tt(hr, hr, t1, ALU.subtract)
                        tt(v4(hi), v4(gr), tbc(sn_), ALU.mult)
                        tt(v4(gi_), v4(gi_), tbc(cs_), ALU.mult)
                        tt(hi, hi, gi_, ALU.add)
                        for ri, hh in enumerate((hr, hi)):
                            hv = v4(hh)
                            if v == 0:
                                for s_ in range(nseq):
                                    base = (((s_ * 2 + l) * 2 + d) * 2 + ri) * 8
                                    cp(NS5[:, base + gq * ppb:base + (gq + 1) * ppb], hv[:, :, s_, NJs - 1], eng="act")
                            hx = HX[d][ri][:, prs, :].re("p q (s n) -> p q s n", s=nseq)
                            if v == 1:
                                h0 = PT[("sre%d%d" if ri == 0 else "sim%d%d") % (l, d)][:, prs].unsq(2).unsq(3)
                            if d == 0:
                                cp(hx[:, :, :, 1:NJs], hv[:, :, :, 0:NJs - 1], eng="act")
                                if v == 1:
                                    cp(hx[:, :, :, 0:1], h0, eng="act")
                                else:
                                    memset(hx[:, :, :, 0:1], 0.0, eng="pool")
                            else:
                                cp(hx[:, :, :, 0:NJs - 1], hv[:, :, :, 0:NJs - 1][:, :, :, ::-1])
                                if v == 1:
                                    cp(hx[:, :, :, NJs - 1:NJs], h0, eng="act")
                                else:
                                    memset(hx[:, :, :, NJs - 1:NJs], 0.0, eng="pool")
                for t in range(2):
                    for p4 in range(4):
                        pair = 4 * t + p4
                        p = nps()
                        mm(p[:, 0:NJ], TSv[:, pair, :], UB[:, pair, :], start=True, stop=False)
                        for d in range(2):
                            for ri in range(2):
                                mm(p[:, 0:NJ], CQ[d][:, ri, pair, :], HX[d][ri][:, pair, :], start=False, stop=(d == 1 and ri == 1))
                        cp(YBS[:, pair, :], p[:, 0:NJ], eng="act")
                    for k0 in range(4):
                        p = nps()
                        for p4 in range(4):
                            mm(p[:, 0:NJ], EMv[:, k0, p4, :], YBS[:, 4 * t + p4, :], start=(p4 == 0), stop=(p4 == 3))
                        act(yview(t, k0), pv3(p), AF.Gelu)
                mark('%s:L%d:s5scan_done' % (pname, l))
                GWv = GW.re("p (t n) -> p t n", t=2)
                for m in range(2):
                    for b in range(nblk):
                        p = nps()
                        for k in range(2):
                            mm(p, GWv[:, k, m * 128:(m + 1) * 128], YG[:, k, b * 512:(b + 1) * 512], start=(k == 0), stop=(k == 1))
                        sig_ = NTMP if (m + b) % 2 == 0 else NTMP2
                        act(sig_, p, AF.Sigmoid, bias=PT["glub%d" % l][:, m:m + 1])
                        tt(HT[:, 6 + m, g0 + b * 512:g0 + (b + 1) * 512], sig_, YG[:, m, b * 512:(b + 1) * 512], ALU.mult)
                if l == 0:
                    dump("mixed" + pname, HT[:, :, 0:512], [128, 8, 512])
                    if GL > 512:
                        dump("mixed2" + pname, HT[:, :, 512:1024], [128, 8, 512])

        RL = XDT[0]
        rlc = [0]
        YS = [FT[0], FT[1]]
        if ML == 512:
            SQF0 = sb("SQF0", [128, 512], F32)
            SQF1 = sb("SQF1", [128, 512], F32)
        s5gen = s5_pre(0, True)
        s5gen1 = s5_pre(1, False) if ML == 1024 else s5_pre(1, True)
        SQD = float(np.sqrt(D))
        NA = {}
        MODW = [[sb("MODW%d%d" % (l_, w_), [128, 8, 2], F32) for w_ in range(6)] for l_ in range(2)]
        NAT = {(l_, v_, w_): sb("NA%d%d%d" % (l_, v_, w_), [128, 8], F32) for l_ in range(2) for v_ in range(2) for w_ in range(2)}
        FNA = sb("FNA", [128, 8], F32)
        ts(FNA, PT["fng"], SQD, ALU.mult)

        def ada_mm(l, part, w):
            pm = PSB[6 + l]
            for j in range(4):
                ch = part * 4 + j
                for k in range(8):
                    mm(pm[:, 2 * ch:2 * ch + 2], w[:, k, j * 128:(j + 1) * 128], SC[:, :, k], start=(k == 0), stop=(k == 7))
            if part % 2 == 1:
                wh = part // 2
                tt(MODW[l][wh], pm[:, wh * 16:(wh + 1) * 16].re("p (c v) -> p c v", v=2),
                   PT["adab%d" % l][:, wh * 8:(wh + 1) * 8].unsq(2).bc([128, 8, 2]), ALU.add)
                for which, (gname, sc_i, sh_i) in enumerate((("n1g%d" % l, 1, 0), ("n2g%d" % l, 4, 3))):
                    if wh == sc_i:
                        for v in range(2):
                            A = NAT[(l, v, which)]
                            ts(A, MODW[l][sc_i][:, :, v], 1.0, ALU.add, SQD, ALU.mult)
                            tt(A, A, PT[gname], ALU.mult)
                            NA[(l, v, which)] = (A, MODW[l][sh_i][:, :, v])

        ada_order = [(l_, part) for l_ in range(2) for part in range(12)]
        ada_state = {"next": 0, "pending": None}
        ada_done = set()

        def bg(prefetch=True):
            st_ = ada_state
            newp = None
            if prefetch and st_["next"] < len(ada_order):
                l_, part = ada_order[st_["next"]]
                st_["next"] += 1
                w = load_w(ada_w[l_][:, part * 512:(part + 1) * 512], 512)
                newp = (l_, part, w)
            if st_["pending"] is not None:
                ada_mm(*st_["pending"])
                ada_done.add(st_["pending"][:2])
            st_["pending"] = newp
            if newp is None and st_["next"] >= len(ada_order):
                ada_all_done[0] = True

        def ada_need(l, part):
            while (l, part) not in ada_done:
                bg()
            bg(prefetch=False)

        pname, row0, NT, BLKS, GROUPS = PASSES[0]
        load_x()
        while (0, 3) not in ada_done:
            bg()
            for _ in range(3):
                next(s5gen, None)
        bg(prefetch=False)
        for _ in s5gen:
            pass
        if ML != 1024:
            for _ in s5gen1:
                pass
        mark('setup_done')
        for pass_i, (pname, row0, NT, BLKS, GROUPS) in enumerate(PASSES):
            pass_idx[0] = pass_i
            if pass_i > 0:
                K.wait_all("sp", [WB[0].tile, WB[1].tile])
            mark(pname + ':start')
            if pass_i > 0:
                load_x()
            for l in range(2):
                ada_need(l, 3)
                for _ in norm_to_HT(l, 0):
                    pass
                if l == 0:
                    dump("h1" + pname, HT[:, :, 0:512], [128, 8, 512])

                mark('%s:L%d:norm1_done' % (pname, l))
                mixers(l)
                mark('%s:L%d:mixers_done' % (pname, l))

                ada_need(l, 9)
                G1 = lambda v: MODW[l][2][:, :, v]
                wparts = [load_w(w_out[l][:, part * 512:(part + 1) * 512], 512, key=("wout", l, part)) for part in range(2)]
                for bi, (c0, n, v) in enumerate(BLKS):
                    for mt in range(8):
                        w = wparts[mt // 4]
                        m = mt % 4
                        p = nps()
                        for k in range(8):
                            mm(p, w[:, k, m * 128:(m + 1) * 128], HT[:, k, c0:c0 + n], start=(k == 0), stop=(k == 7))
                        stt(XT[:, mt, c0:c0 + n], p, G1(v)[:, mt:mt + 1], XT[:, mt, c0:c0 + n], ALU.mult, ALU.add)
                    for _ in norm_to_HT(l, 1, blks=[BLKS[bi]]):
                        pass
                if l == 0:
                    dump("xmid" + pname, XT[:, :, 0:512], [128, 8, 512])
                mark('%s:L%d:wout_done' % (pname, l))
                mark('%s:L%d:norm2_done' % (pname, l))
                ada_need(l, 11)
                G2 = lambda v: MODW[l][5][:, :, v]
                UPv = UP[:, 0:4 * NT].re("p (j n) -> p j n", j=4)
                for part in range(8):
                    w1 = load_w(ffn_w1[l][:, part * 512:(part + 1) * 512], 512, key=("w1", l, part))
                    for j in range(4):
                        for (c0, n, v) in BLKS:
                            p = nps()
                            for k in range(8):
                                mm(p, w1[:, k, j * 128:(j + 1) * 128], HT[:, k, c0:c0 + n], start=(k == 0), stop=(k == 7))
                            rl_ = XDT[rlc[0] % 2]
                            rlc[0] += 1
                            act(rl_, p, AF.Relu)
                            tt(UPv[:, j, c0:c0 + n], rl_, rl_, ALU.mult)
                    if l == 0:
                        for _ in range(2):
                            next(s5gen1, None)
                    w2 = load_w(ffn_w2[l][part * 512:(part + 1) * 512, :], 1024, key=("w2", l, part))
                    for mt in range(8):
                        for (c0, n, v) in BLKS:
                            p = nps()
                            for j in range(4):
                                mm(p, w2[:, j, mt * 128:(mt + 1) * 128], UPv[:, j, c0:c0 + n], start=(j == 0), stop=(j == 3))
                            stt(XT[:, mt, c0:c0 + n], p, G2(v)[:, mt:mt + 1], XT[:, mt, c0:c0 + n], ALU.mult, ALU.add)
                if l == 0:
                    dump("xout" + pname, XT[:, :, 0:512], [128, 8, 512])

                if l == 0:
                    for _ in s5gen1:
                        pass
            mark(pname + ':ffn_done')
            FNTv = lambda k: (FT[2 + k] if ML == 512 else FT[4 + k // 2][:, (k % 2) * 512:(k % 2 + 1) * 512]) if k < 6 or ML != 512 else [SQF0, SQF1][k - 6]
            yi = 0
            cur = None
            for (c0, k, tmp) in norm_to_HT(0, 0, final=True):
                cp(FNTv(k), tmp, eng="act")
                if k == 7:
                    for tb in range(4):
                        ys = YS[yi % 2]
                        yi += 1
                        for half in range(2):
                            p = nps()
                            for q in range(4):
                                kk = half * 4 + q
                                tr(p[:, q * 128:(q + 1) * 128], FNTv(kk)[:, tb * 128:(tb + 1) * 128], IDF)
                            cp(ys[:, half * 512:(half + 1) * 512], p, eng=("act" if half else "dve"))
                        K.dma("sp", yout[row0 + c0 + tb * 128:row0 + c0 + (tb + 1) * 128, :], ys)
        mark('final_done')
        p = nps()
        mm(p[:, 0:128], NS5, IDF)
        cp(NS5T, p[:, 0:128])
        K.dma("sp", ns_s5, NS5T)
        EXTRA_OUT_TILES.append(NS5T.tile)
        K.wait_all("sp", [YS[0].tile, YS[1].tile] + dump_tiles + EXTRA_OUT_TILES)
    import os, json
    if os.environ.get('KPHASES'):
        json.dump(PHASES, open(os.environ['KPHASES'], 'w'))
    return dram_in, dram_out, dbg_out


EXTRA_OUT_TILES = []
PHASES = []


def build_layer_mixers(env, l):
    pass


def make_consts():
    j = np.arange(128)[:, None]
    k = np.arange(128)[None, :]
    c = np.zeros((128, 8 * 128 + 10), np.float32)
    c[:, 0:128] = np.eye(128)
    c[:, 128:256] = 1.0
    c[:, 256:384] = (j <= k)
    c[:, 384:512] = (j >= k)
    c[:, 512:640] = (j > k)
    c[:, 640:768] = (j < k)
    c[:, 768] = (np.arange(128) < 64)
    c[:, 769] = (np.arange(128) >= 64)
    jb = (np.arange(128) // 32)[:, None]
    kb = (np.arange(128) // 32)[None, :]
    c[:, 770:898] = (jb <= kb)
    c[:, 898:1026] = (jb >= kb)
    c[:, 1026:1034] = np.arange(-3, 5, dtype=np.float32)[None, :]
    return c


def make_emat():
    e = np.zeros((128, 4, 4, 128), np.float32)
    for a in range(4):
        for b in range(4):
            for r in range(32):
                e[32 * a + r, a, b, 32 * b + r] = 1.0
    return e.reshape(128, 2048)


_CACHE = {}


def kernel(**inp):
    f = lambda a: np.ascontiguousarray(np.asarray(a, dtype=np.float32))
    dbg_names = tuple(inp.pop("_dbg", ()))
    ncores = 8
    key = dbg_names
    mode = inp.pop('_mode', 'all')
    nc = bass.Bass("TRN2", target_bir_lowering=False)
    dram_in, dram_out, dbg_out = build_program(nc, dbg_names, mode)
    shared = {}
    for name in ["ada_w", "ada_b", "norm1_g", "norm2_g", "w_in", "ssd_conv_w", "ssd_conv_b", "ssd_norm_g", "sgu_norm_g",
                 "sgu_w", "sgu_b", "s5_d", "s5_glu_w", "s5_glu_b", "w_out", "ffn_w1", "ffn_w2", "final_norm_g", "ssd_d"]:
        shared[name] = f(inp[name])
    shared["ssd_dt_bias"] = f(inp["ssd_dt_bias"]).reshape(2, 16)
    shared["ssd_a_log"] = f(inp["ssd_a_log"]).reshape(2, 16)
    shared["s5_lambda_re"] = f(inp["s5_lambda_re"]).reshape(2, 2, 1024)
    shared["s5_lambda_im"] = f(inp["s5_lambda_im"]).reshape(2, 2, 1024)
    shared["s5_log_dt"] = f(inp["s5_log_dt"])
    shared["s5_b_re"] = f(inp["s5_b_re"]).reshape(2, 1024, 16)
    shared["s5_b_im"] = f(inp["s5_b_im"]).reshape(2, 1024, 16)
    shared["s5_c_re"] = f(inp["s5_c_re"])
    shared["s5_c_im"] = f(inp["s5_c_im"])
    shared["consts"] = make_consts()
    shared["emat"] = make_emat()
    shared["posrow"] = np.arange(1024, dtype=np.float32)[None, :]
    xp = f(inp["x_prompt"])
    xs = f(inp["x_sample"])
    sssd = f(inp["state_ssd"])
    sre = f(inp["state_s5_re"])
    sim = f(inp["state_s5_im"])
    c = f(inp["c"])
    cctx = f(inp["c_ctx"])
    in_maps = []
    for core in range(ncores):
        b = core % 2
        m = dict(shared)
        m["xin"] = np.ascontiguousarray(np.concatenate([xp[2 * core], xp[2 * core + 1], xs[b]], axis=0) if mode == 'all' else (np.concatenate([xp[2 * core], xp[2 * core + 1]], axis=0) if mode == 'P' else xs[b]))
        m["st_ssd"] = np.ascontiguousarray(sssd[b].reshape(2, 2, 512, 128))
        m["st_re"] = np.ascontiguousarray(sre[b].reshape(2, 2, 1024))
        m["st_im"] = np.ascontiguousarray(sim[b].reshape(2, 2, 1024))
        m["cvec"] = np.ascontiguousarray(np.stack([cctx, c[b]], axis=0))
        in_maps.append({k: m[k] for k in dram_in})
    res = run_bass_kernel_spmd(nc, in_maps, core_ids=list(range(ncores)))
    R = res.results
    if dbg_names:
        kernel.dbg = {n: np.asarray(R[0]["dbg_" + n]).astype(np.float32) for n in dbg_out}
    if mode != 'all':
        kernel.raw = R
        return None
    y_prompt = np.zeros((16, 256, D), np.float32)
    y_sample = np.zeros((2, 1024, D), np.float32)
    ns_ssd = np.zeros((16, 2, 2, 8, 64, 128), np.float32)
    ns_re = np.zeros((16, 2, 2, 16, 64), np.float32)
    ns_im = np.zeros((16, 2, 2, 16, 64), np.float32)
    for core in range(ncores):
        y = R[core]["yout"]
        y_prompt[2 * core] = y[0:256]
        y_prompt[2 * core + 1] = y[256:512]
        if core < 2:
            y_sample[core] = y[512:1536]
        ns_ssd[2 * core:2 * core + 2] = R[core]["ns_ssd"].reshape(2, 2, 2, 8, 64, 128)
        s5 = R[core]["ns_s5"].reshape(2, 2, 2, 2, 8, 128)
        ns_re[2 * core:2 * core + 2] = s5[:, :, :, 0].reshape(2, 2, 2, 16, 64)
        ns_im[2 * core:2 * core + 2] = s5[:, :, :, 1].reshape(2, 2, 2, 16, 64)
    if dbg_names:
        kernel.dbg = {n: np.asarray(R[0]["dbg_" + n]).astype(np.float32) for n in dbg_out}
    return (y_prompt, y_sample, ns_ssd, ns_re, ns_im)
```

```python
import numpy as np
from contextlib import ExitStack
import concourse.bass as bass
import concourse.mybir as mybir
from concourse.bass_utils import run_bass_kernel_spmd

F32 = mybir.dt.float32
BF16 = mybir.dt.bfloat16
I32 = mybir.dt.int32
AF = mybir.ActivationFunctionType
ALU = mybir.AluOpType
AX = mybir.AxisListType

D = 1024
NT = 1536
EPS = 1e-6
TWO_PI = 6.283185307179586


class Sem:
    def __init__(self, h, name):
        self.h = h
        self.name = name
        self.total = 0


class Tile:
    def __init__(self, name, t):
        self.name = name
        self.t = t
        self.last_w = None
        self.reads = {}
        self.dsem = None


class V:
    def __init__(self, tile, ap):
        self.tile = tile
        self.ap = ap

    def __getitem__(self, key):
        return V(self.tile, self.ap[key])

    def re(self, pat, **kw):
        return V(self.tile, self.ap.rearrange(pat, **kw))

    def bc(self, shape):
        return V(self.tile, self.ap.to_broadcast(list(shape)))

    def unsq(self, ax):
        return V(self.tile, self.ap.unsqueeze(ax))

    @property
    def shape(self):
        return self.ap.shape


class Eng:
    def __init__(self, name, handle, sem):
        self.name = name
        self.h = handle
        self.sem = sem
        self.waited = {}
        self.snaps = {}


class Kern:
    def __init__(self, nc, stack):
        self.nc = nc
        self.stack = stack
        self.engs = {}
        self.nsem = 0
        self.pe_pending = []
        self.pe_pending_w = []

    def new_sem(self, name):
        h = self.stack.enter_context(self.nc.semaphore(name))
        self.nsem += 1
        return Sem(h, name)

    def setup(self):
        nc = self.nc
        for name, h in (("pe", nc.tensor), ("act", nc.scalar), ("dve", nc.vector), ("pool", nc.gpsimd), ("sp", nc.sync)):
            self.engs[name] = Eng(name, h, self.new_sem("e_" + name) if name != "sp" else None)

    def sb(self, name, shape, dtype):
        t = self.stack.enter_context(self.nc.sbuf_tensor(name, list(shape), dtype))
        tl = Tile(name, t)
        return V(tl, t[:])

    def ps(self, name, shape, dtype=F32):
        t = self.stack.enter_context(self.nc.psum_tensor(name, list(shape), dtype))
        tl = Tile(name, t)
        return V(tl, t[:])

    def _need(self, eng, reads, writes):
        need = {}

        def add(p):
            if p is None:
                return
            s, v = p
            if s.name not in need or need[s.name][1] < v:
                need[s.name] = (s, v)

        for t in reads:
            add(t.last_w)
        for t in writes:
            add(t.last_w)
            for p in t.reads.values():
                add(p)
        owner = {e.sem.name: e for e in self.engs.values() if e.sem is not None}
        for nm, (s, v) in sorted(need.items(), key=lambda kv: -kv[1][1]):
            if eng.sem is not None and s is eng.sem and eng.name == "pe":
                continue
            if eng.waited.get(nm, 0) >= v:
                continue
            eng.h.wait_ge(s.h, v)
            eng.waited[nm] = v
            ox = owner.get(nm)
            if ox is not None and ox is not eng and v in ox.snaps:
                for k2, v2 in ox.snaps[v].items():
                    if eng.waited.get(k2, 0) < v2:
                        eng.waited[k2] = v2

    def op(self, engname, fn, reads, writes, noinc=False):
        eng = self.engs[engname]
        reads = [r for r in reads if r is not None]
        if engname != "pe":
            for t in writes:
                assert t not in self.pe_pending, "write to a tile read by an unfinished matmul group: " + t.name
        self._need(eng, reads, writes)
        ins = fn(eng.h)
        if noinc:
            for t in reads:
                if t not in self.pe_pending:
                    self.pe_pending.append(t)
            for t in writes:
                if t not in self.pe_pending_w:
                    self.pe_pending_w.append(t)
            return ins
        eng.sem.total += 1
        ins.then_inc(eng.sem.h, 1)
        eng.snaps[eng.sem.total] = dict(eng.waited)
        p = (eng.sem, eng.sem.total)
        if engname == "pe":
            for t in self.pe_pending:
                if t not in writes:
                    t.reads[eng.sem.name] = p
            for t in self.pe_pending_w:
                t.last_w = p
                t.reads = {}
            self.pe_pending = []
            self.pe_pending_w = []
        for t in writes:
            t.last_w = p
            t.reads = {}
        for t in reads:
            if t not in writes:
                t.reads[eng.sem.name] = p
        return ins

    def dma(self, q, out, in_, **kw):
        eng = self.engs[q]
        reads = [in_.tile] if isinstance(in_, V) else []
        writes = [out.tile] if isinstance(out, V) else []
        self._need(eng, reads, writes)
        st = writes[0] if writes else reads[0]
        if st.dsem is None:
            st.dsem = {}
        qk = "sw" if q == "pool" else "hw"
        if qk not in st.dsem:
            st.dsem[qk] = self.new_sem("d%s_%s" % (qk, st.name))
        dsem = st.dsem[qk]
        oap = out.ap if isinstance(out, V) else out
        iap = in_.ap if isinstance(in_, V) else in_
        ins = eng.h.dma_start(out=oap, in_=iap, **kw)
        dsem.total += 16
        ins.then_inc(dsem.h, 16)
        p = (dsem, dsem.total)
        for t in writes:
            t.last_w = p
            t.reads = {}
        for t in reads:
            t.reads[dsem.name] = p

    def wait_all(self, q, tiles):
        self._need(self.engs[q], [], tiles)


K = None


def _tiles(*vs):
    return [v.tile for v in vs if isinstance(v, V)]


def _a(v):
    return v.ap if isinstance(v, V) else v


def mm(out, lhsT, rhs, start=True, stop=True):
    K.op("pe", lambda e: e.matmul(out.ap, lhsT.ap, rhs.ap, start=start, stop=stop), _tiles(lhsT, rhs), _tiles(out), noinc=(not stop))


def tr(out, in_, ident):
    K.op("pe", lambda e: e.transpose(out.ap, in_.ap, ident.ap), _tiles(in_, ident), _tiles(out))


def act(out, in_, func, bias=None, scale=None, eng="act"):
    kw = {}
    if bias is not None:
        kw["bias"] = _a(bias)
    if scale is not None:
        kw["scale"] = _a(scale)
    K.op("act", lambda e: e.activation(out.ap, in_.ap, func, **kw), _tiles(in_, bias, scale), _tiles(out))


def tt(out, a, b, op, eng="dve"):
    K.op(eng, lambda e: e.tensor_tensor(out.ap, a.ap, b.ap, op), _tiles(a, b), _tiles(out))


def ts(out, a, s1, op0, s2=None, op1=None, eng="dve"):
    if op1 is None:
        K.op(eng, lambda e: e.tensor_scalar(out.ap, a.ap, _a(s1), None, op0), _tiles(a, s1), _tiles(out))
    else:
        K.op(eng, lambda e: e.tensor_scalar(out.ap, a.ap, _a(s1), _a(s2), op0, op1), _tiles(a, s1, s2), _tiles(out))


def stt(out, a, s, b, op0, op1, eng="dve"):
    K.op(eng, lambda e: e.scalar_tensor_tensor(out.ap, a.ap, _a(s), b.ap, op0, op1), _tiles(a, s, b), _tiles(out))


def cp(out, a, eng="dve"):
    if eng == "act":
        K.op("act", lambda e: e.copy(out.ap, a.ap), _tiles(a), _tiles(out))
    else:
        K.op(eng, lambda e: e.tensor_copy(out.ap, a.ap), _tiles(a), _tiles(out))


def memset(out, val, eng="dve"):
    K.op(eng, lambda e: e.memset(out.ap, val), [], _tiles(out))


def scan(out, d0, d1, init, op0=ALU.mult, op1=ALU.add):
    K.op("dve", lambda e: e.tensor_tensor_scan(out.ap, d0.ap, d1.ap, _a(init), op0, op1), _tiles(d0, d1, init), _tiles(out))


def build_program(nc, dbg_names=(), mode='all'):
    global K
    PASS_P = ("P", 0, 512, [(0, 512, 0)], [(0, 512, [(0, 256), (256, 256)], 0)])
    if mode == 'all':
        NTT = 1536; NTM = 1024; ML = 1024
        PASSES = [("S", 512, 1024, [(0, 512, 1), (512, 512, 1)], [(0, 1024, [(0, 1024)], 1)]), PASS_P]
    elif mode == 'P':
        NTT = 512; NTM = 512; ML = 512
        PASSES = [PASS_P]
    else:
        NTT = 1024; NTM = 1024; ML = 1024
        PASSES = [("S", 0, 1024, [(0, 512, 1), (512, 512, 1)], [(0, 1024, [(0, 1024)], 1)])]
    NT = NTM
    BLKS = None
    GROUPS = None
    pname = None
    row0 = 0
    dram_in = {}
    dram_out = {}

    def din(name, shape, dt=F32):
        dram_in[name] = nc.dram_tensor(name, list(shape), dt, kind="ExternalInput").ap()
        return dram_in[name]

    def dout(name, shape):
        dram_out[name] = nc.dram_tensor(name, list(shape), F32, kind="ExternalOutput").ap()
        return dram_out[name]

    xin = din("xin", [NTT, D])
    st_ssd = din("st_ssd", [2, 2, 512, 128])
    st_re = din("st_re", [2, 2, 1024])
    st_im = din("st_im", [2, 2, 1024])
    cvec = din("cvec", [2, D])
    ada_w = din("ada_w", [2, D, 6 * D])
    ada_b = din("ada_b", [2, 6 * D])
    norm1_g = din("norm1_g", [2, D])
    norm2_g = din("norm2_g", [2, D])
    w_in = din("w_in", [2, D, 2320])
    conv_w = din("ssd_conv_w", [2, 5, D])
    conv_b = din("ssd_conv_b", [2, D])
    dt_bias = din("ssd_dt_bias", [2, 16])
    a_log = din("ssd_a_log", [2, 16])
    ssd_d = din("ssd_d", [2, 8])
    ssd_ng = din("ssd_norm_g", [2, 512])
    sgu_ng = din("sgu_norm_g", [2, 256])
    sgu_w = din("sgu_w", [2, 4, 128, 128])
    sgu_b = din("sgu_b", [2, 4, 128])
    lam_re = din("s5_lambda_re", [2, 2, 1024])
    lam_im = din("s5_lambda_im", [2, 2, 1024])
    log_dt = din("s5_log_dt", [2, 2, 16])
    b_re = din("s5_b_re", [2, 1024, 16])
    b_im = din("s5_b_im", [2, 1024, 16])
    c_re = din("s5_c_re", [2, 16, 16, 64])
    c_im = din("s5_c_im", [2, 16, 16, 64])
    s5_d = din("s5_d", [2, 256])
    glu_w = din("s5_glu_w", [2, 256, 256])
    glu_b = din("s5_glu_b", [2, 256])
    w_out = din("w_out", [2, D, D])
    ffn_w1 = din("ffn_w1", [2, D, 4 * D])
    ffn_w2 = din("ffn_w2", [2, 4 * D, D])
    fin_g = din("final_norm_g", [D])
    cst = din("consts", [128, 8 * 128 + 10])
    emat = din("emat", [128, 16 * 128])
    posrow = din("posrow", [1, 1024])

    yout = dout("yout", [NTT, D])
    ns_ssd = dout("ns_ssd", [2, 2, 2, 512, 128])
    ns_s5 = dout("ns_s5", [128, 128])
    dbg_out = {}

    del EXTRA_OUT_TILES[:]
    stack = ExitStack()
    with stack:
        K = Kern(nc, stack)
        K.setup()
        del PHASES[:]

        def mark(label):
            PHASES.append((label, {e: (g.sem.total if g.sem else 0) for e, g in K.engs.items()}))

        sb, ps = K.sb, K.ps

        XT = sb("XT", [128, 8, NTM], F32)
        HTB = [sb("HT%d" % i, [128, 8, 512], BF16) for i in range(NTM // 512)]

        class _HTW:
            def __getitem__(self, key):
                p_, k_, c_ = key
                a_ = c_.start or 0
                blk_ = a_ // 512
                assert (c_.stop - 1) // 512 == blk_
                return HTB[blk_][p_, k_, a_ - 512 * blk_:c_.stop - 512 * blk_]

        HT = _HTW()
        WB = [sb("WB%d" % i, [128, 4096], BF16) for i in range(2)] + [sb("WB%d" % i, [128, 4 * ML], BF16) for i in range(2, 4)]
        UP = sb("UP", [128, max(6 * ML, 4 * NTM)], BF16)
        CST = sb("CST", [128, 8 * 128 + 10], F32)
        IDB = sb("IDB", [128, 128], BF16)
        ONB = sb("ONB", [128, 128], BF16)
        PSB = [ps("PS%d" % i, [128, 512]) for i in range(8)]
        psi = [0]

        ada_all_done = [False]

        def nps():
            nb = 8 if ada_all_done[0] else 6
            p = PSB[psi[0] % nb]
            psi[0] += 1
            return p

        IDF = CST[:, 0:128]
        ONF = CST[:, 128:256]
        TRI_LE = CST[:, 256:384]
        TRI_GE = CST[:, 384:512]
        TRI_GT = CST[:, 512:640]
        TRI_LT = CST[:, 640:768]
        MG2 = CST[:, 768:770]
        MG2N = sb("MG2N", [128, 2], F32)
        MLO = CST[:, 770:898]
        MUP = CST[:, 898:1026]
        MROW = CST[:, 1026:1034]

        K.dma("sp", CST, cst)
        ts(MG2N, MG2, -1.0, ALU.mult)
        K.dma("pool", IDB, cst[:, 0:128])
        K.dma("pool", ONB, cst[:, 128:256])

        def dump(name, v, shape):
            if name in dbg_names:
                o = nc.dram_tensor("dbg_" + name, list(shape), v.ap.dtype, kind="ExternalOutput").ap()
                dbg_out[name] = o
                K.dma("sp", o, v)
                dump_tiles.append(v.tile)

        dump_tiles = []

        PT = {}

        def stage(name, rows):
            n = sum(r[1].shape[0] for r in rows)
            stg = sb("stg_" + name, [n, 128], F32)
            off = 0
            cols = {}
            for key, ap in rows:
                r = ap.shape[0]
                K.dma("sp", stg[off:off + r, :], ap)
                cols[key] = (off, r)
                off += r
            pt = sb("pt_" + name, [128, n], F32)
            p = nps()
            mm(p[:, 0:n], stg[0:n, :], IDF[0:n, 0:n])
            cp(pt, p[:, 0:n])
            for key, (o, r) in cols.items():
                PT[key] = pt[:, o:o + r]

        r128 = lambda ap: ap.rearrange("(t p) -> t p", p=128)
        rowsA = []
        for l in range(2):
            rowsA += [("n1g%d" % l, r128(norm1_g[l])), ("n2g%d" % l, r128(norm2_g[l])), ("convb%d" % l, r128(conv_b[l])),
                      ("ssdng%d" % l, r128(ssd_ng[l])), ("sgung%d" % l, r128(sgu_ng[l])), ("s5d%d" % l, r128(s5_d[l])),
                      ("glub%d" % l, r128(glu_b[l]))]
        rowsA += [("fng", r128(fin_g)), ("cv0", r128(cvec[0])), ("cv1", r128(cvec[1]))]
        stage("A", rowsA)
        stage("B", [("adab%d" % l, r128(ada_b[l])) for l in range(2)])
        rowsC = []
        r128d = lambda ap: ap.rearrange("d (t p) -> (d t) p", p=128)
        for l in range(2):
            rowsC += [("lreL%d" % l, r128d(lam_re[l])), ("limL%d" % l, r128d(lam_im[l])),
                      ("sreL%d" % l, r128d(st_re[l])), ("simL%d" % l, r128d(st_im[l]))]
        stage("C", rowsC)
        for l in range(2):
            for d in range(2):
                for nm in ("lre", "lim", "sre", "sim"):
                    PT["%s%d%d" % (nm, l, d)] = PT["%sL%d" % (nm, l)][:, d * 8:(d + 1) * 8]
        rowsD = []
        for l in range(2):
            for tap in range(5):
                rowsD.append(("cw%d%d" % (l, tap), r128(conv_w[l, tap])))
        stage("D", rowsD)

        SC = sb("SC", [128, 2, 8], BF16)
        act(SC[:, 0, :], PT["cv0"], AF.Silu)
        act(SC[:, 1, :], PT["cv1"], AF.Silu)
        wbi = [0]

        wc_slots = {}
        wc = nc.dram_tensor("wcache", [48, 128, 4096], BF16).ap() if len(PASSES) > 1 else None
        pass_idx = [0]

        def load_w(src_ap, ncols_total, key=None):
            w = WB[wbi[0] % 2]
            wbi[0] += 1
            t = src_ap.shape[0] // 128
            n = src_ap.shape[1]
            flat = w[:, 0:t * n]
            view = flat.re("p (t n) -> p t n", t=t)
            if key is not None and wc is not None and pass_idx[0] > 0:
                K.dma("sp", flat, wc[wc_slots[key]][:, 0:t * n])
                return view
            K.dma("pool", view, src_ap.rearrange("(t p) n -> p t n", p=128))
            if key is not None and wc is not None:
                wc_slots[key] = len(wc_slots)
                K.dma("sp", wc[wc_slots[key]][:, 0:t * n], flat)
            return view


        FT = [sb("FT%d" % i, [128, 1024 if i < 2 else ML], F32) for i in range(3)]
        XS = None
        xpre = {}

        def prefetch_x(row0_, nt_):
            bcf_ = V(BCT.tile, BCT.ap.bitcast(F32))
            w2f_ = V(WB[2].tile, WB[2].ap.bitcast(F32))
            bufs = [bcf_[:, 0:1024], bcf_[:, 1024:2048], w2f_[:, 0:1024], w2f_[:, 1024:2048]]
            assert nt_ // 128 <= len(bufs)
            for tb in range(nt_ // 128):
                K.dma("sp", bufs[tb], xin[row0_ + tb * 128:row0_ + (tb + 1) * 128, :])
                xpre[(row0_, tb)] = bufs[tb]

        def load_x():
            XS_ = [FT[2], FT[3]] if ML == 1024 else [FT[0], FT[1]]
            for tb in range(NT // 128):
                if (row0, tb) in xpre:
                    xs = xpre[(row0, tb)]
                else:
                    xs = XS_[tb % 2]
                    K.dma("sp", xs, xin[row0 + tb * 128:row0 + (tb + 1) * 128, :])
                for half in range(2):
                    p = nps()
                    for q in range(4):
                        t = half * 4 + q
                        tr(p[:, q * 128:(q + 1) * 128], xs[:, t * 128:(t + 1) * 128], IDF)
                    cp(XT[:, half * 4:half * 4 + 4, tb * 128:(tb + 1) * 128], p.re("p (q n) -> p q n", q=4), eng=("act" if half else "dve"))

        YP = sb("YP", [128, max(4 * ML, 4096)], BF16)
        SQ = YP[:, 0:4096].re("p (k n) -> p k n", k=8)
        RS = sb("RS", [128, 512], F32)
        NTMP = sb("NTMP", [128, 512], F32)
        NTMP2 = sb("NTMP2", [128, 512], F32)

        def rstd_from_sq(sqv, ntile, n, dim):
            p = nps()
            for k in range(ntile):
                mm(p[:, 0:n], ONB, sqv[:, k, :], start=(k == 0), stop=(k == ntile - 1))
            act(RS[:, 0:n], p[:, 0:n], AF.Ln, bias=float(dim * EPS))
            act(RS[:, 0:n], RS[:, 0:n], AF.Exp, scale=-0.5)

        def norm_to_HT(l, which, final=False, blks=None):
            for (c0, n, v) in (blks if blks is not None else BLKS):
                act(SQ[:, 0:4, :], XT[:, 0:4, c0:c0 + n], AF.Square)
                act(SQ[:, 4:8, :], XT[:, 4:8, c0:c0 + n], AF.Square)
                rstd_from_sq(SQ, 8, n, D)
                for k in range(8):
                    if final:
                        A = FNA
                    else:
                        A, B = NA[(l, v, which)]
                    nt_ = NTMP if k % 2 == 0 else NTMP2
                    stt(nt_, XT[:, k, c0:c0 + n], A[:, k:k + 1], RS, ALU.mult, ALU.mult)
                    if final:
                        yield (c0, k, nt_)
                    else:
                        act(HT[:, k, c0:c0 + n], nt_, AF.Identity, bias=B[:, k:k + 1])

        XBC = sb("XBC", [128, 8 * (ML + 8)], BF16)
        BCT = sb("BCT", [128, 4 * ML], BF16)
        BMT = sb("BMT", [128, max(2 * ML, 2048)], BF16)
        XBCf = V(XBC.tile, XBC.ap.bitcast(F32))
        for i_ in range(3, 7):
            FT.append(XBCf[:, (i_ - 3) * ML:(i_ - 2) * ML])
        WB3f = V(WB[3].tile, WB[3].ap.bitcast(F32))
        FT.append(WB3f[:, 0:ML])
        TI32 = V(WB[3].tile, WB[3].ap.bitcast(I32))[:, ML:2 * ML]
        MEXPS = [sb("MEXP%d" % i, [128, 1024], BF16) for i in range(2)]
        MMTS = [sb("MMT%d" % i, [128, 1024], BF16) for i in range(2)]
        XDT = [sb("XDT%d" % i, [128, 512], BF16) for i in range(2)]
        CBM = [sb("CBM%d" % i, [128, 256], BF16) for i in range(2)]
        WDT = sb("WDT", [128, 128], BF16)
        DG = sb("DG", [128, 8 * 5 * 128], BF16)
        DD = sb("DD", [128, 8 * 128], BF16)
        CBR = sb("CBR", [1, 1024], BF16)
        SGB = sb("SGB", [1, 512], BF16)
        WST = sb("WST", [128, 512], BF16)
        WSL = sb("WSL", [128, 128], BF16)
        DTB = sb("DTB", [128, 16], F32)
        NEGA = sb("NEGA", [128, 16], F32)
        SDD = sb("SDD", [128, 8], F32)
        DT = sb("DT", [128, 8, 16], F32)
        ADT = sb("ADT", [128, 8, 16], F32)
        ACUM = sb("ACUM", [128, 2, 8, 8], F32)
        TOT = sb("TOT", [128, 2, 8, 8], F32)
        DTE = sb("DTE", [128, 2, 8, 8], F32)
        EA = sb("EA", [128, 2, 8, 8], F32)
        CDC = sb("CDC", [128, 2, 8, 8], F32)
        HST = [sb("HST%d" % i, [128, 512], F32) for i in range(2)]
        HSTB = [sb("HSTB%d" % i, [128, 512], BF16) for i in range(2)]
        STG4 = FT[2][:, 0:512]
        SSO = [FT[i][:, 512:1024].re("p (q n) -> p q n", q=4) for i in range(2)]
        NS5 = sb("NS5", [128, 128], F32)
        NS5T = sb("NS5T", [128, 128], F32)
        SSNG = sb("SSNG", [128, 4], F32)
        SGNG = sb("SGNG", [128, 2], F32)
        BT = DG[:, 0:32 * 128]
        CT = sb("CT", [128, 16 * 128], BF16)
        CQ = [BMT[:, 0:2048].re("p (r q n) -> p r q n", r=2, q=8), CT.re("p (r q n) -> p r q n", r=2, q=8)]
        BQv = BT.re("p (d r q n) -> p d r q n", d=2, r=2, q=8)
        TSv = MMTS[0].re("p (q n) -> p q n", q=8)
        EM = sb("EM", [128, 16 * 128], BF16)
        K.dma("pool", EM, emat)
        EMv = EM.re("p (a b n) -> p a b n", a=4, b=4)
        POS2 = sb("POS2", [128, 256], F32)
        K.dma("sp", POS2, posrow[:, 0:256].partition_broadcast(128))
        DBLK = sb("DBLK", [128, 8], F32)
        CN = [sb("CN%d" % i, [128, 8, 16], F32) for i in range(2)]
        RHO4L = [[sb("RHO4%d%d" % (l_, d), [128, 8], F32) for d in range(2)] for l_ in range(2)]
        TH4L = [[sb("TH4%d%d" % (l_, d), [128, 8], F32) for d in range(2)] for l_ in range(2)]
        H0LL = [[[sb("H0LL%d%d%d" % (l_, d, ri), [128, 8], F32) for ri in range(2)] for d in range(2)] for l_ in range(2)]
        s5w = [nc.dram_tensor("s5w%d" % l_, [128, 9216], BF16).ap() for l_ in range(2)]
        GW = sb("GW", [128, 2 * 256], BF16)
        BRAW = [sb("BRAW%d" % i, [128, 8, 16], F32) for i in range(2)]
        CC = [FT[i][0:16, :].re("c (g s) -> c g s", g=16) for i in range(2)]
        HSB = [YP[:, 0:ML], YP[:, ML:2 * ML]]
        Y5 = BCT[:, 0:2 * ML].re("p (t n) -> p t n", t=2)
        YG = BCT[:, 2 * ML:4 * ML].re("p (t n) -> p t n", t=2)
        SIG = NTMP
        memset(NS5, 0.0)
        PI = 3.141592653589793

        def sincos(out_s, out_c, ang, tmpf, tmpi):
            ts(tmpi, ang, 1.0 / TWO_PI, ALU.mult)
            cp(tmpf, tmpi)
            stt(tmpf, tmpf, -TWO_PI, ang, ALU.mult, ALU.add)
            ts(tmpf, tmpf, PI, ALU.min, -PI, ALU.max)
            act(out_s, tmpf, AF.Sin)
            act(tmpf, tmpf, AF.Abs)
            act(out_c, tmpf, AF.Sin, bias=PI / 2, scale=-1.0)

        def cmul(or_, oi_, ar, ai, br, bi, t0, t1):
            tt(or_, ar, br, ALU.mult)
            tt(t0, ai, bi, ALU.mult)
            tt(or_, or_, t0, ALU.subtract)
            tt(oi_, ar, bi, ALU.mult)
            tt(t1, ai, br, ALU.mult)
            tt(oi_, oi_, t1, ALU.add)


        LDT2 = sb("LDT2", [128, 32], F32)
        S16 = {nm: sb("S16_" + nm, [128, 16], F32) for nm in
               ("step", "are", "the", "lbr", "lbi", "t0", "t1", "t2", "qr", "qi", "nr")}

        def s5_pre(l, startup=True):
            if startup:
                hs0 = HST[0]
                hs1 = HST[1]
                G = [None, None, hs1, V(MEXPS[0].tile, MEXPS[0].ap.bitcast(F32)), None, None]
                PWN = V(MEXPS[1].tile, MEXPS[1].ap.bitcast(I32))[:, 0:128]
            else:
                f32v = lambda T_: V(T_.tile, T_.ap.bitcast(F32))
                Gb, Gy, Gw = f32v(BCT), f32v(YP), f32v(WB[2])
                G = [Gb[:, 0:1024], Gb[:, 1024:2048], Gy[:, 0:1024], Gy[:, 1024:2048], None, Gw[:, 0:1024]]
                PWN = V(WB[2].tile, WB[2].ap.bitcast(I32))[:, 1024:1152]
            PWR, PWI, PWA, PWB = (G[3][:, i * 128:(i + 1) * 128] for i in range(4))
            if startup:
                BBD = [hs0[:, i * 256:(i + 1) * 256].re("p (d q c) -> p d q c", d=2, q=8) for i in range(2)]
                BBT = [hs1[:, i * 256:(i + 1) * 256].re("p (d q c) -> p d q c", d=2, q=8) for i in range(2)]
            else:
                BBD = [G[5][:, i * 256:(i + 1) * 256].re("p (d q c) -> p d q c", d=2, q=8) for i in range(2)]
                BBT = [G[5][:, (2 + i) * 256:(3 + i) * 256].re("p (d q c) -> p d q c", d=2, q=8) for i in range(2)]
            K.dma("sp", BRAW[0], b_re[l].rearrange("(pr p) c -> p pr c", p=128))
            K.dma("sp", BRAW[1], b_im[l].rearrange("(pr p) c -> p pr c", p=128))
            K.dma("sp", CC[0], c_re[l].rearrange("g c s -> c g s"))
            K.dma("sp", CC[1], c_im[l].rearrange("g c s -> c g s"))
            K.dma("sp", LDT2, log_dt[l:l + 1].rearrange("o d g -> o (d g)").partition_broadcast(128))
            with nc.allow_non_contiguous_dma(reason="tiny D-skip gather"):
                for j0 in range(4):
                    K.dma("sp", DBLK[32 * j0:32 * j0 + 32, :], s5_d[l].rearrange("(q r) -> r q", r=32))
            yield
            pcn = nps()
            for ri in range(2):
                for pair in range(8):
                    mm(pcn[:, (ri * 8 + pair) * 16:(ri * 8 + pair + 1) * 16], CC[ri][:, 2 * pair:2 * pair + 2, :].re("c g s -> c (g s)"), IDF[0:16, 0:16])
            for ri in range(2):
                cp(CN[ri], pcn[:, ri * 128:(ri + 1) * 128].re("p (q c) -> p q c", q=8), eng="act")
            S = S16
            d8 = lambda T_: T_.re("p (d q) -> p d q", d=2)
            LV = LDT2.re("p (d q g) -> p d q g", d=2, g=2)
            ts(d8(S["step"]), LV[:, :, :, 0], MG2[:, 0:1], ALU.mult)
            stt(d8(S["step"]), LV[:, :, :, 1], MG2[:, 1:2], d8(S["step"]), ALU.mult, ALU.add)
            act(S["step"], S["step"], AF.Exp)
            lre = PT["lreL%d" % l]
            lim = PT["limL%d" % l]
            tt(S["are"], lre, S["step"], ALU.mult)
            tt(S["the"], lim, S["step"], ALU.mult)
            p4v = lambda T_: T_.re("p (d m q) -> p d m q", d=2, m=8)
            mrow = MROW.unsq(1).unsq(3).bc([128, 2, 8, 8])
            tt(p4v(PWA), d8(S["are"]).unsq(2).bc([128, 2, 8, 8]), mrow, ALU.mult)
            act(PWA, PWA, AF.Exp)
            tt(p4v(PWB), d8(S["the"]).unsq(2).bc([128, 2, 8, 8]), mrow, ALU.mult)
            sincos(PWI, PWR, PWB, PWB, PWN) if False else None
            ts(PWN, PWB, 1.0 / TWO_PI, ALU.mult)
            stt(PWB, PWB, 1.0 / TWO_PI, PWN, ALU.mult, ALU.subtract)
            act(PWI, PWB, AF.Sin, scale=TWO_PI)
            act(PWB, PWB, AF.Abs)
            act(PWR, PWB, AF.Sin, bias=PI / 2, scale=-TWO_PI)
            tt(PWR, PWR, PWA, ALU.mult)
            tt(PWI, PWI, PWA, ALU.mult)
            PR = p4v(PWR)
            PI_ = p4v(PWI)
            RHO4, TH4, H0L = RHO4L[l], TH4L[l], H0LL[l]
            for d in range(2):
                cp(RHO4[d], p4v(PWA)[:, d, 7, :], eng="act")
                ts(TH4[d], S["the"][:, d * 8:(d + 1) * 8], 4.0 / TWO_PI, ALU.mult)
            cp(d8(S["lbr"]), PR[:, :, 4, :])
            cp(d8(S["lbi"]), PI_[:, :, 4, :])
            ts(S["nr"], S["lbr"], -1.0, ALU.add)
            tt(S["t1"], lre, lre, ALU.mult)
            tt(S["t2"], lim, lim, ALU.mult)
            tt(S["t1"], S["t1"], S["t2"], ALU.add)
            K.op("dve", lambda e: e.reciprocal(S["t1"].ap, S["t1"].ap), [S["t1"].tile], [S["t1"].tile])
            tt(S["qr"], S["nr"], lre, ALU.mult)
            tt(S["t2"], S["lbi"], lim, ALU.mult)
            tt(S["qr"], S["qr"], S["t2"], ALU.add)
            tt(S["qr"], S["qr"], S["t1"], ALU.mult)
            tt(S["qi"], S["lbi"], lre, ALU.mult)
            tt(S["t2"], S["nr"], lim, ALU.mult)
            tt(S["qi"], S["qi"], S["t2"], ALU.subtract)
            tt(S["qi"], S["qi"], S["t1"], ALU.mult)
            qrb = d8(S["qr"]).unsq(3).bc([128, 2, 8, 16])
            qib = d8(S["qi"]).unsq(3).bc([128, 2, 8, 16])
            brb = BRAW[0].unsq(1).bc([128, 2, 8, 16])
            bib = BRAW[1].unsq(1).bc([128, 2, 8, 16])
            cmul(BBD[0], BBD[1], brb, bib, qrb, qib, BBT[0], BBT[1])
            h0r, h0i = PT["sreL%d" % l], PT["simL%d" % l]
            p4r, p4i = S["t0"], S["t1"]
            cp(d8(p4r), PR[:, :, 7, :])
            cp(d8(p4i), PI_[:, :, 7, :])
            cmul(S["qr"], S["qi"], p4r, p4i, h0r, h0i, S["t2"], S["nr"])
            for d in range(2):
                cp(H0L[d][0], S["qr"][:, d * 8:(d + 1) * 8], eng="act")
                cp(H0L[d][1], S["qi"][:, d * 8:(d + 1) * 8], eng="act")
            yield
            t4 = lambda T_: T_.re("p (m q c) -> p m q c", m=8, q=8)
            av = lambda T_: T_.re("p (q i g c) -> p q i g c", q=8, i=4, g=2)
            pv_ = lambda T_, pair: T_[:, pair * 128:(pair + 1) * 128]
            TST = [NTMP.re("p (q n) -> p q n", q=4), RS.re("p (q n) -> p q n", q=4)]

            def asm(dst, tab, s0, step, sign):
                dv = av(dst)
                if step > 0:
                    src = t4(tab)[:, s0:s0 + 4]
                else:
                    src = t4(tab)[:, s0 - 3:s0 + 1][:, ::-1]
                srcq = src.re("p m q c -> p q m c")
                mg = MG2 if sign > 0 else MG2N
                for g2 in range(2):
                    ts(dv[:, :, :, g2, :], srcq, mg[:, g2:g2 + 1], ALU.mult)

            def per_d(d, XR, XI, YR, YI, A5, A6, A7, A2, TA, TB, TC):
                prb = PR[:, d].unsq(3).bc([128, 8, 8, 16])
                pib = PI_[:, d].unsq(3).bc([128, 8, 8, 16])
                cmul(t4(XR), t4(XI), BBD[0][:, d].unsq(1).bc([128, 8, 8, 16]), BBD[1][:, d].unsq(1).bc([128, 8, 8, 16]), prb, pib, t4(TA), t4(TB))
                yield
                cmul(t4(YR), t4(YI), CN[0].unsq(1).bc([128, 8, 8, 16]), CN[1].unsq(1).bc([128, 8, 8, 16]), prb, pib, t4(TA), t4(TB))
                yield
                if d == 0:
                    asm(A5, XR, 3, -1, 1.0)
                    asm(A6, XI, 3, -1, -1.0)
                    asm(A7, YR, 3, 1, 1.0)
                    asm(A2, YI, 3, 1, 1.0)
                else:
                    asm(A5, XR, 3, 1, 1.0)
                    asm(A6, XI, 3, 1, -1.0)
                    asm(A7, YR, 3, -1, 1.0)
                    asm(A2, YI, 3, -1, 1.0)
                yield
                pTs = [nps(), nps()]
                for half in range(2):
                    for p4 in range(4):
                        pair = half * 4 + p4
                        o = pTs[half][:, p4 * 128:(p4 + 1) * 128]
                        mm(o, pv_(A5, pair), pv_(A7, pair), start=True, stop=False)
                        mm(o, pv_(A6, pair), pv_(A2, pair), start=False, stop=True)
                for half in range(2):
                    src = pTs[half].re("p (q n) -> p q n", q=4)
                    if d == 0:
                        tt(TST[half], src, MLO.unsq(1).bc([128, 4, 128]), ALU.mult)
                    else:
                        tmpT = TC[:, half * 512:(half + 1) * 512].re("p (q n) -> p q n", q=4)
                        tt(tmpT, src, MUP.unsq(1).bc([128, 4, 128]), ALU.mult)
                        tt(TST[half], TST[half], tmpT, ALU.add)
                yield
                for ri in range(2):
                    if d == 0:
                        asm(A5 if ri == 0 else A7, XR if ri == 0 else XI, 6, -1, 1.0)
                    else:
                        asm(A5 if ri == 0 else A7, XR if ri == 0 else XI, 3, 1, 1.0)
                yield
                for ri in range(2):
                    srcA = A5 if ri == 0 else A7
                    for half in range(2):
                        p = nps()
                        for p4 in range(4):
                            mm(p[:, p4 * 128:(p4 + 1) * 128], pv_(srcA, half * 4 + p4), IDF)
                        cp(BQv[:, d, ri, half * 4:half * 4 + 4, :], p.re("p (q n) -> p q n", q=4), eng="act")
                    yield
                for ri in range(2):
                    dstA = A6 if ri == 0 else A2
                    if d == 0:
                        asm(dstA, YR if ri == 0 else YI, 4, 1, 1.0 if ri == 0 else -1.0)
                    else:
                        asm(dstA, YR if ri == 0 else YI, 7, -1, 1.0 if ri == 0 else -1.0)
                    cp(CQ[d][:, ri].re("p q n -> p (q n)"), dstA, eng="act")
                yield
            f32v_ = lambda T_: V(T_.tile, T_.ap.bitcast(F32))
            if startup:
                w2f = f32v_(WB[2])
                htf0 = f32v_(HTB[0]).re("p k n -> p (k n)")
                htf1 = f32v_(HTB[1]).re("p k n -> p (k n)")
                upf = f32v_(UP)
                bcf = f32v_(BCT)
                ypf = f32v_(YP)
                g0 = per_d(0, FT[0], FT[1], FT[3], FT[4], FT[5], FT[6], FT[7], FT[2], w2f[:, 0:1024], w2f[:, 1024:2048], G[2])
                g1 = per_d(1, htf0[:, 0:1024], htf0[:, 1024:2048], htf1[:, 0:1024], htf1[:, 1024:2048],
                           upf[:, 0:1024], upf[:, 1024:2048], upf[:, 2048:3072], bcf[:, 0:1024], bcf[:, 1024:2048], ypf[:, 0:1024], ypf[:, 1024:2048])
                alive = [g0, g1]
                while alive:
                    for g_ in list(alive):
                        try:
                            next(g_)
                        except StopIteration:
                            alive.remove(g_)
                    yield
            else:
                for d_ in range(2):
                    yield from per_d(d_, FT[0], FT[1], FT[3], FT[4], FT[5], FT[6], FT[7], FT[2], G[0], G[1], G[2])
            for pair in range(8):
                stt(TSv[:, pair, :], IDF, DBLK[:, pair:pair + 1], TST[pair // 4][:, pair % 4, :], ALU.mult, ALU.add)
            K.dma("sp", s5w[l][:, 0:4096], BT)
            K.dma("sp", s5w[l][:, 4096:6144], BMT[:, 0:2048])
            K.dma("sp", s5w[l][:, 6144:8192], CT)
            K.dma("sp", s5w[l][:, 8192:9216], MMTS[0])
            yield

        def s5_load(l):
            K.dma("pool", GW.re("p (t n) -> p t n", t=2), glu_w[l].rearrange("(t p) n -> p t n", p=128))
            K.dma("sp", BT, s5w[l][:, 0:4096])
            K.dma("sp", BMT[:, 0:2048], s5w[l][:, 4096:6144])
            K.dma("sp", CT, s5w[l][:, 6144:8192])
            K.dma("sp", MMTS[0], s5w[l][:, 8192:9216])

        def mixers(l):
            K.dma("sp", DTB, dt_bias[l:l + 1, :].partition_broadcast(128))
            K.dma("sp", NEGA, a_log[l:l + 1, :].partition_broadcast(128))
            K.dma("sp", SDD, ssd_d[l:l + 1, :].partition_broadcast(128))
            act(NEGA, NEGA, AF.Exp)
            ts(NEGA, NEGA, -1.0, ALU.mult)
            K.dma("pool", CBR, conv_b[l:l + 1, :])
            K.dma("pool", SGB, sgu_b[l:l + 1].rearrange("o h q -> o (h q)"))
            DGv = DG.re("p (t a n) -> p t a n", t=8, a=5)
            for t in range(8):
                for tap in range(5):
                    ts(DGv[:, t, tap, :], IDB, PT["cw%d%d" % (l, tap)][:, t:t + 1], ALU.mult)
            DDv = DD.re("p (h n) -> p h n", h=8)
            for h in range(8):
                ts(DDv[:, h, :], IDB, SDD[:, h:h + 1], ALU.mult)
            ts(SSNG, PT["ssdng%d" % l], float(np.sqrt(512.0)), ALU.mult)
            ts(SGNG, PT["sgung%d" % l], float(np.sqrt(256.0)), ALU.mult)
            WSTv = WST.re("p (h q) -> p h q", h=4)
            for h in range(4):
                K.dma("pool", WSL, sgu_w[l, h])
                p = nps()
                mm(p[:, 0:128], WSL, IDB)
                cp(WSTv[:, h, :], p[:, 0:128])
            groups = GROUPS
            for gi, (g0, GL, seqs, v) in enumerate(groups):
                SL = seqs[0][1]
                nseq = len(seqs)
                nblk = GL // 512
                ZS = UP[:, 0:4 * GL].re("p (t n) -> p t n", t=4)
                GV = UP[:, 4 * ML:4 * ML + 2 * GL].re("p (t n) -> p t n", t=2)
                GU = WB[2][:, 0:2 * GL].re("p (t n) -> p t n", t=2)
                U5 = WB[2][:, 2 * ML:2 * ML + 2 * GL].re("p (t n) -> p t n", t=2)
                XBCv = XBC[:, 0:8 * nseq * (SL + 4)].re("p (t s n) -> p t s n", t=8, s=nseq)
                memset(XBCv[:, :, :, 0:2], 0.0)
                memset(XBCv[:, :, :, SL + 2:SL + 4], 0.0)

                def xbc_dst(t, b):
                    res = []
                    if nseq == 2:
                        for s_ in range(2):
                            res.append((XBCv[:, t, s_, 2:2 + 256], (s_ * 256, 256)))
                    else:
                        res.append((XBCv[:, t, 0, 2 + b * 512:2 + (b + 1) * 512], (0, 512)))
                    return res

                def proj(wv, m, b):
                    p = nps()
                    c0 = g0 + b * 512
                    for k in range(8):
                        mm(p, wv[:, k, m * 128:(m + 1) * 128], HT[:, k, c0:c0 + 512], start=(k == 0), stop=(k == 7))
                    return p

                wv = load_w(w_in[l][:, 0:512], 512, key=("win", l, 0))
                for m in range(4):
                    for b in range(nblk):
                        act(ZS[:, m, b * 512:(b + 1) * 512], proj(wv, m, b), AF.Silu)
                for part in range(2):
                    wv = load_w(w_in[l][:, 512 + part * 512:1024 + part * 512], 512, key=("win", l, 1 + part))
                    for m in range(4):
                        for b in range(nblk):
                            p = proj(wv, m, b)
                            for (dst, (s0, sn)) in xbc_dst(part * 4 + m, b):
                                cp(dst, p[:, s0:s0 + sn], eng=("act" if (m + b) % 2 == 0 else "dve"))
                wv = load_w(w_in[l][:, 1552:2064], 512, key=("win", l, 3))
                for m in range(4):
                    for b in range(nblk):
                        dstt = (GU if m < 2 else GV)[:, m % 2, b * 512:(b + 1) * 512]
                        act(dstt, proj(wv, m, b), AF.Gelu)
                wv = load_w(w_in[l][:, 2064:2320], 256, key=("win", l, 4))
                for m in range(2):
                    for b in range(nblk):
                        cp(U5[:, m, b * 512:(b + 1) * 512], proj(wv, m, b), eng=("act" if (m + b) % 2 == 0 else "dve"))
                K.dma("pool", WDT.re("p (t n) -> p t n", t=8), w_in[l][:, 1536:1552].rearrange("(t p) n -> p t n", p=128))
                WDTv = WDT.re("p (t n) -> p t n", t=8)
                nchg = GL // 128
                pdt = nps()
                for ci in range(nchg):
                    for k in range(8):
                        mm(pdt[:, ci * 16:(ci + 1) * 16], HT[:, k, g0 + ci * 128:g0 + (ci + 1) * 128], WDTv[:, k, :], start=(k == 0), stop=(k == 7))
                DTv = DT[:, 0:nchg, :]
                tt(DTv, pdt[:, 0:nchg * 16].re("p (c h) -> p c h", h=16), DTB.unsq(1).bc([128, nchg, 16]), ALU.add)
                act(DTv, DTv, AF.Exp)
                act(DTv, DTv, AF.Ln, bias=1.0)
                ADTv = ADT[:, 0:nchg, :]
                tt(ADTv, DTv, NEGA.unsq(1).bc([128, nchg, 16]), ALU.mult)
                for d in range(2):
                    rhs_ = ADTv[:, :, d * 8:(d + 1) * 8]
                    pa = nps()
                    mm(pa[:, 0:nchg * 8].re("p (c h) -> p c h", h=8), (TRI_LE if d == 0 else TRI_GE), rhs_)
                    cp(ACUM[:, d, 0:nchg, :], pa[:, 0:nchg * 8].re("p (c h) -> p c h", h=8))
                    pb = nps()
                    mm(pb[:, 0:nchg * 8].re("p (c h) -> p c h", h=8), ONF, rhs_)
                    cp(TOT[:, d, 0:nchg, :], pb[:, 0:nchg * 8].re("p (c h) -> p c h", h=8))
                tt(DTE[:, :, 0:nchg, :], TOT[:, :, 0:nchg, :], ACUM[:, :, 0:nchg, :], ALU.subtract)
                act(DTE[:, :, 0:nchg, :], DTE[:, :, 0:nchg, :], AF.Exp)
                act(EA[:, :, 0:nchg, :], ACUM[:, :, 0:nchg, :], AF.Exp)
                act(CDC[:, :, 0:nchg, :], TOT[:, :, 0:nchg, :], AF.Exp)
                for d in range(2):
                    tt(DTE[:, d, 0:nchg, :], DTE[:, d, 0:nchg, :], DTv[:, :, d * 8:(d + 1) * 8], ALU.mult)

                mark('%s:L%d:win_done' % (pname, l))
                BCTv = BCT[:, 0:4 * GL].re("p (t n) -> p t n", t=4)
                XSFv = YP[:, 0:4 * GL].re("p (t n) -> p t n", t=4)
                for t in range(8):
                    for si in range(nseq):
                        for b in range(max(1, SL // 512)):
                            n = min(512, SL)
                            p = nps()
                            for tap in range(5):
                                mm(p[:, 0:n], DGv[:, t, tap, :], XBCv[:, t, si, b * 512 + tap:b * 512 + tap + n], start=(tap == 0), stop=(tap == 4))
                            dstv = XSFv[:, t] if t < 4 else BCTv[:, t - 4]
                            act(dstv[:, si * SL + b * 512:si * SL + b * 512 + n], p[:, 0:n], AF.Silu, bias=PT["convb%d" % l][:, t:t + 1])

                XST = WB[3].re("p (c n) -> p c n", n=512)
                BMTv = BMT.re("p (c n) -> p c n", n=256)
                YPv = YP[:, 0:4 * ML].re("p (c n) -> p c n", n=512)
                for cg in range(GL // 128):
                    tokc = slice(cg * 128, (cg + 1) * 128)
                    px = nps()
                    for q in range(4):
                        mm(px[:, q * 128:(q + 1) * 128], XSFv[:, q, tokc], IDB)
                    cp(XST[:, cg, :], px, eng="act")
                    pbm = nps()
                    for g in range(2):
                        mm(pbm[:, g * 128:(g + 1) * 128], BCTv[:, g, tokc], IDB)
                    cp(BMTv[:, cg, :], pbm[:, 0:256])
                for si, (s0, L) in enumerate(seqs):
                    nch = L // 128
                    cb0 = s0 // 128
                    for d in range(2):
                        if v == 1:
                            for q in range(4):
                                K.dma("sp", STG4[:, q * 128:(q + 1) * 128], st_ssd[l, d, q * 128:(q + 1) * 128, :])
                            p = nps()
                            for q in range(4):
                                mm(p[:, q * 128:(q + 1) * 128], STG4[:, q * 128:(q + 1) * 128], IDF)
                            cp(HST[d], p)
                        else:
                            memset(HST[d], 0.0)
                    for step in range(nch):
                        bg()
                        first = step < nch // 2
                        for d in range(2):
                            c = step if d == 0 else nch - 1 - step
                            cg = cb0 + c
                            cp(HSTB[d], HST[d], eng="act")
                            xw = XDT[d]
                            tt(xw.re("p (h x) -> p h x", h=8), XST[:, cg, :].re("p (h x) -> p h x", h=8), DTE[:, d, cg, :].unsq(2).bc([128, 8, 64]), ALU.mult)
                            pst = nps()
                            for g in range(2):
                                mm(pst[:, g * 256:(g + 1) * 256], BMTv[:, cg, g * 128:(g + 1) * 128], xw[:, g * 256:(g + 1) * 256])
                            po = nps()
                            for g in range(2):
                                mm(po[:, g * 256:(g + 1) * 256], BCTv[:, 2 + g, s0 + c * 128:s0 + (c + 1) * 128], HSTB[d][:, g * 256:(g + 1) * 256])
                            tt(HST[d].re("p (h x) -> p h x", h=8), HST[d].re("p (h x) -> p h x", h=8), CDC[:, d, cg, :].unsq(2).bc([128, 8, 64]), ALU.mult)
                            tt(HST[d], HST[d], pst, ALU.add)
                            eab = EA[:, d, cg, :].unsq(2).bc([128, 8, 64])
                            pov = po.re("p (h x) -> p h x", h=8)
                            if first:
                                tt(YPv[:, cg, :].re("p (h x) -> p h x", h=8), pov, eab, ALU.mult)
                            else:
                                tmpy = FT[d][:, 0:512]
                                tt(tmpy.re("p (h x) -> p h x", h=8), pov, eab, ALU.mult)
                                tt(YPv[:, cg, :], YPv[:, cg, :], tmpy, ALU.add)
                    if v == 0:
                        for d in range(2):
                            p = nps()
                            for q in range(4):
                                mm(p[:, q * 128:(q + 1) * 128], HST[d][:, q * 128:(q + 1) * 128], IDF)
                            so = SSO[d]
                            cp(so, p.re("p (q n) -> p q n", q=4))
                            K.dma("sp", ns_ssd[si, l, d].rearrange("(q p) n -> p q n", p=128), so)
                            if so.tile not in EXTRA_OUT_TILES:
                                EXTRA_OUT_TILES.append(so.tile)
                    rsb = [FT[0], FT[1]]

                    def emit_rs(cg_):
                        for d_ in range(2):
                            msk = TRI_LE if d_ == 0 else TRI_GE
                            tt(rsb[d_].re("p (h q) -> p h q", h=8), msk.unsq(1).bc([128, 8, 128]), ADT[:, cg_, d_ * 8:(d_ + 1) * 8].unsq(2).bc([128, 8, 128]), ALU.mult)

                    emit_rs(cb0)
                    for c in range(nch):
                        bg()
                        cg = cb0 + c
                        tok = slice(s0 + c * 128, s0 + (c + 1) * 128)
                        for d in range(2):
                            tt(XDT[d].re("p (h x) -> p h x", h=8), XST[:, cg, :].re("p (h x) -> p h x", h=8), DT[:, cg, d * 8:(d + 1) * 8].unsq(2).bc([128, 8, 64]), ALU.mult)
                        pcb = nps()
                        for g in range(2):
                            mm(pcb[:, g * 128:(g + 1) * 128], BCTv[:, g, tok], BCTv[:, 2 + g, tok])
                        for d in range(2):
                            strict = TRI_GT if d == 0 else TRI_LT
                            for hh in range(2):
                                psg = nps()
                                mm(psg, strict, rsb[d][:, hh * 512:(hh + 1) * 512])
                                act(MEXPS[d][:, hh * 512:(hh + 1) * 512], psg, AF.Exp)
                        tt(CBM[0].re("p (g q) -> p g q", g=2), pcb[:, 0:256].re("p (g q) -> p g q", g=2), TRI_LE.unsq(1).bc([128, 2, 128]), ALU.mult)
                        tt(CBM[1].re("p (g q) -> p g q", g=2), pcb[:, 0:256].re("p (g q) -> p g q", g=2), TRI_GE.unsq(1).bc([128, 2, 128]), ALU.mult)
                        for d in range(2):
                            tt(MMTS[d].re("p (g r q) -> p g r q", g=2, r=4), MEXPS[d].re("p (g r q) -> p g r q", g=2, r=4),
                               CBM[d].re("p (g q) -> p g q", g=2).unsq(2).bc([128, 2, 4, 128]), ALU.mult)
                        if c + 1 < nch:
                            emit_rs(cg + 1)
                        py = nps()
                        for h in range(8):
                            for d in range(2):
                                mm(py[:, h * 64:(h + 1) * 64], MMTS[d][:, h * 128:(h + 1) * 128], XDT[d][:, h * 64:(h + 1) * 64], start=(d == 0), stop=False)
                            mm(py[:, h * 64:(h + 1) * 64], DDv[:, h, :], XST[:, cg, h * 64:(h + 1) * 64], start=False, stop=True)
                        tt(YPv[:, cg, :], py, YPv[:, cg, :], ALU.add)
                    for c2 in range(0, nch, 2):
                        t0_ = s0 + c2 * 128
                        Gv = MEXPS[0].re("p (q n) -> p q n", q=4)
                        SQg = MEXPS[1].re("p (q n) -> p q n", q=4)
                        for cc in range(2):
                            pt_ = nps()
                            ptv = pt_.re("p (q n) -> p q n", q=4)
                            for q in range(4):
                                mm(ptv[:, q, :], YPv[:, cb0 + c2 + cc, q * 128:(q + 1) * 128], IDB)
                            tt(Gv[:, :, cc * 128:(cc + 1) * 128], ptv, ZS[:, :, t0_ + cc * 128:t0_ + (cc + 1) * 128], ALU.mult)
                        act(SQg, Gv, AF.Square)
                        rstd_from_sq(SQg, 4, 256, 512)
                        for q in range(4):
                            stt(HT[:, q, g0 + t0_:g0 + t0_ + 256], Gv[:, q, :], SSNG[:, q:q + 1], RS[:, 0:256], ALU.mult, ALU.mult)
                mark('%s:L%d:ssd_done' % (pname, l))
                for b in range(nblk):
                    cs = slice(b * 512, (b + 1) * 512)
                    for t in range(2):
                        act(SQ[:, t, :], GV[:, t, cs], AF.Square)
                    rstd_from_sq(SQ[:, 0:2, :], 2, 512, 256)
                    for t in range(2):
                        stt(GV[:, t, cs], GV[:, t, cs], SGNG[:, t:t + 1], RS, ALU.mult, ALU.mult)
                for ci in range(nchg):
                    tok = slice(ci * 128, (ci + 1) * 128)
                    pv = nps()
                    for t in range(2):
                        mm(pv[:, t * 128:(t + 1) * 128], GV[:, t, tok], IDB)
                    vt = XDT[0]
                    cp(vt[:, 0:256], pv[:, 0:256], eng="act")
                    pm_ = nps()
                    for h in range(4):
                        o = pm_[(h % 2) * 64:(h % 2 + 1) * 64, (h // 2) * 128:(h // 2 + 1) * 128]
                        mm(o, vt[:, h * 64:(h + 1) * 64], WSTv[:, h, :], start=True, stop=False)
                        mm(o, ONB[0:1, 0:64], SGB[0:1, h * 128:(h + 1) * 128], start=False, stop=True)
                    for t in range(2):
                        tt(HT[:, 4 + t, g0 + ci * 128:g0 + (ci + 1) * 128], pm_[:, t * 128:(t + 1) * 128], GU[:, t, tok], ALU.mult)

                mark('%s:L%d:sgu_done' % (pname, l))
                s5_load(l)
                RHO4, TH4, H0L = RHO4L[l], TH4L[l], H0LL[l]
                NJs = SL // 4
                NJ = GL // 4
                ppb = 512 // NJ
                ngrp = 8 // ppb
                UB = WB[2][:, 0:8 * NJ].re("p (q n) -> p q n", q=8)
                YBS = UP[:, 4 * ML:4 * ML + 8 * NJ].re("p (q n) -> p q n", q=8)
                HX = [[YP[:, 0:8 * NJ].re("p (q n) -> p q n", q=8), YP[:, 2048:2048 + 8 * NJ].re("p (q n) -> p q n", q=8)],
                      [UP[:, 0:8 * NJ].re("p (q n) -> p q n", q=8), UP[:, 2048:2048 + 8 * NJ].re("p (q n) -> p q n", q=8)]]
                if v == 1:
                    uview = lambda t, j0: U5[:, t, :].re("p (jj k c) -> p k c jj", jj=4, k=4)[:, j0]
                    yview = lambda t, k0: YG[:, t, 0:1024].re("p (jj k c) -> p k c jj", jj=4, k=4)[:, k0]
                    pv3 = lambda p_: p_[:, 0:256].re("p (c jj) -> p c jj", jj=4)
                else:
                    uview = lambda t, j0: U5[:, t, :].re("p (j k) -> p k j", k=4)[:, j0]
                    yview = lambda t, k0: YG[:, t, 0:512].re("p (j k) -> p k j", k=4)[:, k0]
                    pv3 = lambda p_: p_[:, 0:128]
                for pair in range(8):
                    t, p4 = pair // 4, pair % 4
                    p = nps()
                    for j0 in range(4):
                        mm(pv3(p), EMv[:, p4, j0, :], uview(t, j0), start=(j0 == 0), stop=(j0 == 3))
                    cp(UB[:, pair, :], p[:, 0:NJ], eng="act")
                T5 = lambda i: FT[i // 2][:, (i % 2) * 512:(i % 2 + 1) * 512]
                v4 = lambda T_: T_.re("p (q s n) -> p q s n", q=ppb, s=nseq)
                ntab = ppb * NJs
                tabv = lambda T_: T_[:, 0:ntab].re("p (q n) -> p q n", q=ppb)
                tbc = lambda T_: tabv(T_).unsq(2).bc([128, ppb, nseq, NJs])
                CH = [FT[3][:, 0:512], FT[3][:, 512:1024], FT[4][:, 0:512], FT[4][:, 512:1024], FT[5][:, 0:512]]
                TAB = [(FT[0][:, 0:512], FT[0][:, 512:1024], HST[0]), (FT[1][:, 0:512], FT[1][:, 512:1024], HST[1])]
                grp_l2 = [(d_, g_) for d_ in range(2) for g_ in range(ngrp)]

                def emit_tables(i_):
                    d_, g_ = grp_l2[i_]
                    fr_, c_, s_n = TAB[i_ % 2]
                    TI = TI32[:, 0:ntab]
                    tt(tabv(fr_), POS2[:, 0:NJs].unsq(1).bc([128, ppb, NJs]), TH4[d_][:, g_ * ppb:(g_ + 1) * ppb].unsq(2).bc([128, ppb, NJs]), ALU.mult)
                    ts(TI, fr_[:, 0:ntab], 1.0, ALU.mult)
                    tt(fr_[:, 0:ntab], fr_[:, 0:ntab], TI, ALU.subtract)
                    act(s_n[:, 0:ntab], fr_[:, 0:ntab], AF.Sin, scale=TWO_PI)
                    act(fr_[:, 0:ntab], fr_[:, 0:ntab], AF.Abs)
                    act(c_[:, 0:ntab], fr_[:, 0:ntab], AF.Sin, bias=PI / 2, scale=-TWO_PI)

                emit_tables(0)
                for gi_l2, (d, gq) in enumerate(grp_l2):
                    if True:
                        bg()
                        prs = slice(gq * ppb, (gq + 1) * ppb)
                        psS = [nps(), nps()]
                        for ri in range(2):
                            for q in range(ppb):
                                mm(psS[ri][:, q * NJ:(q + 1) * NJ], BQv[:, d, ri, gq * ppb + q, :], UB[:, gq * ppb + q, :])
                        if gi_l2 + 1 < len(grp_l2):
                            emit_tables(gi_l2 + 1)
                        fr, cs_, sn_ = TAB[gi_l2 % 2]
                        wr, wi, t1, gr, gi_ = CH
                        Sr = v4(psS[0]) if d == 0 else v4(psS[0])[:, :, :, ::-1]
                        Si = v4(psS[1]) if d == 0 else v4(psS[1])[:, :, :, ::-1]
                        tt(v4(wr), Sr, tbc(cs_), ALU.mult)
                        tt(v4(t1), Si, tbc(sn_), ALU.mult)
                        tt(wr, wr, t1, ALU.add)
                        tt(v4(wi), Si, tbc(cs_), ALU.mult)
                        tt(v4(t1), Sr, tbc(sn_), ALU.mult)
                        tt(wi, wi, t1, ALU.subtract)
                        if v == 1:
                            tt(v4(wr)[:, :, 0, 0:1], v4(wr)[:, :, 0, 0:1], H0L[d][0][:, prs].unsq(2), ALU.add)
                            tt(v4(wi)[:, :, 0, 0:1], v4(wi)[:, :, 0, 0:1], H0L[d][1][:, prs].unsq(2), ALU.add)
                        for q in range(ppb):
                            pair = gq * ppb + q
                            rb = RHO4[d][:, pair:pair + 1].bc([128, NJs])
                            for s_ in range(nseq):
                                sl = slice(q * NJ + s_ * NJs, q * NJ + (s_ + 1) * NJs)
                                scan(gr[:, sl], rb, wr[:, sl], 0.0)
                                scan(gi_[:, sl], rb, wi[:, sl], 0.0)
                        hr, hi = wr, wi
                        tt(v4(hr), v4(gr), tbc(cs_), ALU.mult)
                        tt(v4(t1), v4(gi_), tbc(sn_), ALU.mult)
                        tt(hr, hr, t1, ALU.subtract)
                        tt(v4(hi), v4(gr), tbc(sn_), ALU.mult)
                        tt(v4(gi_), v4(gi_), tbc(cs_), ALU.mult)
                        tt(hi, hi, gi_, ALU.add)
                        for ri, hh in enumerate((hr, hi)):
                            hv = v4(hh)
                            if v == 0:
                                for s_ in range(nseq):
                                    base = (((s_ * 2 + l) * 2 + d) * 2 + ri) * 8
                                    cp(NS5[:, base + gq * ppb:base + (gq + 1) * ppb], hv[:, :, s_, NJs - 1], eng="act")
                            hx = HX[d][ri][:, prs, :].re("p q (s n) -> p q s n", s=nseq)
                            if v == 1:
                                h0 = PT[("sre%d%d" if ri == 0 else "sim%d%d") % (l, d)][:, prs].unsq(2).unsq(3)
                            if d == 0:
                                cp(hx[:, :, :, 1:NJs], hv[:, :, :, 0:NJs - 1], eng="act")
                                if v == 1:
                                    cp(hx[:, :, :, 0:1], h0, eng="act")
                                else:
                                    memset(hx[:, :, :, 0:1], 0.0, eng="pool")
                            else:
                                cp(hx[:, :, :, 0:NJs - 1], hv[:, :, :, 0:NJs - 1][:, :, :, ::-1])
                                if v == 1:
                                    cp(hx[:, :, :, NJs - 1:NJs], h0, eng="act")
                                else:
                                    memset(hx[:, :, :, NJs - 1:NJs], 0.0, eng="pool")
                for t in range(2):
                    for p4 in range(4):
                        pair = 4 * t + p4
                        p = nps()
                        mm(p[:, 0:NJ], TSv[:, pair, :], UB[:, pair, :], start=True, stop=False)
                        for d in range(2):
                            for ri in range(2):
                                mm(p[:, 0:NJ], CQ[d][:, ri, pair, :], HX[d][ri][:, pair, :], start=False, stop=(d == 1 and ri == 1))
                        cp(YBS[:, pair, :], p[:, 0:NJ], eng="act")
                    for k0 in range(4):
                        p = nps()
                        for p4 in range(4):
                            mm(p[:, 0:NJ], EMv[:, k0, p4, :], YBS[:, 4 * t + p4, :], start=(p4 == 0), stop=(p4 == 3))
                        act(yview(t, k0), pv3(p), AF.Gelu)
                mark('%s:L%d:s5scan_done' % (pname, l))
                GWv = GW.re("p (t n) -> p t n", t=2)
                for m in range(2):
                    for b in range(nblk):
                        p = nps()
                        for k in range(2):
                            mm(p, GWv[:, k, m * 128:(m + 1) * 128], YG[:, k, b * 512:(b + 1) * 512], start=(k == 0), stop=(k == 1))
                        sig_ = NTMP if (m + b) % 2 == 0 else NTMP2
                        act(sig_, p, AF.Sigmoid, bias=PT["glub%d" % l][:, m:m + 1])
                        tt(HT[:, 6 + m, g0 + b * 512:g0 + (b + 1) * 512], sig_, YG[:, m, b * 512:(b + 1) * 512], ALU.mult)
                if l == 0:
                    dump("mixed" + pname, HT[:, :, 0:512], [128, 8, 512])
                    if GL > 512:
                        dump("mixed2" + pname, HT[:, :, 512:1024], [128, 8, 512])

        RL = XDT[0]
        rlc = [0]
        YS = [FT[0], FT[1], FT[2]]
        if ML == 512:
            SQF0 = sb("SQF0", [128, 512], F32)
            SQF1 = sb("SQF1", [128, 512], F32)
        s5gen = s5_pre(0, True)
        s5gen1 = s5_pre(1, False) if ML == 1024 else s5_pre(1, True)
        SQD = float(np.sqrt(D))
        NA = {}
        MODW = [[sb("MODW%d%d" % (l_, w_), [128, 8, 2], F32) for w_ in range(6)] for l_ in range(2)]
        NAT = {(l_, v_, w_): sb("NA%d%d%d" % (l_, v_, w_), [128, 8], F32) for l_ in range(2) for v_ in range(2) for w_ in range(2)}
        FNA = sb("FNA", [128, 8], F32)
        ts(FNA, PT["fng"], SQD, ALU.mult)

        def ada_mm(l, part, w):
            pm = PSB[6 + l]
            for j in range(4):
                ch = part * 4 + j
                for k in range(8):
                    mm(pm[:, 2 * ch:2 * ch + 2], w[:, k, j * 128:(j + 1) * 128], SC[:, :, k], start=(k == 0), stop=(k == 7))
            if part % 2 == 1:
                wh = part // 2
                tt(MODW[l][wh], pm[:, wh * 16:(wh + 1) * 16].re("p (c v) -> p c v", v=2),
                   PT["adab%d" % l][:, wh * 8:(wh + 1) * 8].unsq(2).bc([128, 8, 2]), ALU.add)
                for which, (gname, sc_i, sh_i) in enumerate((("n1g%d" % l, 1, 0), ("n2g%d" % l, 4, 3))):
                    if wh == sc_i:
                        for v in range(2):
                            A = NAT[(l, v, which)]
                            ts(A, MODW[l][sc_i][:, :, v], 1.0, ALU.add, SQD, ALU.mult)
                            tt(A, A, PT[gname], ALU.mult)
                            NA[(l, v, which)] = (A, MODW[l][sh_i][:, :, v])

        ada_order = [(l_, part) for l_ in range(2) for part in range(12)]
        ada_state = {"next": 0, "pending": None}
        ada_done = set()

        def bg(prefetch=True):
            st_ = ada_state
            newp = None
            if prefetch and st_["next"] < len(ada_order):
                l_, part = ada_order[st_["next"]]
                st_["next"] += 1
                w = load_w(ada_w[l_][:, part * 512:(part + 1) * 512], 512)
                newp = (l_, part, w)
            if st_["pending"] is not None:
                ada_mm(*st_["pending"])
                ada_done.add(st_["pending"][:2])
            st_["pending"] = newp
            if newp is None and st_["next"] >= len(ada_order):
                ada_all_done[0] = True

        def ada_need(l, part):
            while (l, part) not in ada_done:
                bg()
            bg(prefetch=False)

        pname, row0, NT, BLKS, GROUPS = PASSES[0]
        load_x()
        while (0, 3) not in ada_done:
            bg()
            for _ in range(3):
                next(s5gen, None)
        bg(prefetch=False)
        for _ in s5gen:
            pass
        if ML != 1024:
            for _ in s5gen1:
                pass
        mark('setup_done')
        for pass_i, (pname, row0, NT, BLKS, GROUPS) in enumerate(PASSES):
            pass_idx[0] = pass_i
            if pass_i > 0:
                K.wait_all("sp", [WB[0].tile, WB[1].tile])
            mark(pname + ':start')
            if pass_i > 0:
                load_x()
            for l in range(2):
                ada_need(l, 3)
                for _ in norm_to_HT(l, 0):
                    pass
                if l == 0:
                    dump("h1" + pname, HT[:, :, 0:512], [128, 8, 512])

                mark('%s:L%d:norm1_done' % (pname, l))
                mixers(l)
                mark('%s:L%d:mixers_done' % (pname, l))

                ada_need(l, 9)
                G1 = lambda v: MODW[l][2][:, :, v]
                wparts = [load_w(w_out[l][:, part * 512:(part + 1) * 512], 512, key=("wout", l, part)) for part in range(2)]
                for bi, (c0, n, v) in enumerate(BLKS):
                    for mt in range(8):
                        w = wparts[mt // 4]
                        m = mt % 4
                        p = nps()
                        for k in range(8):
                            mm(p, w[:, k, m * 128:(m + 1) * 128], HT[:, k, c0:c0 + n], start=(k == 0), stop=(k == 7))
                        stt(XT[:, mt, c0:c0 + n], p, G1(v)[:, mt:mt + 1], XT[:, mt, c0:c0 + n], ALU.mult, ALU.add)
                    for _ in norm_to_HT(l, 1, blks=[BLKS[bi]]):
                        pass
                if l == 0:
                    dump("xmid" + pname, XT[:, :, 0:512], [128, 8, 512])
                mark('%s:L%d:wout_done' % (pname, l))
                mark('%s:L%d:norm2_done' % (pname, l))
                ada_need(l, 11)
                if l == 1 and pass_i + 1 < len(PASSES) and ML == 1024:
                    prefetch_x(PASSES[pass_i + 1][1], PASSES[pass_i + 1][2])
                G2 = lambda v: MODW[l][5][:, :, v]
                UPv = UP[:, 0:4 * NT].re("p (j n) -> p j n", j=4)
                for part in range(8):
                    w1 = load_w(ffn_w1[l][:, part * 512:(part + 1) * 512], 512, key=("w1", l, part))
                    for j in range(4):
                        for (c0, n, v) in BLKS:
                            p = nps()
                            for k in range(8):
                                mm(p, w1[:, k, j * 128:(j + 1) * 128], HT[:, k, c0:c0 + n], start=(k == 0), stop=(k == 7))
                            rl_ = XDT[rlc[0] % 2]
                            rlc[0] += 1
                            act(rl_, p, AF.Relu)
                            tt(UPv[:, j, c0:c0 + n], rl_, rl_, ALU.mult)
                    if l == 0:
                        for _ in range(2):
                            next(s5gen1, None)
                    w2 = load_w(ffn_w2[l][part * 512:(part + 1) * 512, :], 1024, key=("w2", l, part))
                    for mt in range(8):
                        for (c0, n, v) in BLKS:
                            p = nps()
                            for j in range(4):
                                mm(p, w2[:, j, mt * 128:(mt + 1) * 128], UPv[:, j, c0:c0 + n], start=(j == 0), stop=(j == 3))
                            stt(XT[:, mt, c0:c0 + n], p, G2(v)[:, mt:mt + 1], XT[:, mt, c0:c0 + n], ALU.mult, ALU.add)
                if l == 0:
                    dump("xout" + pname, XT[:, :, 0:512], [128, 8, 512])

                if l == 0:
                    for _ in s5gen1:
                        pass
            mark(pname + ':ffn_done')
            FNTv = lambda k: (FT[2 + k] if ML == 512 else FT[4 + k // 2][:, (k % 2) * 512:(k % 2 + 1) * 512]) if k < 6 or ML != 512 else [SQF0, SQF1][k - 6]
            yi = 0
            cur = None
            for (c0, k, tmp) in norm_to_HT(0, 0, final=True):
                cp(FNTv(k), tmp, eng="act")
                if k == 7:
                    for tb in range(4):
                        ys = YS[yi % len(YS)]
                        yi += 1
                        for half in range(2):
                            p = nps()
                            for q in range(4):
                                kk = half * 4 + q
                                tr(p[:, q * 128:(q + 1) * 128], FNTv(kk)[:, tb * 128:(tb + 1) * 128], IDF)
                            cp(ys[:, half * 512:(half + 1) * 512], p, eng=("act" if half else "dve"))
                        K.dma("sp", yout[row0 + c0 + tb * 128:row0 + c0 + (tb + 1) * 128, :], ys)
        mark('final_done')
        p = nps()
        mm(p[:, 0:128], NS5, IDF)
        cp(NS5T, p[:, 0:128])
        K.dma("sp", ns_s5, NS5T)
        EXTRA_OUT_TILES.append(NS5T.tile)
        K.wait_all("sp", [y_.tile for y_ in YS] + dump_tiles + EXTRA_OUT_TILES)
    import os, json
    if os.environ.get('KPHASES'):
        json.dump(PHASES, open(os.environ['KPHASES'], 'w'))
    return dram_in, dram_out, dbg_out


EXTRA_OUT_TILES = []
PHASES = []


def build_layer_mixers(env, l):
    pass


def make_consts():
    j = np.arange(128)[:, None]
    k = np.arange(128)[None, :]
    c = np.zeros((128, 8 * 128 + 10), np.float32)
    c[:, 0:128] = np.eye(128)
    c[:, 128:256] = 1.0
    c[:, 256:384] = (j <= k)
    c[:, 384:512] = (j >= k)
    c[:, 512:640] = (j > k)
    c[:, 640:768] = (j < k)
    c[:, 768] = (np.arange(128) < 64)
    c[:, 769] = (np.arange(128) >= 64)
    jb = (np.arange(128) // 32)[:, None]
    kb = (np.arange(128) // 32)[None, :]
    c[:, 770:898] = (jb <= kb)
    c[:, 898:1026] = (jb >= kb)
    c[:, 1026:1034] = np.arange(-3, 5, dtype=np.float32)[None, :]
    return c


def make_emat():
    e = np.zeros((128, 4, 4, 128), np.float32)
    for a in range(4):
        for b in range(4):
            for r in range(32):
                e[32 * a + r, a, b, 32 * b + r] = 1.0
    return e.reshape(128, 2048)


_CACHE = {}


def kernel(**inp):
    f = lambda a: np.ascontiguousarray(np.asarray(a, dtype=np.float32))
    dbg_names = tuple(inp.pop("_dbg", ()))
    ncores = 8
    key = dbg_names
    mode = inp.pop('_mode', 'all')
    nc = bass.Bass("TRN2", target_bir_lowering=False)
    dram_in, dram_out, dbg_out = build_program(nc, dbg_names, mode)
    shared = {}
    for name in ["ada_w", "ada_b", "norm1_g", "norm2_g", "w_in", "ssd_conv_w", "ssd_conv_b", "ssd_norm_g", "sgu_norm_g",
                 "sgu_w", "sgu_b", "s5_d", "s5_glu_w", "s5_glu_b", "w_out", "ffn_w1", "ffn_w2", "final_norm_g", "ssd_d"]:
        shared[name] = f(inp[name])
    shared["ssd_dt_bias"] = f(inp["ssd_dt_bias"]).reshape(2, 16)
    shared["ssd_a_log"] = f(inp["ssd_a_log"]).reshape(2, 16)
    shared["s5_lambda_re"] = f(inp["s5_lambda_re"]).reshape(2, 2, 1024)
    shared["s5_lambda_im"] = f(inp["s5_lambda_im"]).reshape(2, 2, 1024)
    shared["s5_log_dt"] = f(inp["s5_log_dt"])
    shared["s5_b_re"] = f(inp["s5_b_re"]).reshape(2, 1024, 16)
    shared["s5_b_im"] = f(inp["s5_b_im"]).reshape(2, 1024, 16)
    shared["s5_c_re"] = f(inp["s5_c_re"])
    shared["s5_c_im"] = f(inp["s5_c_im"])
    shared["consts"] = make_consts()
    shared["emat"] = make_emat()
    shared["posrow"] = np.arange(1024, dtype=np.float32)[None, :]
    xp = f(inp["x_prompt"])
    xs = f(inp["x_sample"])
    sssd = f(inp["state_ssd"])
    sre = f(inp["state_s5_re"])
    sim = f(inp["state_s5_im"])
    c = f(inp["c"])
    cctx = f(inp["c_ctx"])
    in_maps = []
    for core in range(ncores):
        b = core % 2
        m = dict(shared)
        m["xin"] = np.ascontiguousarray(np.concatenate([xp[2 * core], xp[2 * core + 1], xs[b]], axis=0) if mode == 'all' else (np.concatenate([xp[2 * core], xp[2 * core + 1]], axis=0) if mode == 'P' else xs[b]))
        m["st_ssd"] = np.ascontiguousarray(sssd[b].reshape(2, 2, 512, 128))
        m["st_re"] = np.ascontiguousarray(sre[b].reshape(2, 2, 1024))
        m["st_im"] = np.ascontiguousarray(sim[b].reshape(2, 2, 1024))
        m["cvec"] = np.ascontiguousarray(np.stack([cctx, c[b]], axis=0))
        in_maps.append({k: m[k] for k in dram_in})
    res = run_bass_kernel_spmd(nc, in_maps, core_ids=list(range(ncores)))
    R = res.results
    if dbg_names:
        kernel.dbg = {n: np.asarray(R[0]["dbg_" + n]).astype(np.float32) for n in dbg_out}
    if mode != 'all':
        kernel.raw = R
        return None
    y_prompt = np.zeros((16, 256, D), np.float32)
    y_sample = np.zeros((2, 1024, D), np.float32)
    ns_ssd = np.zeros((16, 2, 2, 8, 64, 128), np.float32)
    ns_re = np.zeros((16, 2, 2, 16, 64), np.float32)
    ns_im = np.zeros((16, 2, 2, 16, 64), np.float32)
    for core in range(ncores):
        y = R[core]["yout"]
        y_prompt[2 * core] = y[0:256]
        y_prompt[2 * core + 1] = y[256:512]
        if core < 2:
            y_sample[core] = y[512:1536]
        ns_ssd[2 * core:2 * core + 2] = R[core]["ns_ssd"].reshape(2, 2, 2, 8, 64, 128)
        s5 = R[core]["ns_s5"].reshape(2, 2, 2, 2, 8, 128)
        ns_re[2 * core:2 * core + 2] = s5[:, :, :, 0].reshape(2, 2, 2, 16, 64)
        ns_im[2 * core:2 * core + 2] = s5[:, :, :, 1].reshape(2, 2, 2, 16, 64)
    if dbg_names:
        kernel.dbg = {n: np.asarray(R[0]["dbg_" + n]).astype(np.float32) for n in dbg_out}
    return (y_prompt, y_sample, ns_ssd, ns_re, ns_im)
```

```python
import numpy as np
from contextlib import ExitStack
import concourse.bass as bass
import concourse.mybir as mybir
from concourse.bass_utils import run_bass_kernel_spmd

F32 = mybir.dt.float32
BF16 = mybir.dt.bfloat16
I32 = mybir.dt.int32
AF = mybir.ActivationFunctionType
ALU = mybir.AluOpType
AX = mybir.AxisListType

D = 1024
NT = 1536
EPS = 1e-6
TWO_PI = 6.283185307179586


class Sem:
    def __init__(self, h, name):
        self.h = h
        self.name = name
        self.total = 0


class Tile:
    def __init__(self, name, t):
        self.name = name
        self.t = t
        self.last_w = None
        self.reads = {}
        self.dsem = None


class V:
    def __init__(self, tile, ap):
        self.tile = tile
        self.ap = ap

    def __getitem__(self, key):
        return V(self.tile, self.ap[key])

    def re(self, pat, **kw):
        return V(self.tile, self.ap.rearrange(pat, **kw))

    def bc(self, shape):
        return V(self.tile, self.ap.to_broadcast(list(shape)))

    def unsq(self, ax):
        return V(self.tile, self.ap.unsqueeze(ax))

    @property
    def shape(self):
        return self.ap.shape


class Eng:
    def __init__(self, name, handle, sem):
        self.name = name
        self.h = handle
        self.sem = sem
        self.waited = {}
        self.snaps = {}


class Kern:
    def __init__(self, nc, stack):
        self.nc = nc
        self.stack = stack
        self.engs = {}
        self.nsem = 0
        self.pe_pending = []
        self.pe_pending_w = []

    def new_sem(self, name):
        h = self.stack.enter_context(self.nc.semaphore(name))
        self.nsem += 1
        return Sem(h, name)

    def setup(self):
        nc = self.nc
        for name, h in (("pe", nc.tensor), ("act", nc.scalar), ("dve", nc.vector), ("pool", nc.gpsimd), ("sp", nc.sync)):
            self.engs[name] = Eng(name, h, self.new_sem("e_" + name) if name != "sp" else None)

    def sb(self, name, shape, dtype):
        t = self.stack.enter_context(self.nc.sbuf_tensor(name, list(shape), dtype))
        tl = Tile(name, t)
        return V(tl, t[:])

    def ps(self, name, shape, dtype=F32):
        t = self.stack.enter_context(self.nc.psum_tensor(name, list(shape), dtype))
        tl = Tile(name, t)
        return V(tl, t[:])

    def _need(self, eng, reads, writes):
        need = {}

        def add(p):
            if p is None:
                return
            s, v = p
            if s.name not in need or need[s.name][1] < v:
                need[s.name] = (s, v)

        for t in reads:
            add(t.last_w)
        for t in writes:
            add(t.last_w)
            for p in t.reads.values():
                add(p)
        owner = {e.sem.name: e for e in self.engs.values() if e.sem is not None}
        for nm, (s, v) in sorted(need.items(), key=lambda kv: -kv[1][1]):
            if eng.sem is not None and s is eng.sem and eng.name == "pe":
                continue
            if eng.waited.get(nm, 0) >= v:
                continue
            eng.h.wait_ge(s.h, v)
            eng.waited[nm] = v
            ox = owner.get(nm)
            if ox is not None and ox is not eng and v in ox.snaps:
                for k2, v2 in ox.snaps[v].items():
                    if eng.waited.get(k2, 0) < v2:
                        eng.waited[k2] = v2

    def op(self, engname, fn, reads, writes, noinc=False):
        eng = self.engs[engname]
        reads = [r for r in reads if r is not None]
        if engname != "pe":
            for t in writes:
                assert t not in self.pe_pending, "write to a tile read by an unfinished matmul group: " + t.name
        self._need(eng, reads, writes)
        ins = fn(eng.h)
        if noinc:
            for t in reads:
                if t not in self.pe_pending:
                    self.pe_pending.append(t)
            for t in writes:
                if t not in self.pe_pending_w:
                    self.pe_pending_w.append(t)
            return ins
        eng.sem.total += 1
        ins.then_inc(eng.sem.h, 1)
        eng.snaps[eng.sem.total] = dict(eng.waited)
        p = (eng.sem, eng.sem.total)
        if engname == "pe":
            for t in self.pe_pending:
                if t not in writes:
                    t.reads[eng.sem.name] = p
            for t in self.pe_pending_w:
                t.last_w = p
                t.reads = {}
            self.pe_pending = []
            self.pe_pending_w = []
        for t in writes:
            t.last_w = p
            t.reads = {}
        for t in reads:
            if t not in writes:
                t.reads[eng.sem.name] = p
        return ins

    def dma(self, q, out, in_, **kw):
        eng = self.engs[q]
        reads = [in_.tile] if isinstance(in_, V) else []
        writes = [out.tile] if isinstance(out, V) else []
        self._need(eng, reads, writes)
        st = writes[0] if writes else reads[0]
        if st.dsem is None:
            st.dsem = {}
        qk = "sw" if q == "pool" else "hw"
        if qk not in st.dsem:
            st.dsem[qk] = self.new_sem("d%s_%s" % (qk, st.name))
        dsem = st.dsem[qk]
        oap = out.ap if isinstance(out, V) else out
        iap = in_.ap if isinstance(in_, V) else in_
        ins = eng.h.dma_start(out=oap, in_=iap, **kw)
        dsem.total += 16
        ins.then_inc(dsem.h, 16)
        p = (dsem, dsem.total)
        for t in writes:
            t.last_w = p
            t.reads = {}
        for t in reads:
            t.reads[dsem.name] = p

    def wait_all(self, q, tiles):
        self._need(self.engs[q], [], tiles)


K = None


def _tiles(*vs):
    return [v.tile for v in vs if isinstance(v, V)]


def _a(v):
    return v.ap if isinstance(v, V) else v


def mm(out, lhsT, rhs, start=True, stop=True):
    K.op("pe", lambda e: e.matmul(out.ap, lhsT.ap, rhs.ap, start=start, stop=stop), _tiles(lhsT, rhs), _tiles(out), noinc=(not stop))


def tr(out, in_, ident):
    K.op("pe", lambda e: e.transpose(out.ap, in_.ap, ident.ap), _tiles(in_, ident), _tiles(out))


def act(out, in_, func, bias=None, scale=None, eng="act"):
    kw = {}
    if bias is not None:
        kw["bias"] = _a(bias)
    if scale is not None:
        kw["scale"] = _a(scale)
    K.op("act", lambda e: e.activation(out.ap, in_.ap, func, **kw), _tiles(in_, bias, scale), _tiles(out))


def tt(out, a, b, op, eng="dve"):
    K.op(eng, lambda e: e.tensor_tensor(out.ap, a.ap, b.ap, op), _tiles(a, b), _tiles(out))


def ts(out, a, s1, op0, s2=None, op1=None, eng="dve"):
    if op1 is None:
        K.op(eng, lambda e: e.tensor_scalar(out.ap, a.ap, _a(s1), None, op0), _tiles(a, s1), _tiles(out))
    else:
        K.op(eng, lambda e: e.tensor_scalar(out.ap, a.ap, _a(s1), _a(s2), op0, op1), _tiles(a, s1, s2), _tiles(out))


def stt(out, a, s, b, op0, op1, eng="dve"):
    K.op(eng, lambda e: e.scalar_tensor_tensor(out.ap, a.ap, _a(s), b.ap, op0, op1), _tiles(a, s, b), _tiles(out))


def cp(out, a, eng="dve"):
    if eng == "act":
        K.op("act", lambda e: e.copy(out.ap, a.ap), _tiles(a), _tiles(out))
    else:
        K.op(eng, lambda e: e.tensor_copy(out.ap, a.ap), _tiles(a), _tiles(out))


def memset(out, val, eng="dve"):
    K.op(eng, lambda e: e.memset(out.ap, val), [], _tiles(out))


def scan(out, d0, d1, init, op0=ALU.mult, op1=ALU.add):
    K.op("dve", lambda e: e.tensor_tensor_scan(out.ap, d0.ap, d1.ap, _a(init), op0, op1), _tiles(d0, d1, init), _tiles(out))


def build_program(nc, dbg_names=(), mode='all'):
    global K
    PASS_P = ("P", 0, 512, [(0, 512, 0)], [(0, 512, [(0, 256), (256, 256)], 0)])
    if mode == 'all':
        NTT = 1536; NTM = 1024; ML = 1024
        PASSES = [("S", 512, 1024, [(0, 512, 1), (512, 512, 1)], [(0, 1024, [(0, 1024)], 1)]), PASS_P]
    elif mode == 'P':
        NTT = 512; NTM = 512; ML = 512
        PASSES = [PASS_P]
    else:
        NTT = 1024; NTM = 1024; ML = 1024
        PASSES = [("S", 0, 1024, [(0, 512, 1), (512, 512, 1)], [(0, 1024, [(0, 1024)], 1)])]
    NT = NTM
    BLKS = None
    GROUPS = None
    pname = None
    row0 = 0
    dram_in = {}
    dram_out = {}

    def din(name, shape, dt=F32):
        dram_in[name] = nc.dram_tensor(name, list(shape), dt, kind="ExternalInput").ap()
        return dram_in[name]

    def dout(name, shape):
        dram_out[name] = nc.dram_tensor(name, list(shape), F32, kind="ExternalOutput").ap()
        return dram_out[name]

    xin = din("xin", [NTT, D])
    st_ssd = din("st_ssd", [2, 2, 512, 128])
    st_re = din("st_re", [2, 2, 1024])
    st_im = din("st_im", [2, 2, 1024])
    cvec = din("cvec", [2, D])
    ada_w = din("ada_w", [2, D, 6 * D])
    ada_b = din("ada_b", [2, 6 * D])
    norm1_g = din("norm1_g", [2, D])
    norm2_g = din("norm2_g", [2, D])
    w_in = din("w_in", [2, D, 2320])
    conv_w = din("ssd_conv_w", [2, 5, D])
    conv_b = din("ssd_conv_b", [2, D])
    dt_bias = din("ssd_dt_bias", [2, 16])
    a_log = din("ssd_a_log", [2, 16])
    ssd_d = din("ssd_d", [2, 8])
    ssd_ng = din("ssd_norm_g", [2, 512])
    sgu_ng = din("sgu_norm_g", [2, 256])
    sgu_w = din("sgu_w", [2, 4, 128, 128])
    sgu_b = din("sgu_b", [2, 4, 128])
    lam_re = din("s5_lambda_re", [2, 2, 1024])
    lam_im = din("s5_lambda_im", [2, 2, 1024])
    log_dt = din("s5_log_dt", [2, 2, 16])
    b_re = din("s5_b_re", [2, 1024, 16])
    b_im = din("s5_b_im", [2, 1024, 16])
    c_re = din("s5_c_re", [2, 16, 16, 64])
    c_im = din("s5_c_im", [2, 16, 16, 64])
    s5_d = din("s5_d", [2, 256])
    glu_w = din("s5_glu_w", [2, 256, 256])
    glu_b = din("s5_glu_b", [2, 256])
    w_out = din("w_out", [2, D, D])
    ffn_w1 = din("ffn_w1", [2, D, 4 * D])
    ffn_w2 = din("ffn_w2", [2, 4 * D, D])
    fin_g = din("final_norm_g", [D])
    cst = din("consts", [128, 8 * 128 + 10])
    emat = din("emat", [128, 16 * 128])
    posrow = din("posrow", [1, 1024])

    yout = dout("yout", [NTT, D])
    ns_ssd = dout("ns_ssd", [2, 2, 2, 512, 128])
    ns_s5 = dout("ns_s5", [128, 128])
    dbg_out = {}

    del EXTRA_OUT_TILES[:]
    stack = ExitStack()
    with stack:
        K = Kern(nc, stack)
        K.setup()
        del PHASES[:]

        def mark(label):
            PHASES.append((label, {e: (g.sem.total if g.sem else 0) for e, g in K.engs.items()}))

        sb, ps = K.sb, K.ps

        XT = sb("XT", [128, 8, NTM], F32)
        HTB = [sb("HT%d" % i, [128, 8, 512], BF16) for i in range(NTM // 512)]

        class _HTW:
            def __getitem__(self, key):
                p_, k_, c_ = key
                a_ = c_.start or 0
                blk_ = a_ // 512
                assert (c_.stop - 1) // 512 == blk_
                return HTB[blk_][p_, k_, a_ - 512 * blk_:c_.stop - 512 * blk_]

        HT = _HTW()
        WB = [sb("WB%d" % i, [128, 4096], BF16) for i in range(2)] + [sb("WB%d" % i, [128, 4 * ML], BF16) for i in range(2, 4)]
        UP = sb("UP", [128, max(6 * ML, 4 * NTM)], BF16)
        CST = sb("CST", [128, 8 * 128 + 10], F32)
        IDB = sb("IDB", [128, 128], BF16)
        ONB = sb("ONB", [128, 128], BF16)
        PSB = [ps("PS%d" % i, [128, 512]) for i in range(8)]
        psi = [0]

        ada_all_done = [False]

        def nps():
            nb = 8 if ada_all_done[0] else 7
            p = PSB[psi[0] % nb]
            psi[0] += 1
            return p

        IDF = CST[:, 0:128]
        ONF = CST[:, 128:256]
        TRI_LE = CST[:, 256:384]
        TRI_GE = CST[:, 384:512]
        TRI_GT = CST[:, 512:640]
        TRI_LT = CST[:, 640:768]
        MG2 = CST[:, 768:770]
        MG2N = sb("MG2N", [128, 2], F32)
        MLO = CST[:, 770:898]
        MUP = CST[:, 898:1026]
        MROW = CST[:, 1026:1034]

        K.dma("sp", CST, cst)
        ts(MG2N, MG2, -1.0, ALU.mult)
        K.dma("pool", IDB, cst[:, 0:128])
        K.dma("pool", ONB, cst[:, 128:256])

        def dump(name, v, shape):
            if name in dbg_names:
                o = nc.dram_tensor("dbg_" + name, list(shape), v.ap.dtype, kind="ExternalOutput").ap()
                dbg_out[name] = o
                K.dma("sp", o, v)
                dump_tiles.append(v.tile)

        dump_tiles = []

        PT = {}

        def stage(name, rows):
            n = sum(r[1].shape[0] for r in rows)
            stg = sb("stg_" + name, [n, 128], F32)
            off = 0
            cols = {}
            for key, ap in rows:
                r = ap.shape[0]
                K.dma("sp", stg[off:off + r, :], ap)
                cols[key] = (off, r)
                off += r
            pt = sb("pt_" + name, [128, n], F32)
            p = nps()
            mm(p[:, 0:n], stg[0:n, :], IDF[0:n, 0:n])
            cp(pt, p[:, 0:n])
            for key, (o, r) in cols.items():
                PT[key] = pt[:, o:o + r]

        r128 = lambda ap: ap.rearrange("(t p) -> t p", p=128)
        rowsA = []
        for l in range(2):
            rowsA += [("n1g%d" % l, r128(norm1_g[l])), ("n2g%d" % l, r128(norm2_g[l])), ("convb%d" % l, r128(conv_b[l])),
                      ("ssdng%d" % l, r128(ssd_ng[l])), ("sgung%d" % l, r128(sgu_ng[l])), ("s5d%d" % l, r128(s5_d[l])),
                      ("glub%d" % l, r128(glu_b[l]))]
        rowsA += [("fng", r128(fin_g)), ("cv0", r128(cvec[0])), ("cv1", r128(cvec[1]))]
        stage("A", rowsA)
        stage("B", [("adab%d" % l, r128(ada_b[l])) for l in range(2)])
        rowsC = []
        r128d = lambda ap: ap.rearrange("d (t p) -> (d t) p", p=128)
        for l in range(2):
            rowsC += [("lreL%d" % l, r128d(lam_re[l])), ("limL%d" % l, r128d(lam_im[l])),
                      ("sreL%d" % l, r128d(st_re[l])), ("simL%d" % l, r128d(st_im[l]))]
        stage("C", rowsC)
        for l in range(2):
            for d in range(2):
                for nm in ("lre", "lim", "sre", "sim"):
                    PT["%s%d%d" % (nm, l, d)] = PT["%sL%d" % (nm, l)][:, d * 8:(d + 1) * 8]
        rowsD = []
        for l in range(2):
            for tap in range(5):
                rowsD.append(("cw%d%d" % (l, tap), r128(conv_w[l, tap])))
        stage("D", rowsD)

        SC = sb("SC", [128, 2, 8], BF16)
        act(SC[:, 0, :], PT["cv0"], AF.Silu)
        act(SC[:, 1, :], PT["cv1"], AF.Silu)
        wbi = [0]

        wc_slots = {}
        wc = nc.dram_tensor("wcache", [48, 128, 4096], BF16).ap() if len(PASSES) > 1 else None
        pass_idx = [0]

        def load_w(src_ap, ncols_total, key=None):
            w = WB[wbi[0] % 2]
            wbi[0] += 1
            t = src_ap.shape[0] // 128
            n = src_ap.shape[1]
            flat = w[:, 0:t * n]
            view = flat.re("p (t n) -> p t n", t=t)
            if key is not None and wc is not None and pass_idx[0] > 0:
                K.dma("sp", flat, wc[wc_slots[key]][:, 0:t * n])
                return view
            K.dma("pool", view, src_ap.rearrange("(t p) n -> p t n", p=128))
            if key is not None and wc is not None:
                wc_slots[key] = len(wc_slots)
                K.dma("sp", wc[wc_slots[key]][:, 0:t * n], flat)
            return view


        FT = [sb("FT%d" % i, [128, 1024 if i < 2 else ML], F32) for i in range(3)]
        XS = None
        xpre = {}

        def prefetch_x(row0_, nt_):
            bcf_ = V(BCT.tile, BCT.ap.bitcast(F32))
            w2f_ = V(WB[2].tile, WB[2].ap.bitcast(F32))
            bufs = [bcf_[:, 0:1024], bcf_[:, 1024:2048], w2f_[:, 0:1024], w2f_[:, 1024:2048]]
            assert nt_ // 128 <= len(bufs)
            for tb in range(nt_ // 128):
                K.dma("sp", bufs[tb], xin[row0_ + tb * 128:row0_ + (tb + 1) * 128, :])
                xpre[(row0_, tb)] = bufs[tb]

        def load_x():
            XS_ = [FT[2], FT[3]] if ML == 1024 else [FT[0], FT[1]]
            for tb in range(NT // 128):
                if (row0, tb) in xpre:
                    xs = xpre[(row0, tb)]
                else:
                    xs = XS_[tb % 2]
                    K.dma("sp", xs, xin[row0 + tb * 128:row0 + (tb + 1) * 128, :])
                for half in range(2):
                    p = nps()
                    for q in range(4):
                        t = half * 4 + q
                        tr(p[:, q * 128:(q + 1) * 128], xs[:, t * 128:(t + 1) * 128], IDF)
                    cp(XT[:, half * 4:half * 4 + 4, tb * 128:(tb + 1) * 128], p.re("p (q n) -> p q n", q=4), eng=("act" if half else "dve"))

        YP = sb("YP", [128, max(4 * ML, 4096)], BF16)
        SQ = YP[:, 0:4096].re("p (k n) -> p k n", k=8)
        RS = sb("RS", [128, 512], F32)
        NTMP = sb("NTMP", [128, 512], F32)
        NTMP2 = sb("NTMP2", [128, 512], F32)

        def rstd_from_sq(sqv, ntile, n, dim):
            p = nps()
            for k in range(ntile):
                mm(p[:, 0:n], ONB, sqv[:, k, :], start=(k == 0), stop=(k == ntile - 1))
            act(RS[:, 0:n], p[:, 0:n], AF.Ln, bias=float(dim * EPS))
            act(RS[:, 0:n], RS[:, 0:n], AF.Exp, scale=-0.5)

        def norm_to_HT(l, which, final=False, blks=None):
            for (c0, n, v) in (blks if blks is not None else BLKS):
                act(SQ[:, 0:4, :], XT[:, 0:4, c0:c0 + n], AF.Square)
                act(SQ[:, 4:8, :], XT[:, 4:8, c0:c0 + n], AF.Square)
                rstd_from_sq(SQ, 8, n, D)
                for k in range(8):
                    if final:
                        A = FNA
                    else:
                        A, B = NA[(l, v, which)]
                    nt_ = NTMP if k % 2 == 0 else NTMP2
                    stt(nt_, XT[:, k, c0:c0 + n], A[:, k:k + 1], RS, ALU.mult, ALU.mult)
                    if final:
                        yield (c0, k, nt_)
                    else:
                        act(HT[:, k, c0:c0 + n], nt_, AF.Identity, bias=B[:, k:k + 1])

        XBC = sb("XBC", [128, 8 * (ML + 8)], BF16)
        BCT = sb("BCT", [128, 4 * ML], BF16)
        BMT = sb("BMT", [128, max(2 * ML, 2048)], BF16)
        XBCf = V(XBC.tile, XBC.ap.bitcast(F32))
        for i_ in range(3, 7):
            FT.append(XBCf[:, (i_ - 3) * ML:(i_ - 2) * ML])
        WB3f = V(WB[3].tile, WB[3].ap.bitcast(F32))
        FT.append(WB3f[:, 0:ML])
        TI32 = V(WB[3].tile, WB[3].ap.bitcast(I32))[:, ML:2 * ML]
        MEXPS = [sb("MEXP%d" % i, [128, 1024], BF16) for i in range(2)]
        MMTS = [sb("MMT%d" % i, [128, 1024], BF16) for i in range(2)]
        XDT = [sb("XDT%d" % i, [128, 512], BF16) for i in range(2)]
        CBM = [sb("CBM%d" % i, [128, 256], BF16) for i in range(2)]
        WDT = sb("WDT", [128, 128], BF16)
        DG = sb("DG", [128, 8 * 5 * 128], BF16)
        DD = sb("DD", [128, 8 * 128], BF16)
        CBR = sb("CBR", [1, 1024], BF16)
        SGB = sb("SGB", [1, 512], BF16)
        WST = sb("WST", [128, 512], BF16)
        WSL = sb("WSL", [128, 128], BF16)
        DTB = sb("DTB", [128, 16], F32)
        NEGA = sb("NEGA", [128, 16], F32)
        SDD = sb("SDD", [128, 8], F32)
        DT = sb("DT", [128, 8, 16], F32)
        ADT = sb("ADT", [128, 8, 16], F32)
        ACUM = sb("ACUM", [128, 2, 8, 8], F32)
        TOT = sb("TOT", [128, 2, 8, 8], F32)
        DTE = sb("DTE", [128, 2, 8, 8], F32)
        EA = sb("EA", [128, 2, 8, 8], F32)
        CDC = sb("CDC", [128, 2, 8, 8], F32)
        HST = [sb("HST%d" % i, [128, 512], F32) for i in range(2)]
        HSTB = [sb("HSTB%d" % i, [128, 512], BF16) for i in range(2)]
        STG4 = FT[2][:, 0:512]
        SSO = [FT[i][:, 512:1024].re("p (q n) -> p q n", q=4) for i in range(2)]
        NS5 = sb("NS5", [128, 128], F32)
        NS5T = sb("NS5T", [128, 128], F32)
        SSNG = sb("SSNG", [128, 4], F32)
        SGNG = sb("SGNG", [128, 2], F32)
        BT = DG[:, 0:32 * 128]
        CT = sb("CT", [128, 16 * 128], BF16)
        CQ = [BMT[:, 0:2048].re("p (r q n) -> p r q n", r=2, q=8), CT.re("p (r q n) -> p r q n", r=2, q=8)]
        BQv = BT.re("p (d r q n) -> p d r q n", d=2, r=2, q=8)
        TSv = MMTS[0].re("p (q n) -> p q n", q=8)
        EM = sb("EM", [128, 16 * 128], BF16)
        K.dma("pool", EM, emat)
        EMv = EM.re("p (a b n) -> p a b n", a=4, b=4)
        POS2 = sb("POS2", [128, 256], F32)
        K.dma("sp", POS2, posrow[:, 0:256].partition_broadcast(128))
        DBLK = sb("DBLK", [128, 8], F32)
        CN = [sb("CN%d" % i, [128, 8, 16], F32) for i in range(2)]
        RHO4L = [[sb("RHO4%d%d" % (l_, d), [128, 8], F32) for d in range(2)] for l_ in range(2)]
        TH4L = [[sb("TH4%d%d" % (l_, d), [128, 8], F32) for d in range(2)] for l_ in range(2)]
        H0LL = [[[sb("H0LL%d%d%d" % (l_, d, ri), [128, 8], F32) for ri in range(2)] for d in range(2)] for l_ in range(2)]
        s5w = [nc.dram_tensor("s5w%d" % l_, [128, 9216], BF16).ap() for l_ in range(2)]
        GW = sb("GW", [128, 2 * 256], BF16)
        BRAW = [sb("BRAW%d" % i, [128, 8, 16], F32) for i in range(2)]
        CC = [FT[i][0:16, :].re("c (g s) -> c g s", g=16) for i in range(2)]
        HSB = [YP[:, 0:ML], YP[:, ML:2 * ML]]
        Y5 = BCT[:, 0:2 * ML].re("p (t n) -> p t n", t=2)
        YG = BCT[:, 2 * ML:4 * ML].re("p (t n) -> p t n", t=2)
        SIG = NTMP
        memset(NS5, 0.0)
        PI = 3.141592653589793

        def sincos(out_s, out_c, ang, tmpf, tmpi):
            ts(tmpi, ang, 1.0 / TWO_PI, ALU.mult)
            cp(tmpf, tmpi)
            stt(tmpf, tmpf, -TWO_PI, ang, ALU.mult, ALU.add)
            ts(tmpf, tmpf, PI, ALU.min, -PI, ALU.max)
            act(out_s, tmpf, AF.Sin)
            act(tmpf, tmpf, AF.Abs)
            act(out_c, tmpf, AF.Sin, bias=PI / 2, scale=-1.0)

        def cmul(or_, oi_, ar, ai, br, bi, t0, t1):
            tt(or_, ar, br, ALU.mult)
            tt(t0, ai, bi, ALU.mult)
            tt(or_, or_, t0, ALU.subtract)
            tt(oi_, ar, bi, ALU.mult)
            tt(t1, ai, br, ALU.mult)
            tt(oi_, oi_, t1, ALU.add)


        LDT2 = sb("LDT2", [128, 32], F32)
        S16 = {nm: sb("S16_" + nm, [128, 16], F32) for nm in
               ("step", "are", "the", "lbr", "lbi", "t0", "t1", "t2", "qr", "qi", "nr")}

        def s5_pre(l, startup=True):
            if startup:
                hs0 = HST[0]
                hs1 = HST[1]
                G = [None, None, hs1, V(MEXPS[0].tile, MEXPS[0].ap.bitcast(F32)), None, None]
                PWN = V(MEXPS[1].tile, MEXPS[1].ap.bitcast(I32))[:, 0:128]
            else:
                f32v = lambda T_: V(T_.tile, T_.ap.bitcast(F32))
                Gb, Gy, Gw = f32v(BCT), f32v(YP), f32v(WB[2])
                G = [Gb[:, 0:1024], Gb[:, 1024:2048], Gy[:, 0:1024], Gy[:, 1024:2048], None, Gw[:, 0:1024]]
                PWN = V(WB[2].tile, WB[2].ap.bitcast(I32))[:, 1024:1152]
            PWR, PWI, PWA, PWB = (G[3][:, i * 128:(i + 1) * 128] for i in range(4))
            if startup:
                BBD = [hs0[:, i * 256:(i + 1) * 256].re("p (d q c) -> p d q c", d=2, q=8) for i in range(2)]
                BBT = [hs1[:, i * 256:(i + 1) * 256].re("p (d q c) -> p d q c", d=2, q=8) for i in range(2)]
            else:
                BBD = [G[5][:, i * 256:(i + 1) * 256].re("p (d q c) -> p d q c", d=2, q=8) for i in range(2)]
                BBT = [G[5][:, (2 + i) * 256:(3 + i) * 256].re("p (d q c) -> p d q c", d=2, q=8) for i in range(2)]
            K.dma("sp", BRAW[0], b_re[l].rearrange("(pr p) c -> p pr c", p=128))
            K.dma("sp", BRAW[1], b_im[l].rearrange("(pr p) c -> p pr c", p=128))
            K.dma("sp", CC[0], c_re[l].rearrange("g c s -> c g s"))
            K.dma("sp", CC[1], c_im[l].rearrange("g c s -> c g s"))
            K.dma("sp", LDT2, log_dt[l:l + 1].rearrange("o d g -> o (d g)").partition_broadcast(128))
            with nc.allow_non_contiguous_dma(reason="tiny D-skip gather"):
                for j0 in range(4):
                    K.dma("sp", DBLK[32 * j0:32 * j0 + 32, :], s5_d[l].rearrange("(q r) -> r q", r=32))
            yield
            pcn = nps()
            for ri in range(2):
                for pair in range(8):
                    mm(pcn[:, (ri * 8 + pair) * 16:(ri * 8 + pair + 1) * 16], CC[ri][:, 2 * pair:2 * pair + 2, :].re("c g s -> c (g s)"), IDF[0:16, 0:16])
            for ri in range(2):
                cp(CN[ri], pcn[:, ri * 128:(ri + 1) * 128].re("p (q c) -> p q c", q=8), eng="act")
            S = S16
            d8 = lambda T_: T_.re("p (d q) -> p d q", d=2)
            LV = LDT2.re("p (d q g) -> p d q g", d=2, g=2)
            ts(d8(S["step"]), LV[:, :, :, 0], MG2[:, 0:1], ALU.mult)
            stt(d8(S["step"]), LV[:, :, :, 1], MG2[:, 1:2], d8(S["step"]), ALU.mult, ALU.add)
            act(S["step"], S["step"], AF.Exp)
            lre = PT["lreL%d" % l]
            lim = PT["limL%d" % l]
            tt(S["are"], lre, S["step"], ALU.mult)
            tt(S["the"], lim, S["step"], ALU.mult)
            p4v = lambda T_: T_.re("p (d m q) -> p d m q", d=2, m=8)
            mrow = MROW.unsq(1).unsq(3).bc([128, 2, 8, 8])
            tt(p4v(PWA), d8(S["are"]).unsq(2).bc([128, 2, 8, 8]), mrow, ALU.mult)
            act(PWA, PWA, AF.Exp)
            tt(p4v(PWB), d8(S["the"]).unsq(2).bc([128, 2, 8, 8]), mrow, ALU.mult)
            sincos(PWI, PWR, PWB, PWB, PWN) if False else None
            ts(PWN, PWB, 1.0 / TWO_PI, ALU.mult)
            stt(PWB, PWB, 1.0 / TWO_PI, PWN, ALU.mult, ALU.subtract)
            act(PWI, PWB, AF.Sin, scale=TWO_PI)
            act(PWB, PWB, AF.Abs)
            act(PWR, PWB, AF.Sin, bias=PI / 2, scale=-TWO_PI)
            tt(PWR, PWR, PWA, ALU.mult)
            tt(PWI, PWI, PWA, ALU.mult)
            PR = p4v(PWR)
            PI_ = p4v(PWI)
            RHO4, TH4, H0L = RHO4L[l], TH4L[l], H0LL[l]
            for d in range(2):
                cp(RHO4[d], p4v(PWA)[:, d, 7, :], eng="act")
                ts(TH4[d], S["the"][:, d * 8:(d + 1) * 8], 4.0 / TWO_PI, ALU.mult)
            cp(d8(S["lbr"]), PR[:, :, 4, :])
            cp(d8(S["lbi"]), PI_[:, :, 4, :])
            ts(S["nr"], S["lbr"], -1.0, ALU.add)
            tt(S["t1"], lre, lre, ALU.mult)
            tt(S["t2"], lim, lim, ALU.mult)
            tt(S["t1"], S["t1"], S["t2"], ALU.add)
            K.op("dve", lambda e: e.reciprocal(S["t1"].ap, S["t1"].ap), [S["t1"].tile], [S["t1"].tile])
            tt(S["qr"], S["nr"], lre, ALU.mult)
            tt(S["t2"], S["lbi"], lim, ALU.mult)
            tt(S["qr"], S["qr"], S["t2"], ALU.add)
            tt(S["qr"], S["qr"], S["t1"], ALU.mult)
            tt(S["qi"], S["lbi"], lre, ALU.mult)
            tt(S["t2"], S["nr"], lim, ALU.mult)
            tt(S["qi"], S["qi"], S["t2"], ALU.subtract)
            tt(S["qi"], S["qi"], S["t1"], ALU.mult)
            qrb = d8(S["qr"]).unsq(3).bc([128, 2, 8, 16])
            qib = d8(S["qi"]).unsq(3).bc([128, 2, 8, 16])
            brb = BRAW[0].unsq(1).bc([128, 2, 8, 16])
            bib = BRAW[1].unsq(1).bc([128, 2, 8, 16])
            cmul(BBD[0], BBD[1], brb, bib, qrb, qib, BBT[0], BBT[1])
            h0r, h0i = PT["sreL%d" % l], PT["simL%d" % l]
            p4r, p4i = S["t0"], S["t1"]
            cp(d8(p4r), PR[:, :, 7, :])
            cp(d8(p4i), PI_[:, :, 7, :])
            cmul(S["qr"], S["qi"], p4r, p4i, h0r, h0i, S["t2"], S["nr"])
            for d in range(2):
                cp(H0L[d][0], S["qr"][:, d * 8:(d + 1) * 8], eng="act")
                cp(H0L[d][1], S["qi"][:, d * 8:(d + 1) * 8], eng="act")
            yield
            t4 = lambda T_: T_.re("p (m q c) -> p m q c", m=8, q=8)
            av = lambda T_: T_.re("p (q i g c) -> p q i g c", q=8, i=4, g=2)
            pv_ = lambda T_, pair: T_[:, pair * 128:(pair + 1) * 128]
            TST = [NTMP.re("p (q n) -> p q n", q=4), RS.re("p (q n) -> p q n", q=4)]

            def asm(dst, tab, s0, step, sign):
                dv = av(dst)
                if step > 0:
                    src = t4(tab)[:, s0:s0 + 4]
                else:
                    src = t4(tab)[:, s0 - 3:s0 + 1][:, ::-1]
                srcq = src.re("p m q c -> p q m c")
                mg = MG2 if sign > 0 else MG2N
                for g2 in range(2):
                    ts(dv[:, :, :, g2, :], srcq, mg[:, g2:g2 + 1], ALU.mult)

            def per_d(d, XR, XI, YR, YI, A5, A6, A7, A2, TA, TB, TC):
                prb = PR[:, d].unsq(3).bc([128, 8, 8, 16])
                pib = PI_[:, d].unsq(3).bc([128, 8, 8, 16])
                cmul(t4(XR), t4(XI), BBD[0][:, d].unsq(1).bc([128, 8, 8, 16]), BBD[1][:, d].unsq(1).bc([128, 8, 8, 16]), prb, pib, t4(TA), t4(TB))
                yield
                cmul(t4(YR), t4(YI), CN[0].unsq(1).bc([128, 8, 8, 16]), CN[1].unsq(1).bc([128, 8, 8, 16]), prb, pib, t4(TA), t4(TB))
                yield
                if d == 0:
                    asm(A5, XR, 3, -1, 1.0)
                    asm(A6, XI, 3, -1, -1.0)
                    asm(A7, YR, 3, 1, 1.0)
                    asm(A2, YI, 3, 1, 1.0)
                else:
                    asm(A5, XR, 3, 1, 1.0)
                    asm(A6, XI, 3, 1, -1.0)
                    asm(A7, YR, 3, -1, 1.0)
                    asm(A2, YI, 3, -1, 1.0)
                yield
                pTs = [nps(), nps()]
                for half in range(2):
                    for p4 in range(4):
                        pair = half * 4 + p4
                        o = pTs[half][:, p4 * 128:(p4 + 1) * 128]
                        mm(o, pv_(A5, pair), pv_(A7, pair), start=True, stop=False)
                        mm(o, pv_(A6, pair), pv_(A2, pair), start=False, stop=True)
                for half in range(2):
                    src = pTs[half].re("p (q n) -> p q n", q=4)
                    if d == 0:
                        tt(TST[half], src, MLO.unsq(1).bc([128, 4, 128]), ALU.mult)
                    else:
                        tmpT = TC[:, half * 512:(half + 1) * 512].re("p (q n) -> p q n", q=4)
                        tt(tmpT, src, MUP.unsq(1).bc([128, 4, 128]), ALU.mult)
                        tt(TST[half], TST[half], tmpT, ALU.add)
                yield
                for ri in range(2):
                    if d == 0:
                        asm(A5 if ri == 0 else A7, XR if ri == 0 else XI, 6, -1, 1.0)
                    else:
                        asm(A5 if ri == 0 else A7, XR if ri == 0 else XI, 3, 1, 1.0)
                yield
                for ri in range(2):
                    srcA = A5 if ri == 0 else A7
                    for half in range(2):
                        p = nps()
                        for p4 in range(4):
                            mm(p[:, p4 * 128:(p4 + 1) * 128], pv_(srcA, half * 4 + p4), IDF)
                        cp(BQv[:, d, ri, half * 4:half * 4 + 4, :], p.re("p (q n) -> p q n", q=4), eng="act")
                    yield
                for ri in range(2):
                    dstA = A6 if ri == 0 else A2
                    if d == 0:
                        asm(dstA, YR if ri == 0 else YI, 4, 1, 1.0 if ri == 0 else -1.0)
                    else:
                        asm(dstA, YR if ri == 0 else YI, 7, -1, 1.0 if ri == 0 else -1.0)
                    cp(CQ[d][:, ri].re("p q n -> p (q n)"), dstA, eng="act")
                yield
            f32v_ = lambda T_: V(T_.tile, T_.ap.bitcast(F32))
            if startup:
                w2f = f32v_(WB[2])
                htf0 = f32v_(HTB[0]).re("p k n -> p (k n)")
                htf1 = f32v_(HTB[1]).re("p k n -> p (k n)")
                upf = f32v_(UP)
                bcf = f32v_(BCT)
                ypf = f32v_(YP)
                g0 = per_d(0, FT[0], FT[1], FT[3], FT[4], FT[5], FT[6], FT[7], FT[2], w2f[:, 0:1024], w2f[:, 1024:2048], G[2])
                g1 = per_d(1, htf0[:, 0:1024], htf0[:, 1024:2048], htf1[:, 0:1024], htf1[:, 1024:2048],
                           upf[:, 0:1024], upf[:, 1024:2048], upf[:, 2048:3072], bcf[:, 0:1024], bcf[:, 1024:2048], ypf[:, 0:1024], ypf[:, 1024:2048])
                alive = [g0, g1]
                while alive:
                    for g_ in list(alive):
                        try:
                            next(g_)
                        except StopIteration:
                            alive.remove(g_)
                    yield
            else:
                for d_ in range(2):
                    yield from per_d(d_, FT[0], FT[1], FT[3], FT[4], FT[5], FT[6], FT[7], FT[2], G[0], G[1], G[2])
            for pair in range(8):
                stt(TSv[:, pair, :], IDF, DBLK[:, pair:pair + 1], TST[pair // 4][:, pair % 4, :], ALU.mult, ALU.add)
            K.dma("sp", s5w[l][:, 0:4096], BT)
            K.dma("sp", s5w[l][:, 4096:6144], BMT[:, 0:2048])
            K.dma("sp", s5w[l][:, 6144:8192], CT)
            K.dma("sp", s5w[l][:, 8192:9216], MMTS[0])
            yield

        def s5_load(l):
            K.dma("pool", GW.re("p (t n) -> p t n", t=2), glu_w[l].rearrange("(t p) n -> p t n", p=128))
            K.dma("sp", BT, s5w[l][:, 0:4096])
            K.dma("sp", BMT[:, 0:2048], s5w[l][:, 4096:6144])
            K.dma("sp", CT, s5w[l][:, 6144:8192])
            K.dma("sp", MMTS[0], s5w[l][:, 8192:9216])

        def mixers(l):
            K.dma("sp", DTB, dt_bias[l:l + 1, :].partition_broadcast(128))
            K.dma("sp", NEGA, a_log[l:l + 1, :].partition_broadcast(128))
            K.dma("sp", SDD, ssd_d[l:l + 1, :].partition_broadcast(128))
            act(NEGA, NEGA, AF.Exp)
            ts(NEGA, NEGA, -1.0, ALU.mult)
            K.dma("pool", CBR, conv_b[l:l + 1, :])
            K.dma("pool", SGB, sgu_b[l:l + 1].rearrange("o h q -> o (h q)"))
            DGv = DG.re("p (t a n) -> p t a n", t=8, a=5)
            for t in range(8):
                for tap in range(5):
                    ts(DGv[:, t, tap, :], IDB, PT["cw%d%d" % (l, tap)][:, t:t + 1], ALU.mult)
            DDv = DD.re("p (h n) -> p h n", h=8)
            for h in range(8):
                ts(DDv[:, h, :], IDB, SDD[:, h:h + 1], ALU.mult)
            ts(SSNG, PT["ssdng%d" % l], float(np.sqrt(512.0)), ALU.mult)
            ts(SGNG, PT["sgung%d" % l], float(np.sqrt(256.0)), ALU.mult)
            WSTv = WST.re("p (h q) -> p h q", h=4)
            for h in range(4):
                K.dma("pool", WSL, sgu_w[l, h])
                p = nps()
                mm(p[:, 0:128], WSL, IDB)
                cp(WSTv[:, h, :], p[:, 0:128])
            groups = GROUPS
            for gi, (g0, GL, seqs, v) in enumerate(groups):
                SL = seqs[0][1]
                nseq = len(seqs)
                nblk = GL // 512
                ZS = UP[:, 0:4 * GL].re("p (t n) -> p t n", t=4)
                GV = UP[:, 4 * ML:4 * ML + 2 * GL].re("p (t n) -> p t n", t=2)
                GU = WB[2][:, 0:2 * GL].re("p (t n) -> p t n", t=2)
                U5 = WB[2][:, 2 * ML:2 * ML + 2 * GL].re("p (t n) -> p t n", t=2)
                XBCv = XBC[:, 0:8 * nseq * (SL + 4)].re("p (t s n) -> p t s n", t=8, s=nseq)
                memset(XBCv[:, :, :, 0:2], 0.0)
                memset(XBCv[:, :, :, SL + 2:SL + 4], 0.0)

                def xbc_dst(t, b):
                    res = []
                    if nseq == 2:
                        for s_ in range(2):
                            res.append((XBCv[:, t, s_, 2:2 + 256], (s_ * 256, 256)))
                    else:
                        res.append((XBCv[:, t, 0, 2 + b * 512:2 + (b + 1) * 512], (0, 512)))
                    return res

                def proj(wv, m, b):
                    p = nps()
                    c0 = g0 + b * 512
                    for k in range(8):
                        mm(p, wv[:, k, m * 128:(m + 1) * 128], HT[:, k, c0:c0 + 512], start=(k == 0), stop=(k == 7))
                    return p

                wv = load_w(w_in[l][:, 0:512], 512, key=("win", l, 0))
                for m in range(4):
                    for b in range(nblk):
                        act(ZS[:, m, b * 512:(b + 1) * 512], proj(wv, m, b), AF.Silu)
                for part in range(2):
                    wv = load_w(w_in[l][:, 512 + part * 512:1024 + part * 512], 512, key=("win", l, 1 + part))
                    for m in range(4):
                        for b in range(nblk):
                            p = proj(wv, m, b)
                            for (dst, (s0, sn)) in xbc_dst(part * 4 + m, b):
                                cp(dst, p[:, s0:s0 + sn], eng=("act" if (m + b) % 2 == 0 else "dve"))
                wv = load_w(w_in[l][:, 1552:2064], 512, key=("win", l, 3))
                for m in range(4):
                    for b in range(nblk):
                        dstt = (GU if m < 2 else GV)[:, m % 2, b * 512:(b + 1) * 512]
                        act(dstt, proj(wv, m, b), AF.Gelu)
                wv = load_w(w_in[l][:, 2064:2320], 256, key=("win", l, 4))
                for m in range(2):
                    for b in range(nblk):
                        cp(U5[:, m, b * 512:(b + 1) * 512], proj(wv, m, b), eng=("act" if (m + b) % 2 == 0 else "dve"))
                K.dma("pool", WDT.re("p (t n) -> p t n", t=8), w_in[l][:, 1536:1552].rearrange("(t p) n -> p t n", p=128))
                WDTv = WDT.re("p (t n) -> p t n", t=8)
                nchg = GL // 128
                pdt = nps()
                for ci in range(nchg):
                    for k in range(8):
                        mm(pdt[:, ci * 16:(ci + 1) * 16], HT[:, k, g0 + ci * 128:g0 + (ci + 1) * 128], WDTv[:, k, :], start=(k == 0), stop=(k == 7))
                DTv = DT[:, 0:nchg, :]
                tt(DTv, pdt[:, 0:nchg * 16].re("p (c h) -> p c h", h=16), DTB.unsq(1).bc([128, nchg, 16]), ALU.add)
                act(DTv, DTv, AF.Exp)
                act(DTv, DTv, AF.Ln, bias=1.0)
                ADTv = ADT[:, 0:nchg, :]
                tt(ADTv, DTv, NEGA.unsq(1).bc([128, nchg, 16]), ALU.mult)
                for d in range(2):
                    rhs_ = ADTv[:, :, d * 8:(d + 1) * 8]
                    pa = nps()
                    mm(pa[:, 0:nchg * 8].re("p (c h) -> p c h", h=8), (TRI_LE if d == 0 else TRI_GE), rhs_)
                    cp(ACUM[:, d, 0:nchg, :], pa[:, 0:nchg * 8].re("p (c h) -> p c h", h=8))
                    pb = nps()
                    mm(pb[:, 0:nchg * 8].re("p (c h) -> p c h", h=8), ONF, rhs_)
                    cp(TOT[:, d, 0:nchg, :], pb[:, 0:nchg * 8].re("p (c h) -> p c h", h=8))
                tt(DTE[:, :, 0:nchg, :], TOT[:, :, 0:nchg, :], ACUM[:, :, 0:nchg, :], ALU.subtract)
                act(DTE[:, :, 0:nchg, :], DTE[:, :, 0:nchg, :], AF.Exp)
                act(EA[:, :, 0:nchg, :], ACUM[:, :, 0:nchg, :], AF.Exp)
                act(CDC[:, :, 0:nchg, :], TOT[:, :, 0:nchg, :], AF.Exp)
                for d in range(2):
                    tt(DTE[:, d, 0:nchg, :], DTE[:, d, 0:nchg, :], DTv[:, :, d * 8:(d + 1) * 8], ALU.mult)

                mark('%s:L%d:win_done' % (pname, l))
                BCTv = BCT[:, 0:4 * GL].re("p (t n) -> p t n", t=4)
                XSFv = YP[:, 0:4 * GL].re("p (t n) -> p t n", t=4)
                for t in range(8):
                    for si in range(nseq):
                        for b in range(max(1, SL // 512)):
                            n = min(512, SL)
                            p = nps()
                            for tap in range(5):
                                mm(p[:, 0:n], DGv[:, t, tap, :], XBCv[:, t, si, b * 512 + tap:b * 512 + tap + n], start=(tap == 0), stop=(tap == 4))
                            dstv = XSFv[:, t] if t < 4 else BCTv[:, t - 4]
                            act(dstv[:, si * SL + b * 512:si * SL + b * 512 + n], p[:, 0:n], AF.Silu, bias=PT["convb%d" % l][:, t:t + 1])

                XST = WB[3].re("p (c n) -> p c n", n=512)
                BMTv = BMT.re("p (c n) -> p c n", n=256)
                YPv = YP[:, 0:4 * ML].re("p (c n) -> p c n", n=512)
                for cg in range(GL // 128):
                    tokc = slice(cg * 128, (cg + 1) * 128)
                    px = nps()
                    for q in range(4):
                        mm(px[:, q * 128:(q + 1) * 128], XSFv[:, q, tokc], IDB)
                    cp(XST[:, cg, :], px, eng="act")
                    pbm = nps()
                    for g in range(2):
                        mm(pbm[:, g * 128:(g + 1) * 128], BCTv[:, g, tokc], IDB)
                    cp(BMTv[:, cg, :], pbm[:, 0:256])
                for si, (s0, L) in enumerate(seqs):
                    nch = L // 128
                    cb0 = s0 // 128
                    for d in range(2):
                        if v == 1:
                            for q in range(4):
                                K.dma("sp", STG4[:, q * 128:(q + 1) * 128], st_ssd[l, d, q * 128:(q + 1) * 128, :])
                            p = nps()
                            for q in range(4):
                                mm(p[:, q * 128:(q + 1) * 128], STG4[:, q * 128:(q + 1) * 128], IDF)
                            cp(HST[d], p)
                        else:
                            memset(HST[d], 0.0)
                    for step in range(nch):
                        bg()
                        first = step < nch // 2
                        for d in range(2):
                            c = step if d == 0 else nch - 1 - step
                            cg = cb0 + c
                            cp(HSTB[d], HST[d], eng="act")
                            xw = XDT[d]
                            tt(xw.re("p (h x) -> p h x", h=8), XST[:, cg, :].re("p (h x) -> p h x", h=8), DTE[:, d, cg, :].unsq(2).bc([128, 8, 64]), ALU.mult)
                            pst = nps()
                            for g in range(2):
                                mm(pst[:, g * 256:(g + 1) * 256], BMTv[:, cg, g * 128:(g + 1) * 128], xw[:, g * 256:(g + 1) * 256])
                            po = nps()
                            for g in range(2):
                                mm(po[:, g * 256:(g + 1) * 256], BCTv[:, 2 + g, s0 + c * 128:s0 + (c + 1) * 128], HSTB[d][:, g * 256:(g + 1) * 256])
                            tt(HST[d].re("p (h x) -> p h x", h=8), HST[d].re("p (h x) -> p h x", h=8), CDC[:, d, cg, :].unsq(2).bc([128, 8, 64]), ALU.mult)
                            tt(HST[d], HST[d], pst, ALU.add)
                            eab = EA[:, d, cg, :].unsq(2).bc([128, 8, 64])
                            pov = po.re("p (h x) -> p h x", h=8)
                            if first:
                                tt(YPv[:, cg, :].re("p (h x) -> p h x", h=8), pov, eab, ALU.mult)
                            else:
                                tmpy = FT[d][:, 0:512]
                                tt(tmpy.re("p (h x) -> p h x", h=8), pov, eab, ALU.mult)
                                tt(YPv[:, cg, :], YPv[:, cg, :], tmpy, ALU.add)
                    if v == 0:
                        for d in range(2):
                            p = nps()
                            for q in range(4):
                                mm(p[:, q * 128:(q + 1) * 128], HST[d][:, q * 128:(q + 1) * 128], IDF)
                            so = SSO[d]
                            cp(so, p.re("p (q n) -> p q n", q=4))
                            K.dma("sp", ns_ssd[si, l, d].rearrange("(q p) n -> p q n", p=128), so)
                            if so.tile not in EXTRA_OUT_TILES:
                                EXTRA_OUT_TILES.append(so.tile)
                    rsb = [FT[0], FT[1]]

                    def emit_rs(cg_):
                        for d_ in range(2):
                            msk = TRI_LE if d_ == 0 else TRI_GE
                            tt(rsb[d_].re("p (h q) -> p h q", h=8), msk.unsq(1).bc([128, 8, 128]), ADT[:, cg_, d_ * 8:(d_ + 1) * 8].unsq(2).bc([128, 8, 128]), ALU.mult)

                    emit_rs(cb0)
                    for c in range(nch):
                        bg()
                        cg = cb0 + c
                        tok = slice(s0 + c * 128, s0 + (c + 1) * 128)
                        for d in range(2):
                            tt(XDT[d].re("p (h x) -> p h x", h=8), XST[:, cg, :].re("p (h x) -> p h x", h=8), DT[:, cg, d * 8:(d + 1) * 8].unsq(2).bc([128, 8, 64]), ALU.mult)
                        pcb = nps()
                        for g in range(2):
                            mm(pcb[:, g * 128:(g + 1) * 128], BCTv[:, g, tok], BCTv[:, 2 + g, tok])
                        for d in range(2):
                            strict = TRI_GT if d == 0 else TRI_LT
                            for hh in range(2):
                                psg = nps()
                                mm(psg, strict, rsb[d][:, hh * 512:(hh + 1) * 512])
                                act(MEXPS[d][:, hh * 512:(hh + 1) * 512], psg, AF.Exp)
                        tt(CBM[0].re("p (g q) -> p g q", g=2), pcb[:, 0:256].re("p (g q) -> p g q", g=2), TRI_LE.unsq(1).bc([128, 2, 128]), ALU.mult)
                        tt(CBM[1].re("p (g q) -> p g q", g=2), pcb[:, 0:256].re("p (g q) -> p g q", g=2), TRI_GE.unsq(1).bc([128, 2, 128]), ALU.mult)
                        for d in range(2):
                            tt(MMTS[d].re("p (g r q) -> p g r q", g=2, r=4), MEXPS[d].re("p (g r q) -> p g r q", g=2, r=4),
                               CBM[d].re("p (g q) -> p g q", g=2).unsq(2).bc([128, 2, 4, 128]), ALU.mult)
                        if c + 1 < nch:
                            emit_rs(cg + 1)
                        py = nps()
                        for h in range(8):
                            for d in range(2):
                                mm(py[:, h * 64:(h + 1) * 64], MMTS[d][:, h * 128:(h + 1) * 128], XDT[d][:, h * 64:(h + 1) * 64], start=(d == 0), stop=False)
                            mm(py[:, h * 64:(h + 1) * 64], DDv[:, h, :], XST[:, cg, h * 64:(h + 1) * 64], start=False, stop=True)
                        tt(YPv[:, cg, :], py, YPv[:, cg, :], ALU.add)
                    for c2 in range(0, nch, 2):
                        t0_ = s0 + c2 * 128
                        Gv = MEXPS[0].re("p (q n) -> p q n", q=4)
                        SQg = MEXPS[1].re("p (q n) -> p q n", q=4)
                        for cc in range(2):
                            pt_ = nps()
                            ptv = pt_.re("p (q n) -> p q n", q=4)
                            for q in range(4):
                                mm(ptv[:, q, :], YPv[:, cb0 + c2 + cc, q * 128:(q + 1) * 128], IDB)
                            tt(Gv[:, :, cc * 128:(cc + 1) * 128], ptv, ZS[:, :, t0_ + cc * 128:t0_ + (cc + 1) * 128], ALU.mult)
                        act(SQg, Gv, AF.Square)
                        rstd_from_sq(SQg, 4, 256, 512)
                        for q in range(4):
                            stt(HT[:, q, g0 + t0_:g0 + t0_ + 256], Gv[:, q, :], SSNG[:, q:q + 1], RS[:, 0:256], ALU.mult, ALU.mult)
                mark('%s:L%d:ssd_done' % (pname, l))
                for b in range(nblk):
                    cs = slice(b * 512, (b + 1) * 512)
                    for t in range(2):
                        act(SQ[:, t, :], GV[:, t, cs], AF.Square)
                    rstd_from_sq(SQ[:, 0:2, :], 2, 512, 256)
                    for t in range(2):
                        stt(GV[:, t, cs], GV[:, t, cs], SGNG[:, t:t + 1], RS, ALU.mult, ALU.mult)
                for ci in range(nchg):
                    tok = slice(ci * 128, (ci + 1) * 128)
                    pv = nps()
                    for t in range(2):
                        mm(pv[:, t * 128:(t + 1) * 128], GV[:, t, tok], IDB)
                    vt = XDT[0]
                    cp(vt[:, 0:256], pv[:, 0:256], eng="act")
                    pm_ = nps()
                    for h in range(4):
                        o = pm_[(h % 2) * 64:(h % 2 + 1) * 64, (h // 2) * 128:(h // 2 + 1) * 128]
                        mm(o, vt[:, h * 64:(h + 1) * 64], WSTv[:, h, :], start=True, stop=False)
                        mm(o, ONB[0:1, 0:64], SGB[0:1, h * 128:(h + 1) * 128], start=False, stop=True)
                    for t in range(2):
                        tt(HT[:, 4 + t, g0 + ci * 128:g0 + (ci + 1) * 128], pm_[:, t * 128:(t + 1) * 128], GU[:, t, tok], ALU.mult)

                mark('%s:L%d:sgu_done' % (pname, l))
                s5_load(l)
                RHO4, TH4, H0L = RHO4L[l], TH4L[l], H0LL[l]
                NJs = SL // 4
                NJ = GL // 4
                ppb = 512 // NJ
                ngrp = 8 // ppb
                UB = WB[2][:, 0:8 * NJ].re("p (q n) -> p q n", q=8)
                YBS = UP[:, 4 * ML:4 * ML + 8 * NJ].re("p (q n) -> p q n", q=8)
                HX = [[YP[:, 0:8 * NJ].re("p (q n) -> p q n", q=8), YP[:, 2048:2048 + 8 * NJ].re("p (q n) -> p q n", q=8)],
                      [UP[:, 0:8 * NJ].re("p (q n) -> p q n", q=8), UP[:, 2048:2048 + 8 * NJ].re("p (q n) -> p q n", q=8)]]
                if v == 1:
                    uview = lambda t, j0: U5[:, t, :].re("p (jj k c) -> p k c jj", jj=4, k=4)[:, j0]
                    yview = lambda t, k0: YG[:, t, 0:1024].re("p (jj k c) -> p k c jj", jj=4, k=4)[:, k0]
                    pv3 = lambda p_: p_[:, 0:256].re("p (c jj) -> p c jj", jj=4)
                else:
                    uview = lambda t, j0: U5[:, t, :].re("p (j k) -> p k j", k=4)[:, j0]
                    yview = lambda t, k0: YG[:, t, 0:512].re("p (j k) -> p k j", k=4)[:, k0]
                    pv3 = lambda p_: p_[:, 0:128]
                for pair in range(8):
                    t, p4 = pair // 4, pair % 4
                    p = nps()
                    for j0 in range(4):
                        mm(pv3(p), EMv[:, p4, j0, :], uview(t, j0), start=(j0 == 0), stop=(j0 == 3))
                    cp(UB[:, pair, :], p[:, 0:NJ], eng="act")
                T5 = lambda i: FT[i // 2][:, (i % 2) * 512:(i % 2 + 1) * 512]
                v4 = lambda T_: T_.re("p (q s n) -> p q s n", q=ppb, s=nseq)
                ntab = ppb * NJs
                tabv = lambda T_: T_[:, 0:ntab].re("p (q n) -> p q n", q=ppb)
                tbc = lambda T_: tabv(T_).unsq(2).bc([128, ppb, nseq, NJs])
                CH = [FT[3][:, 0:512], FT[3][:, 512:1024], FT[4][:, 0:512], FT[4][:, 512:1024], FT[5][:, 0:512]]
                TAB = [(FT[0][:, 0:512], FT[0][:, 512:1024], HST[0]), (FT[1][:, 0:512], FT[1][:, 512:1024], HST[1])]
                grp_l2 = [(d_, g_) for d_ in range(2) for g_ in range(ngrp)]

                def emit_tables(i_):
                    d_, g_ = grp_l2[i_]
                    fr_, c_, s_n = TAB[i_ % 2]
                    TI = TI32[:, 0:ntab]
                    tt(tabv(fr_), POS2[:, 0:NJs].unsq(1).bc([128, ppb, NJs]), TH4[d_][:, g_ * ppb:(g_ + 1) * ppb].unsq(2).bc([128, ppb, NJs]), ALU.mult)
                    ts(TI, fr_[:, 0:ntab], 1.0, ALU.mult)
                    tt(fr_[:, 0:ntab], fr_[:, 0:ntab], TI, ALU.subtract)
                    act(s_n[:, 0:ntab], fr_[:, 0:ntab], AF.Sin, scale=TWO_PI)
                    act(fr_[:, 0:ntab], fr_[:, 0:ntab], AF.Abs)
                    act(c_[:, 0:ntab], fr_[:, 0:ntab], AF.Sin, bias=PI / 2, scale=-TWO_PI)

                emit_tables(0)
                for gi_l2, (d, gq) in enumerate(grp_l2):
                    if True:
                        bg()
                        prs = slice(gq * ppb, (gq + 1) * ppb)
                        psS = [nps(), nps()]
                        for ri in range(2):
                            for q in range(ppb):
                                mm(psS[ri][:, q * NJ:(q + 1) * NJ], BQv[:, d, ri, gq * ppb + q, :], UB[:, gq * ppb + q, :])
                        if gi_l2 + 1 < len(grp_l2):
                            emit_tables(gi_l2 + 1)
                        fr, cs_, sn_ = TAB[gi_l2 % 2]
                        wr, wi, t1, gr, gi_ = CH
                        Sr = v4(psS[0]) if d == 0 else v4(psS[0])[:, :, :, ::-1]
                        Si = v4(psS[1]) if d == 0 else v4(psS[1])[:, :, :, ::-1]
                        tt(v4(wr), Sr, tbc(cs_), ALU.mult)
                        tt(v4(t1), Si, tbc(sn_), ALU.mult)
                        tt(wr, wr, t1, ALU.add)
                        tt(v4(wi), Si, tbc(cs_), ALU.mult)
                        tt(v4(t1), Sr, tbc(sn_), ALU.mult)
                        tt(wi, wi, t1, ALU.subtract)
                        if v == 1:
                            tt(v4(wr)[:, :, 0, 0:1], v4(wr)[:, :, 0, 0:1], H0L[d][0][:, prs].unsq(2), ALU.add)
                            tt(v4(wi)[:, :, 0, 0:1], v4(wi)[:, :, 0, 0:1], H0L[d][1][:, prs].unsq(2), ALU.add)
                        for q in range(ppb):
                            pair = gq * ppb + q
                            rb = RHO4[d][:, pair:pair + 1].bc([128, NJs])
                            for s_ in range(nseq):
                                sl = slice(q * NJ + s_ * NJs, q * NJ + (s_ + 1) * NJs)
                                scan(gr[:, sl], rb, wr[:, sl], 0.0)
                                scan(gi_[:, sl], rb, wi[:, sl], 0.0)
                        hr, hi = wr, wi
                        tt(v4(hr), v4(gr), tbc(cs_), ALU.mult)
                        tt(v4(t1), v4(gi_), tbc(sn_), ALU.mult)
                        tt(hr, hr, t1, ALU.subtract)
                        tt(v4(hi), v4(gr), tbc(sn_), ALU.mult)
                        tt(v4(gi_), v4(gi_), tbc(cs_), ALU.mult)
                        tt(hi, hi, gi_, ALU.add)
                        for ri, hh in enumerate((hr, hi)):
                            hv = v4(hh)
                            if v == 0:
                                for s_ in range(nseq):
                                    base = (((s_ * 2 + l) * 2 + d) * 2 + ri) * 8
                                    cp(NS5[:, base + gq * ppb:base + (gq + 1) * ppb], hv[:, :, s_, NJs - 1], eng="act")
                            hx = HX[d][ri][:, prs, :].re("p q (s n) -> p q s n", s=nseq)
                            if v == 1:
                                h0 = PT[("sre%d%d" if ri == 0 else "sim%d%d") % (l, d)][:, prs].unsq(2).unsq(3)
                            if d == 0:
                                cp(hx[:, :, :, 1:NJs], hv[:, :, :, 0:NJs - 1], eng="act")
                                if v == 1:
                                    cp(hx[:, :, :, 0:1], h0, eng="act")
                                else:
                                    memset(hx[:, :, :, 0:1], 0.0, eng="pool")
                            else:
                                cp(hx[:, :, :, 0:NJs - 1], hv[:, :, :, 0:NJs - 1][:, :, :, ::-1])
                                if v == 1:
                                    cp(hx[:, :, :, NJs - 1:NJs], h0, eng="act")
                                else:
                                    memset(hx[:, :, :, NJs - 1:NJs], 0.0, eng="pool")
                for t in range(2):
                    for p4 in range(4):
                        pair = 4 * t + p4
                        p = nps()
                        mm(p[:, 0:NJ], TSv[:, pair, :], UB[:, pair, :], start=True, stop=False)
                        for d in range(2):
                            for ri in range(2):
                                mm(p[:, 0:NJ], CQ[d][:, ri, pair, :], HX[d][ri][:, pair, :], start=False, stop=(d == 1 and ri == 1))
                        cp(YBS[:, pair, :], p[:, 0:NJ], eng="act")
                    for k0 in range(4):
                        p = nps()
                        for p4 in range(4):
                            mm(p[:, 0:NJ], EMv[:, k0, p4, :], YBS[:, 4 * t + p4, :], start=(p4 == 0), stop=(p4 == 3))
                        act(yview(t, k0), pv3(p), AF.Gelu)
                mark('%s:L%d:s5scan_done' % (pname, l))
                GWv = GW.re("p (t n) -> p t n", t=2)
                for m in range(2):
                    for b in range(nblk):
                        p = nps()
                        for k in range(2):
                            mm(p, GWv[:, k, m * 128:(m + 1) * 128], YG[:, k, b * 512:(b + 1) * 512], start=(k == 0), stop=(k == 1))
                        sig_ = NTMP if (m + b) % 2 == 0 else NTMP2
                        act(sig_, p, AF.Sigmoid, bias=PT["glub%d" % l][:, m:m + 1])
                        tt(HT[:, 6 + m, g0 + b * 512:g0 + (b + 1) * 512], sig_, YG[:, m, b * 512:(b + 1) * 512], ALU.mult)
                if l == 0:
                    dump("mixed" + pname, HT[:, :, 0:512], [128, 8, 512])
                    if GL > 512:
                        dump("mixed2" + pname, HT[:, :, 512:1024], [128, 8, 512])

        RL = XDT[0]
        rlc = [0]
        YS = [FT[0], FT[1], FT[2]]
        if ML == 512:
            SQF0 = sb("SQF0", [128, 512], F32)
            SQF1 = sb("SQF1", [128, 512], F32)
        s5gen = s5_pre(0, True)
        s5gen1 = s5_pre(1, False) if ML == 1024 else s5_pre(1, True)
        SQD = float(np.sqrt(D))
        NA = {}
        MODW = [[sb("MODW%d%d" % (l_, w_), [128, 8, 2], F32) for w_ in range(6)] for l_ in range(2)]
        NAT = {(l_, v_, w_): sb("NA%d%d%d" % (l_, v_, w_), [128, 8], F32) for l_ in range(2) for v_ in range(2) for w_ in range(2)}
        FNA = sb("FNA", [128, 8], F32)
        ts(FNA, PT["fng"], SQD, ALU.mult)

        def ada_mm(l, part, w):
            pm = PSB[7][:, l * 96:(l + 1) * 96]
            for j in range(4):
                ch = part * 4 + j
                for k in range(8):
                    mm(pm[:, 2 * ch:2 * ch + 2], w[:, k, j * 128:(j + 1) * 128], SC[:, :, k], start=(k == 0), stop=(k == 7))
            if part % 2 == 1:
                wh = part // 2
                tt(MODW[l][wh], pm[:, wh * 16:(wh + 1) * 16].re("p (c v) -> p c v", v=2),
                   PT["adab%d" % l][:, wh * 8:(wh + 1) * 8].unsq(2).bc([128, 8, 2]), ALU.add)
                for which, (gname, sc_i, sh_i) in enumerate((("n1g%d" % l, 1, 0), ("n2g%d" % l, 4, 3))):
                    if wh == sc_i:
                        for v in range(2):
                            A = NAT[(l, v, which)]
                            ts(A, MODW[l][sc_i][:, :, v], 1.0, ALU.add, SQD, ALU.mult)
                            tt(A, A, PT[gname], ALU.mult)
                            NA[(l, v, which)] = (A, MODW[l][sh_i][:, :, v])

        ada_order = [(l_, part) for l_ in range(2) for part in range(12)]
        ada_state = {"next": 0, "pending": None}
        ada_done = set()

        def bg(prefetch=True):
            st_ = ada_state
            newp = None
            if prefetch and st_["next"] < len(ada_order):
                l_, part = ada_order[st_["next"]]
                st_["next"] += 1
                w = load_w(ada_w[l_][:, part * 512:(part + 1) * 512], 512)
                newp = (l_, part, w)
            if st_["pending"] is not None:
                ada_mm(*st_["pending"])
                ada_done.add(st_["pending"][:2])
            st_["pending"] = newp
            if newp is None and st_["next"] >= len(ada_order):
                ada_all_done[0] = True

        def ada_need(l, part):
            while (l, part) not in ada_done:
                bg()
            bg(prefetch=False)

        pname, row0, NT, BLKS, GROUPS = PASSES[0]
        load_x()
        while (0, 3) not in ada_done:
            bg()
            for _ in range(3):
                next(s5gen, None)
        bg(prefetch=False)
        for _ in s5gen:
            pass
        if ML != 1024:
            for _ in s5gen1:
                pass
        mark('setup_done')
        for pass_i, (pname, row0, NT, BLKS, GROUPS) in enumerate(PASSES):
            pass_idx[0] = pass_i
            if pass_i > 0:
                K.wait_all("sp", [WB[0].tile, WB[1].tile])
            mark(pname + ':start')
            if pass_i > 0:
                load_x()
            for l in range(2):
                ada_need(l, 3)
                for _ in norm_to_HT(l, 0):
                    pass
                if l == 0:
                    dump("h1" + pname, HT[:, :, 0:512], [128, 8, 512])

                mark('%s:L%d:norm1_done' % (pname, l))
                mixers(l)
                mark('%s:L%d:mixers_done' % (pname, l))

                ada_need(l, 9)
                G1 = lambda v: MODW[l][2][:, :, v]
                wparts = [load_w(w_out[l][:, part * 512:(part + 1) * 512], 512, key=("wout", l, part)) for part in range(2)]
                for bi, (c0, n, v) in enumerate(BLKS):
                    for mt in range(8):
                        w = wparts[mt // 4]
                        m = mt % 4
                        p = nps()
                        for k in range(8):
                            mm(p, w[:, k, m * 128:(m + 1) * 128], HT[:, k, c0:c0 + n], start=(k == 0), stop=(k == 7))
                        stt(XT[:, mt, c0:c0 + n], p, G1(v)[:, mt:mt + 1], XT[:, mt, c0:c0 + n], ALU.mult, ALU.add)
                    for _ in norm_to_HT(l, 1, blks=[BLKS[bi]]):
                        pass
                if l == 0:
                    dump("xmid" + pname, XT[:, :, 0:512], [128, 8, 512])
                mark('%s:L%d:wout_done' % (pname, l))
                mark('%s:L%d:norm2_done' % (pname, l))
                ada_need(l, 11)
                if l == 1 and pass_i + 1 < len(PASSES) and ML == 1024:
                    prefetch_x(PASSES[pass_i + 1][1], PASSES[pass_i + 1][2])
                G2 = lambda v: MODW[l][5][:, :, v]
                UPv = UP[:, 0:4 * NT].re("p (j n) -> p j n", j=4)
                for part in range(8):
                    w1 = load_w(ffn_w1[l][:, part * 512:(part + 1) * 512], 512, key=("w1", l, part))
                    for j in range(4):
                        for (c0, n, v) in BLKS:
                            p = nps()
                            for k in range(8):
                                mm(p, w1[:, k, j * 128:(j + 1) * 128], HT[:, k, c0:c0 + n], start=(k == 0), stop=(k == 7))
                            rl_ = XDT[rlc[0] % 2]
                            rlc[0] += 1
                            act(rl_, p, AF.Relu)
                            tt(UPv[:, j, c0:c0 + n], rl_, rl_, ALU.mult)
                    if l == 0:
                        for _ in range(2):
                            next(s5gen1, None)
                    w2 = load_w(ffn_w2[l][part * 512:(part + 1) * 512, :], 1024, key=("w2", l, part))
                    for mt in range(8):
                        for (c0, n, v) in BLKS:
                            p = nps()
                            for j in range(4):
                                mm(p, w2[:, j, mt * 128:(mt + 1) * 128], UPv[:, j, c0:c0 + n], start=(j == 0), stop=(j == 3))
                            stt(XT[:, mt, c0:c0 + n], p, G2(v)[:, mt:mt + 1], XT[:, mt, c0:c0 + n], ALU.mult, ALU.add)
                if l == 0:
                    dump("xout" + pname, XT[:, :, 0:512], [128, 8, 512])

                if l == 0:
                    for _ in s5gen1:
                        pass
            mark(pname + ':ffn_done')
            FNTv = lambda k: (FT[2 + k] if ML == 512 else FT[4 + k // 2][:, (k % 2) * 512:(k % 2 + 1) * 512]) if k < 6 or ML != 512 else [SQF0, SQF1][k - 6]
            yi = 0
            cur = None
            for (c0, k, tmp) in norm_to_HT(0, 0, final=True):
                cp(FNTv(k), tmp, eng="act")
                if k == 7:
                    for tb in range(4):
                        ys = YS[yi % len(YS)]
                        yi += 1
                        for half in range(2):
                            p = nps()
                            for q in range(4):
                                kk = half * 4 + q
                                tr(p[:, q * 128:(q + 1) * 128], FNTv(kk)[:, tb * 128:(tb + 1) * 128], IDF)
                            cp(ys[:, half * 512:(half + 1) * 512], p, eng=("act" if half else "dve"))
                        K.dma("sp", yout[row0 + c0 + tb * 128:row0 + c0 + (tb + 1) * 128, :], ys)
        mark('final_done')
        p = nps()
        mm(p[:, 0:128], NS5, IDF)
        cp(NS5T, p[:, 0:128])
        K.dma("sp", ns_s5, NS5T)
        EXTRA_OUT_TILES.append(NS5T.tile)
        K.wait_all("sp", [y_.tile for y_ in YS] + dump_tiles + EXTRA_OUT_TILES)
    import os, json
    if os.environ.get('KPHASES'):
        json.dump(PHASES, open(os.environ['KPHASES'], 'w'))
    return dram_in, dram_out, dbg_out


EXTRA_OUT_TILES = []
PHASES = []


def build_layer_mixers(env, l):
    pass


def make_consts():
    j = np.arange(128)[:, None]
    k = np.arange(128)[None, :]
    c = np.zeros((128, 8 * 128 + 10), np.float32)
    c[:, 0:128] = np.eye(128)
    c[:, 128:256] = 1.0
    c[:, 256:384] = (j <= k)
    c[:, 384:512] = (j >= k)
    c[:, 512:640] = (j > k)
    c[:, 640:768] = (j < k)
    c[:, 768] = (np.arange(128) < 64)
    c[:, 769] = (np.arange(128) >= 64)
    jb = (np.arange(128) // 32)[:, None]
    kb = (np.arange(128) // 32)[None, :]
    c[:, 770:898] = (jb <= kb)
    c[:, 898:1026] = (jb >= kb)
    c[:, 1026:1034] = np.arange(-3, 5, dtype=np.float32)[None, :]
    return c


def make_emat():
    e = np.zeros((128, 4, 4, 128), np.float32)
    for a in range(4):
        for b in range(4):
            for r in range(32):
                e[32 * a + r, a, b, 32 * b + r] = 1.0
    return e.reshape(128, 2048)


_CACHE = {}


def kernel(**inp):
    f = lambda a: np.ascontiguousarray(np.asarray(a, dtype=np.float32))
    dbg_names = tuple(inp.pop("_dbg", ()))
    ncores = 8
    key = dbg_names
    mode = inp.pop('_mode', 'all')
    nc = bass.Bass("TRN2", target_bir_lowering=False)
    dram_in, dram_out, dbg_out = build_program(nc, dbg_names, mode)
    shared = {}
    for name in ["ada_w", "ada_b", "norm1_g", "norm2_g", "w_in", "ssd_conv_w", "ssd_conv_b", "ssd_norm_g", "sgu_norm_g",
                 "sgu_w", "sgu_b", "s5_d", "s5_glu_w", "s5_glu_b", "w_out", "ffn_w1", "ffn_w2", "final_norm_g", "ssd_d"]:
        shared[name] = f(inp[name])
    shared["ssd_dt_bias"] = f(inp["ssd_dt_bias"]).reshape(2, 16)
    shared["ssd_a_log"] = f(inp["ssd_a_log"]).reshape(2, 16)
    shared["s5_lambda_re"] = f(inp["s5_lambda_re"]).reshape(2, 2, 1024)
    shared["s5_lambda_im"] = f(inp["s5_lambda_im"]).reshape(2, 2, 1024)
    shared["s5_log_dt"] = f(inp["s5_log_dt"])
    shared["s5_b_re"] = f(inp["s5_b_re"]).reshape(2, 1024, 16)
    shared["s5_b_im"] = f(inp["s5_b_im"]).reshape(2, 1024, 16)
    shared["s5_c_re"] = f(inp["s5_c_re"])
    shared["s5_c_im"] = f(inp["s5_c_im"])
    shared["consts"] = make_consts()
    shared["emat"] = make_emat()
    shared["posrow"] = np.arange(1024, dtype=np.float32)[None, :]
    xp = f(inp["x_prompt"])
    xs = f(inp["x_sample"])
    sssd = f(inp["state_ssd"])
    sre = f(inp["state_s5_re"])
    sim = f(inp["state_s5_im"])
    c = f(inp["c"])
    cctx = f(inp["c_ctx"])
    in_maps = []
    for core in range(ncores):
        b = core % 2
        m = dict(shared)
        m["xin"] = np.ascontiguousarray(np.concatenate([xp[2 * core], xp[2 * core + 1], xs[b]], axis=0) if mode == 'all' else (np.concatenate([xp[2 * core], xp[2 * core + 1]], axis=0) if mode == 'P' else xs[b]))
        m["st_ssd"] = np.ascontiguousarray(sssd[b].reshape(2, 2, 512, 128))
        m["st_re"] = np.ascontiguousarray(sre[b].reshape(2, 2, 1024))
        m["st_im"] = np.ascontiguousarray(sim[b].reshape(2, 2, 1024))
        m["cvec"] = np.ascontiguousarray(np.stack([cctx, c[b]], axis=0))
        in_maps.append({k: m[k] for k in dram_in})
    res = run_bass_kernel_spmd(nc, in_maps, core_ids=list(range(ncores)))
    R = res.results
    if dbg_names:
        kernel.dbg = {n: np.asarray(R[0]["dbg_" + n]).astype(np.float32) for n in dbg_out}
    if mode != 'all':
        kernel.raw = R
        return None
    y_prompt = np.zeros((16, 256, D), np.float32)
    y_sample = np.zeros((2, 1024, D), np.float32)
    ns_ssd = np.zeros((16, 2, 2, 8, 64, 128), np.float32)
    ns_re = np.zeros((16, 2, 2, 16, 64), np.float32)
    ns_im = np.zeros((16, 2, 2, 16, 64), np.float32)
    for core in range(ncores):
        y = R[core]["yout"]
        y_prompt[2 * core] = y[0:256]
        y_prompt[2 * core + 1] = y[256:512]
        if core < 2:
            y_sample[core] = y[512:1536]
        ns_ssd[2 * core:2 * core + 2] = R[core]["ns_ssd"].reshape(2, 2, 2, 8, 64, 128)
        s5 = R[core]["ns_s5"].reshape(2, 2, 2, 2, 8, 128)
        ns_re[2 * core:2 * core + 2] = s5[:, :, :, 0].reshape(2, 2, 2, 16, 64)
        ns_im[2 * core:2 * core + 2] = s5[:, :, :, 1].reshape(2, 2, 2, 16, 64)
    if dbg_names:
        kernel.dbg = {n: np.asarray(R[0]["dbg_" + n]).astype(np.float32) for n in dbg_out}
    return (y_prompt, y_sample, ns_ssd, ns_re, ns_im)
```
